# Optimizing a Trainium2 kernel written in Bass

```python
import math
import jax, jax.numpy as jnp
from jax import lax
import numpy as np

D_MODEL = 1024
BATCH = 16
SEQ = 4096
DEPTH = 4
DEC_BATCH = 4
DEC_SEQ = 4096
PAST_LEN = 128

HEAD_DIM = 64
A_HEADS = 8
B_HEADS = 8
B_KV_HEADS = 2
A_WIDTH = A_HEADS * HEAD_DIM
B_WIDTH = B_HEADS * HEAD_DIM
B_KV_WIDTH = B_KV_HEADS * HEAD_DIM
MIX_WIDTH = A_WIDTH + B_WIDTH
ATTN_IN = 3 * A_WIDTH + B_WIDTH + 2 * B_KV_WIDTH + MIX_WIDTH
DILATED_PATTERNS = ((128, 1), (512, 4), (2048, 16))
ROPE_THETA = 500000.0
ROPE_DIM = HEAD_DIM // 4
AXIAL_THETA = 10000.0
GRID_W = 64
Q_BLOCK = 128
POOL_WINDOWS = (2, 4, 8, 16)
POOL_GROUPS = 4
POOL_WIDTH = D_MODEL
POOL_GROUP_DIM = POOL_WIDTH // POOL_GROUPS
NORM_EPS = 1e-6
NEG_INF = -1e30

kernel_name = 'hybrid_dilated_axial_pool_encoder'


def _rmsnorm(x, g):
    x32 = x.astype(jnp.float32)
    y = x32 * lax.rsqrt(jnp.mean(x32 * x32, axis=-1, keepdims=True) + NORM_EPS)
    return (y * g.astype(jnp.float32)).astype(x.dtype)


def _rope(x, pos, theta):
    r = x.shape[-1]
    inv = theta ** (-jnp.arange(0, r, 2, dtype=jnp.float32) / r)
    ang = pos.astype(jnp.float32)[:, None] * inv[None, :]
    cos = jnp.cos(ang)[:, None, :]
    sin = jnp.sin(ang)[:, None, :]
    x32 = x.astype(jnp.float32)
    x1, x2 = x32[..., : r // 2], x32[..., r // 2:]
    return jnp.concatenate([x1 * cos - x2 * sin, x1 * sin + x2 * cos], axis=-1).astype(x.dtype)


def _partial_rope(x, pos):
    return jnp.concatenate([_rope(x[..., :ROPE_DIM], pos, ROPE_THETA), x[..., ROPE_DIM:]], axis=-1)


def _axial_rope(x, row, col):
    half = HEAD_DIM // 2
    return jnp.concatenate([_rope(x[..., :half], row, AXIAL_THETA),
                            _rope(x[..., half:], col, AXIAL_THETA)], axis=-1)


def _banded_attention(q, k, v, radius):
    n, L, h, dh = q.shape
    blk = radius
    nb = -(-L // blk)
    lp = nb * blk
    qp = jnp.pad(q, ((0, 0), (0, lp - L), (0, 0), (0, 0))).reshape(n, nb, blk, h, dh)
    pad_k = ((0, 0), (blk, lp - L + blk), (0, 0), (0, 0))
    kb = jnp.pad(k, pad_k).reshape(n, nb + 2, blk, h, dh)
    vb = jnp.pad(v, pad_k).reshape(n, nb + 2, blk, h, dh)
    kw = jnp.concatenate([kb[:, :-2], kb[:, 1:-1], kb[:, 2:]], axis=2)
    vw = jnp.concatenate([vb[:, :-2], vb[:, 1:-1], vb[:, 2:]], axis=2)
    qpos = jnp.arange(lp).reshape(nb, blk)
    kpos = jnp.arange(nb)[:, None] * blk - blk + jnp.arange(3 * blk)[None, :]
    kp = kpos[:, None, :]
    valid = (jnp.abs(qpos[:, :, None] - kp) <= radius) & (kp >= 0) & (kp < L)
    s = jnp.einsum('nbqhd,nbkhd->nbhqk', qp, kw).astype(jnp.float32) / math.sqrt(dh)
    s = jnp.where(valid[None, :, None], s, NEG_INF)
    lse = jax.nn.logsumexp(s, axis=-1)
    p = jnp.exp(s - lse[..., None])
    o = jnp.einsum('nbhqk,nbkhd->nbqhd', p.astype(v.dtype), vw)
    o = o.reshape(n, lp, h, dh)[:, :L]
    lse = lse.transpose(0, 1, 3, 2).reshape(n, lp, h)[:, :L]
    return o, lse


def _dilated_window_attention(q, k, v):
    b, s, h, dh = q.shape
    outs, lses = [], []
    for window, dil in DILATED_PATTERNS:
        radius = window // (2 * dil)
        L = s // dil

        def to_res(t):
            return t.reshape(b, L, dil, h, dh).swapaxes(1, 2).reshape(b * dil, L, h, dh)

        o, lse = _banded_attention(to_res(q), to_res(k), to_res(v), radius)
        outs.append(o.reshape(b, dil, L, h, dh).swapaxes(1, 2).reshape(b, s, h, dh))
        lses.append(lse.reshape(b, dil, L, h).swapaxes(1, 2).reshape(b, s, h))
    wts = jax.nn.softmax(jnp.stack(lses), axis=0)
    out = jnp.einsum('pbsh,pbshd->bshd', wts, jnp.stack(outs).astype(jnp.float32))
    return out.astype(q.dtype)


def _blocked_gqa(q, k, v):
    b, s, hq, dh = q.shape
    hkv = k.shape[2]
    g = hq // hkv
    nb = s // Q_BLOCK
    qb = q.reshape(b, nb, Q_BLOCK, hkv, g, dh).transpose(1, 0, 2, 3, 4, 5)

    def one_block(qi):
        sc = jnp.einsum('bqhgd,bshd->bhgqs', qi, k).astype(jnp.float32) / math.sqrt(dh)
        p = jax.nn.softmax(sc, axis=-1)
        return jnp.einsum('bhgqs,bshd->bqhgd', p.astype(v.dtype), v)

    o = lax.map(one_block, qb)
    return o.transpose(1, 0, 2, 3, 4, 5).reshape(b, s, hq, dh)


def _attn_mixer(h, w_in, q_norm, k_norm, w_out):
    b, s, _ = h.shape
    proj = h @ w_in
    cuts = [A_WIDTH, 2 * A_WIDTH, 3 * A_WIDTH, 3 * A_WIDTH + B_WIDTH,
            3 * A_WIDTH + B_WIDTH + B_KV_WIDTH, 3 * A_WIDTH + B_WIDTH + 2 * B_KV_WIDTH]
    qa, ka, va, qb, kb, vb, gate = jnp.split(proj, cuts, axis=-1)
    pos = jnp.arange(s)
    qa = _partial_rope(qa.reshape(b, s, A_HEADS, HEAD_DIM), pos)
    ka = _partial_rope(ka.reshape(b, s, A_HEADS, HEAD_DIM), pos)
    va = va.reshape(b, s, A_HEADS, HEAD_DIM)
    oa = _dilated_window_attention(qa, ka, va).reshape(b, s, A_WIDTH)
    rows = s // GRID_W
    row = jnp.repeat(jnp.arange(rows), GRID_W)
    col = jnp.tile(jnp.arange(GRID_W), rows)
    qb = _axial_rope(_rmsnorm(qb.reshape(b, s, B_HEADS, HEAD_DIM), q_norm), row, col)
    kb = _axial_rope(_rmsnorm(kb.reshape(b, s, B_KV_HEADS, HEAD_DIM), k_norm), row, col)
    vb = vb.reshape(b, s, B_KV_HEADS, HEAD_DIM)
    ob = _blocked_gqa(qb, kb, vb).reshape(b, s, B_WIDTH)
    y = jnp.concatenate([oa, ob], axis=-1) * jax.nn.silu(gate)
    return y @ w_out


def _pool_mixer(h, w_in, w_grp, scale, w_out):
    b, s, _ = h.shape
    u, gate = jnp.split(h @ w_in, 2, axis=-1)
    u32 = u.astype(jnp.float32)
    cs = jnp.concatenate([jnp.zeros((b, 1, POOL_WIDTH), jnp.float32), jnp.cumsum(u32, axis=1)], axis=1)
    cs = cs.reshape(b, s + 1, POOL_GROUPS, POOL_GROUP_DIM)
    half = jnp.array(POOL_WINDOWS, dtype=jnp.int32) // 2
    t = jnp.arange(s, dtype=jnp.int32)[:, None]
    lo = jnp.clip(t - half[None, :], 0, s)
    hi = jnp.clip(t + half[None, :], 0, s)
    gidx = jnp.arange(POOL_GROUPS)[None, :]
    win_sum = cs[:, hi, gidx] - cs[:, lo, gidx]
    cnt = (hi - lo).astype(jnp.float32)[None, :, :, None]
    pooled = win_sum / cnt - u32.reshape(b, s, POOL_GROUPS, POOL_GROUP_DIM)
    mixed = jnp.einsum('bsgc,gcd->bsgd', pooled.astype(h.dtype), w_grp)
    mixed = mixed * scale.reshape(POOL_GROUPS, POOL_GROUP_DIM)
    y = mixed.reshape(b, s, POOL_WIDTH) * jax.nn.silu(gate)
    return y @ w_out


def _trunk(x, c, ada_w, ada_b, pre_norm, post_norm, attn_w_in, attn_q_norm, attn_k_norm,
           attn_w_out, pool_w_in, pool_w_grp, pool_scale, pool_w_out):
    for l in range(DEPTH):
        i = l // 2
        mod = jax.nn.silu(c) @ ada_w[l] + ada_b[l]
        shift, scl, gate = jnp.split(mod[:, None, :], 3, axis=-1)
        h = _rmsnorm(x, pre_norm[l]) * (1.0 + scl) + shift
        if l % 2 == 0:
            m = _attn_mixer(h, attn_w_in[i], attn_q_norm[i], attn_k_norm[i], attn_w_out[i])
        else:
            m = _pool_mixer(h, pool_w_in[i], pool_w_grp[i], pool_scale[i], pool_w_out[i])
        x = x + gate * _rmsnorm(m, post_norm[l])
    return x


def setup_inputs(seed: int = 0) -> dict:
    key = jax.random.key(seed)
    ks = jax.random.split(key, 20)
    n_attn = (DEPTH + 1) // 2
    n_pool = DEPTH // 2
    f32 = jnp.float32
    nrm = lambda k, shp: jax.random.normal(k, shp, f32)
    return {
        'x_prompt': nrm(ks[0], (BATCH, SEQ, D_MODEL)),
        'x_sample': nrm(ks[1], (DEC_BATCH, DEC_SEQ, D_MODEL)),
        'c_prompt': nrm(ks[2], (BATCH, D_MODEL)),
        'c_sample': nrm(ks[3], (DEC_BATCH, D_MODEL)),
        'ada_w': nrm(ks[4], (DEPTH, D_MODEL, 3 * D_MODEL)) * (0.3 * D_MODEL ** -0.5),
        'ada_b': nrm(ks[5], (DEPTH, 3 * D_MODEL)) * 0.01,
        'pre_norm': 1.0 + 0.05 * nrm(ks[6], (DEPTH, D_MODEL)),
        'post_norm': 1.0 + 0.05 * nrm(ks[7], (DEPTH, D_MODEL)),
        'attn_w_in': nrm(ks[8], (n_attn, D_MODEL, ATTN_IN)) * D_MODEL ** -0.5,
        'attn_q_norm': 1.0 + 0.05 * nrm(ks[9], (n_attn, HEAD_DIM)),
        'attn_k_norm': 1.0 + 0.05 * nrm(ks[10], (n_attn, HEAD_DIM)),
        'attn_w_out': nrm(ks[11], (n_attn, MIX_WIDTH, D_MODEL)) * MIX_WIDTH ** -0.5,
        'pool_w_in': nrm(ks[12], (n_pool, D_MODEL, 2 * POOL_WIDTH)) * D_MODEL ** -0.5,
        'pool_w_grp': nrm(ks[13], (n_pool, POOL_GROUPS, POOL_GROUP_DIM, POOL_GROUP_DIM)) * POOL_GROUP_DIM ** -0.5,
        'pool_scale': 1.0 + 0.1 * nrm(ks[14], (n_pool, POOL_WIDTH)),
        'pool_w_out': nrm(ks[15], (n_pool, POOL_WIDTH, D_MODEL)) * POOL_WIDTH ** -0.5,
    }


def reference(x_prompt, x_sample, c_prompt, c_sample, ada_w, ada_b, pre_norm, post_norm,
              attn_w_in, attn_q_norm, attn_k_norm, attn_w_out,
              pool_w_in, pool_w_grp, pool_scale, pool_w_out):
    y_prompt = _trunk(x_prompt, c_prompt, ada_w, ada_b, pre_norm, post_norm, attn_w_in, attn_q_norm,
                      attn_k_norm, attn_w_out, pool_w_in, pool_w_grp, pool_scale, pool_w_out)
    y_sample = _trunk(x_sample, c_sample, ada_w, ada_b, pre_norm, post_norm, attn_w_in, attn_q_norm,
                      attn_k_norm, attn_w_out, pool_w_in, pool_w_grp, pool_scale, pool_w_out)
    return (y_prompt, y_sample)
```

```python
import math
from contextlib import ExitStack

import numpy as np
import ml_dtypes

import concourse.bass as bass
import concourse.mybir as mybir
from concourse.bass_utils import run_bass_kernel_spmd

F32 = mybir.dt.float32
BF16 = mybir.dt.bfloat16
AF = mybir.ActivationFunctionType
ALU = mybir.AluOpType
NPBF = ml_dtypes.bfloat16

D = 1024
HD = 64
EPS = 1e-6
PATTERNS = (1, 4, 16)
POOL_HALF = (1, 2, 4, 8)
N_CORES = 8
NSEQ_FULL = 3
S_FULL = 4096


def sl(start, count, step=1):
    return slice(start, start + (count - 1) * step + 1, step)


class Ref:
    __slots__ = ("kind", "eng", "sem", "value")

    def __init__(self, kind, eng, sem, value):
        self.kind, self.eng, self.sem, self.value = kind, eng, sem, value


class T:
    __slots__ = ("name", "writers", "readers")

    def __init__(self, name):
        self.name = name
        self.writers = []
        self.readers = {}


class EngS:
    def __init__(self, name, h, sem):
        self.name, self.h, self.sem = name, h, sem
        self.count = 0
        self.pending = []
        self.waited = {}


class KB:
    def __init__(self, nc, stack, n_dma_sems=40):
        self.nc = nc
        self.E = {}
        for name, h in (("pe", nc.tensor), ("act", nc.scalar), ("dve", nc.vector),
                        ("pool", nc.gpsimd), ("sp", nc.sync)):
            sem = stack.enter_context(nc.semaphore("sem_" + name))
            self.E[name] = EngS(name, h, sem)
        self.dsems = [[stack.enter_context(nc.semaphore("dsem%d" % i)), 0] for i in range(n_dma_sems)]
        self.dnext = 0
        self.swsems = [[stack.enter_context(nc.semaphore("swsem%d" % i)), 0] for i in range(8)]
        self.swnext = 0
        self.nops = 0

    def _wait(self, E, sem, val):
        key = id(sem)
        if E.waited.get(key, 0) >= val:
            return
        E.h.wait_ge(sem, val)
        E.waited[key] = val

    def op(self, eng, fn, reads=(), writes=(), signal=True, partial=False, dma=False):
        E = self.E[eng]
        self.nops += 1
        for t in reads:
            for r in t.writers:
                self._dep(E, eng, r, "raw")
        for t in writes:
            for r in t.readers.values():
                self._dep(E, eng, r, "war")
            for r in t.writers:
                self._dep(E, eng, r, "waw")
        if dma:
            if eng == "pool":
                slot = self.swsems[self.swnext]
                self.swnext = (self.swnext + 1) % len(self.swsems)
            else:
                slot = self.dsems[self.dnext]
                self.dnext = (self.dnext + 1) % len(self.dsems)
            if slot[1] > 0:
                self._wait(E, slot[0], slot[1])
            ins = fn(E.h)
            ins.then_inc(slot[0], 16)
            slot[1] += 16
            ref = Ref("dma", eng, slot[0], slot[1])
        else:
            ins = fn(E.h)
            ref = Ref("eng", eng, E.sem, None)
            if signal:
                E.count += 1
                ins.then_inc(E.sem, 1)
                ref.value = E.count
                for p in E.pending:
                    p.value = E.count
                E.pending = []
            else:
                E.pending.append(ref)
        for t in reads:
            t.readers[eng if not dma else ("dma", id(ref))] = ref
        for t in writes:
            if partial:
                t.writers.append(ref)
            else:
                t.writers = [ref]
                t.readers = {}
        return ref

    def _dep(self, E, eng, r, kind):
        if r.kind == "eng" and r.eng == eng:
            if eng == "pe":
                return
        assert r.value is not None, "dependency on an unsignaled op (engine %s)" % r.eng
        self._wait(E, r.sem, r.value)

    def barrier(self):
        for E in self.E.values():
            assert not E.pending, "pending unsignaled ops on %s at barrier" % E.name
        for E in self.E.values():
            for F in self.E.values():
                if F is not E and F.count > 0:
                    self._wait(E, F.sem, F.count)
            for sem, cnt in self.dsems + self.swsems:
                if cnt > 0:
                    self._wait(E, sem, cnt)


def _rope_tables(S):
    t = np.arange(S, dtype=np.float32)
    invA = (np.float32(500000.0) ** (-np.arange(0, 16, 2, dtype=np.float32) / np.float32(16))).astype(np.float32)
    angA = (t[:, None] * invA[None, :]).astype(np.float32)
    CA = np.ones((128, S), np.float32)
    SA = np.zeros((128, S), np.float32)
    for p in range(128):
        dd = p % 64
        if dd < 16:
            CA[p] = np.cos(angA[:, dd % 8])
            SA[p] = np.sin(angA[:, dd % 8])
    invB = (np.float32(10000.0) ** (-np.arange(0, 32, 2, dtype=np.float32) / np.float32(32))).astype(np.float32)
    row = (np.arange(S) // 64).astype(np.float32)
    col = (np.arange(S) % 64).astype(np.float32)
    CB = np.zeros((128, S), np.float32)
    SB = np.zeros((128, S), np.float32)
    for p in range(128):
        dd = p % 64
        pos = row if dd < 32 else col
        f = (dd % 32) % 16
        ang = (pos * invB[f]).astype(np.float32)
        CB[p] = np.cos(ang)
        SB[p] = np.sin(ang)
    return CA, SA, CB, SB


def _rot_mats():
    RA = np.zeros((128, 128), np.float32)
    RB = np.zeros((128, 128), np.float32)
    for m in range(128):
        dd = m % 64
        if dd < 8:
            RA[m + 8, m] = -1.0
        elif dd < 16:
            RA[m - 8, m] = 1.0
        e = dd % 32
        if e < 16:
            RB[m + 16, m] = -1.0
        else:
            RB[m - 16, m] = 1.0
    return RA, RB


def _band_mats():
    BC = np.zeros((128, 4, 3, 128), np.float32)
    BP = np.zeros((128, 4, 128), np.float32)
    BN = np.zeros((128, 4, 128), np.float32)
    for g, h in enumerate(POOL_HALF):
        for var in range(3):
            for tout in range(128):
                lo, hi = tout - h, tout + h
                clo = max(lo, 0) if var == 1 else lo
                chi = min(hi, 128) if var == 2 else hi
                cnt = chi - clo
                for tin in range(max(clo, 0), min(chi, 128)):
                    BC[tin, g, var, tout] += 1.0 / cnt
                BC[tout, g, var, tout] -= 1.0
        for tout in range(128):
            for i in range(32):
                if -32 + i >= tout - h:
                    BP[96 + i, g, tout] = 1.0 / (2 * h)
                if 128 + i < tout + h:
                    BN[i, g, tout] = 1.0 / (2 * h)
    return BC, BP, BN


def _consts(S, nseq):
    c = {}
    CA, SA, CB, SB = _rope_tables(S)
    c["rope"] = np.stack([CA, SA, CB, SB], axis=1).astype(np.float32)
    RA, RB = _rot_mats()
    ident = np.eye(128, dtype=np.float32)
    bd = np.zeros((128, 128), np.float32)
    bd[:64, :64] = 1.0
    bd[64:, 64:] = 1.0
    kj = np.arange(128)[:, None]
    qi = np.arange(128)[None, :]
    mlo = (kj >= qi).astype(np.float32)
    mhi = (kj <= qi).astype(np.float32)
    mask = np.concatenate([mlo, mhi] * 4, axis=1)
    BC, BP, BN = _band_mats()
    pack = np.concatenate([ident, RA, RB, bd, mask, BC.reshape(128, -1), BP.reshape(128, -1),
                           BN.reshape(128, -1)], axis=1)
    c["cpack"] = pack.astype(NPBF)
    sel = np.zeros((nseq, nseq, 128), np.float32)
    for s in range(nseq):
        sel[s, s, :] = 1.0
    c["sel"] = sel
    sel2 = np.zeros((2, 128), np.float32)
    sel2[0, :64] = 1.0
    sel2[1, 64:] = 1.0
    c["sel2"] = sel2
    return c


CP_IDENT, CP_RA, CP_RB, CP_BD, CP_MASK = 0, 128, 256, 384, 512
CP_BC = 512 + 1024
CP_BP = CP_BC + 4 * 3 * 128
CP_BN = CP_BP + 4 * 128
CP_END = CP_BN + 4 * 128


class Prog:
    def __init__(self, nseq, S, layers=(0, 1, 2, 3)):
        self.nseq, self.S, self.layers = nseq, S, tuple(layers)
        self.NT = S // 128
        self.NC = S // 512
        nc = bass.Bass("TRN2", target_bir_lowering=False)
        self.nc = nc
        dt = nc.dram_tensor
        self.x_in = dt("x", [nseq, S, D], F32, kind="ExternalInput")
        self.cT = dt("cT", [128, 8, nseq], F32, kind="ExternalInput")
        self.ada_w = dt("ada_w", [4, D, 3 * D], F32, kind="ExternalInput")
        self.ada_bT = dt("ada_bT", [128, 4, 16], F32, kind="ExternalInput")
        self.ada_bg = dt("ada_bg", [nseq, 4, D], F32, kind="ExternalInput")
        self.pre_T = dt("pre_T", [128, 4, 8], F32, kind="ExternalInput")
        self.post_rep = dt("post_rep", [nseq, 4, D], F32, kind="ExternalInput")
        self.attn_w_in = dt("attn_w_in", [2, D, 3328], F32, kind="ExternalInput")
        self.qk_g = dt("qk_g", [128, 2, 2], F32, kind="ExternalInput")
        self.attn_w_out = dt("attn_w_out", [2, D, D], F32, kind="ExternalInput")
        self.pool_w_in = dt("pool_w_in", [2, D, 2 * D], F32, kind="ExternalInput")
        self.pool_w_grp = dt("pool_w_grp", [2, 4, 256, 256], F32, kind="ExternalInput")
        self.pool_scT = dt("pool_scT", [128, 2, 8], F32, kind="ExternalInput")
        self.pool_w_out = dt("pool_w_out", [2, D, D], F32, kind="ExternalInput")
        self.rope = dt("rope", [128, 4, S], F32, kind="ExternalInput")
        self.cpack = dt("cpack", [128, CP_END], BF16, kind="ExternalInput")
        self.sel = dt("sel", [nseq, nseq, 128], F32, kind="ExternalInput")
        self.sel2 = dt("sel2", [2, 128], F32, kind="ExternalInput")
        self.y_out = dt("y", [nseq, S, D], F32, kind="ExternalOutput")
        self.xs = [dt("xs0", [nseq, S, D], F32, kind="Internal"), dt("xs1", [nseq, S, D], F32, kind="Internal")]
        self.qkaT = dt("qkaT", [nseq, 1024, S], BF16, kind="Internal")
        self.qkbT = dt("qkbT", [nseq, 640, S], BF16, kind="Internal")
        self.gateT = dt("gateT", [nseq, 1024, S], BF16, kind="Internal")
        self.va = dt("va", [nseq, S, 4, 192], BF16, kind="Internal")
        self.vb = dt("vb", [nseq, S, 2, 192], BF16, kind="Internal")
        self.yT = dt("yT", [nseq, 1024, S], BF16, kind="Internal")
        self.u = dt("u", [nseq, S, D], BF16, kind="Internal")
        self.build()

    def sb(self, st, name, shape, dtype):
        self._uid = getattr(self, "_uid", 0) + 1
        return st.enter_context(self.nc.sbuf_tensor("%s_%d" % (name, self._uid), list(shape), dtype))

    def ps(self, st, name, shape, dtype):
        self._uid = getattr(self, "_uid", 0) + 1
        return st.enter_context(self.nc.psum_tensor("%s_%d" % (name, self._uid), list(shape), dtype))

    def build(self):
        nc = self.nc
        with ExitStack() as st:
            k = KB(nc, st)
            self.k = k
            self.CP = self.sb(st, "cpk", [128, CP_END], BF16)
            self.SC = self.sb(st, "SC", [128, 4, self.nseq, 8], F32)
            self.SH = self.sb(st, "SH", [128, 4, self.nseq, 8], F32)
            self.GP = self.sb(st, "GP", [self.nseq, 4, D], F32)
            self.SEL = self.sb(st, "SEL", [self.nseq, self.nseq, 128], F32)
            self.SEL2 = self.sb(st, "SEL2", [2, 128], F32)
            self.QKG = self.sb(st, "QKG", [128, 2, 2], F32)
            self.PSC = self.sb(st, "PSC", [128, 2, 8], F32)
            self.EPS = self.sb(st, "EPSt", [128, 1], F32)
            tc = T("consts")
            k.op("sp", lambda e: e.dma_start(out=self.CP[:, :], in_=self.cpack[:, :]), writes=[tc], dma=True)
            k.op("sp", lambda e: e.dma_start(out=self.SEL[:, :, :], in_=self.sel[:, :, :]), writes=[tc], partial=True, dma=True)
            k.op("sp", lambda e: e.dma_start(out=self.QKG[:, :, :], in_=self.qk_g[:, :, :]), writes=[tc], partial=True, dma=True)
            k.op("sp", lambda e: e.dma_start(out=self.SEL2[:, :], in_=self.sel2[:, :]), writes=[tc], partial=True, dma=True)
            k.op("sp", lambda e: e.dma_start(out=self.PSC[:, :, :], in_=self.pool_scT[:, :, :]), writes=[tc], partial=True, dma=True)
            k.op("dve", lambda e: e.memset(self.EPS[:, :], EPS), writes=[tc], partial=True)
            self.prologue()
            k.barrier()
            xin = self.x_in
            for li, l in enumerate(self.layers):
                last = li == len(self.layers) - 1
                xout = self.y_out if last else self.xs[li % 2]
                if l % 2 == 0:
                    self.phase_P(l, xin)
                    k.barrier()
                    self.phase_A(l)
                    k.barrier()
                    self.phase_B(l)
                    k.barrier()
                    self.phase_O(l, xin, xout, attn=True)
                    k.barrier()
                else:
                    self.phase_U(l, xin)
                    k.barrier()
                    self.phase_O(l, xin, xout, attn=False)
                    k.barrier()
                xin = xout
            k.barrier()

    def prologue(self):
        k, nseq = self.k, self.nseq
        with ExitStack() as st:
            cT = self.sb(st, "cTs", [128, 8, nseq], F32)
            scT = self.sb(st, "scT", [128, 8, nseq], F32)
            bT = self.sb(st, "bT", [128, 4, 16], F32)
            preT = self.sb(st, "preT", [128, 4, 8], F32)
            bg = self.sb(st, "bg", [nseq, 4, D], F32)
            pr = self.sb(st, "pr", [nseq, 4, D], F32)
            blk = self.sb(st, "blk", [128, 2, 8, 512], F32)
            modT = self.sb(st, "modT", [128, 16, nseq], F32)
            pm = self.ps(st, "pm", [128, 2, 512], F32)
            pg = self.ps(st, "pg", [128, 2, 512], F32)
            t_in = T("pin")
            k.op("sp", lambda e: e.dma_start(out=cT[:, :, :], in_=self.cT[:, :, :]), writes=[t_in], dma=True)
            for dst, src in ((bT, self.ada_bT), (preT, self.pre_T), (bg, self.ada_bg), (pr, self.post_rep)):
                k.op("sp", lambda e, dst=dst, src=src: e.dma_start(out=dst[:, :, :], in_=src[:, :, :]),
                     writes=[t_in], partial=True, dma=True)
            t_sc = T("scT")
            k.op("act", lambda e: e.activation(out=scT[:, :, :], in_=cT[:, :, :], func=AF.Silu),
                 reads=[t_in], writes=[t_sc])
            t_blk = [T("blk0"), T("blk1")]
            t_pm = [T("pm0"), T("pm1")]
            t_pg = T("pg")
            t_mod = T("modT")
            t_out = T("modout")
            nb_i = 0
            for l in self.layers:
                wv = self.ada_w[l].rearrange("(kc p) n -> p kc n", p=128)
                for nb in range(6):
                    b = nb_i % 2
                    nb_i += 1
                    k.op("sp", lambda e, b=b, nb=nb, wv=wv: e.dma_start(out=blk[:, b, :, :], in_=wv[:, :, nb * 512:(nb + 1) * 512]),
                         writes=[t_blk[b]], dma=True)
                    if nb < 4:
                        pb = nb % 2
                        for j in range(4):
                            for kc in range(8):
                                k.op("pe", lambda e, b=b, j=j, kc=kc, pb=pb: e.matmul(
                                    pm[:, pb, j * nseq:(j + 1) * nseq], lhsT=blk[:, b, kc, j * 128:(j + 1) * 128],
                                    rhs=scT[:, kc, :], start=(kc == 0), stop=(kc == 7)),
                                    reads=[t_blk[b], t_sc], writes=[t_pm[pb]], signal=(kc == 7 and j == 3),
                                    partial=not (j == 0 and kc == 0))
                        k.op("dve", lambda e, nb=nb, pb=pb: e.tensor_copy(
                            out=modT[:, nb * 4:(nb + 1) * 4, :],
                            in_=pm[:, pb, 0:4 * nseq].rearrange("p (j s) -> p j s", s=nseq)),
                            reads=[t_pm[pb]], writes=[t_mod], partial=(nb > 0))
                    else:
                        half = nb - 4
                        for kc in range(8):
                            k.op("pe", lambda e, b=b, kc=kc, half=half: e.matmul(
                                pg[0:nseq, half, :], lhsT=scT[:, kc, :], rhs=blk[:, b, kc, :],
                                start=(kc == 0), stop=(kc == 7)),
                                reads=[t_blk[b], t_sc], writes=[t_pg], signal=(kc == 7),
                                partial=not (half == 0 and kc == 0))
                for s in range(nseq):
                    k.op("dve", lambda e, l=l, s=s: e.tensor_tensor(
                        out=self.SH[:, l, s, :], in0=modT[:, 0:8, s], in1=bT[:, l, 0:8], op=ALU.add),
                        reads=[t_mod, t_in], writes=[t_out], partial=True)
                    k.op("dve", lambda e, l=l, s=s: e.tensor_tensor(
                        out=self.SC[:, l, s, :], in0=modT[:, 8:16, s], in1=bT[:, l, 8:16], op=ALU.add),
                        reads=[t_mod, t_in], writes=[t_out], partial=True)
                    k.op("dve", lambda e, l=l, s=s: e.scalar_tensor_tensor(
                        out=self.SC[:, l, s, :], in0=self.SC[:, l, s, :], scalar=1.0, in1=preT[:, l, :],
                        op0=ALU.add, op1=ALU.mult),
                        reads=[t_out, t_in], writes=[t_out], partial=True)
                k.op("dve", lambda e, l=l: e.tensor_tensor(
                    out=self.GP[:, l, :], in0=pg[0:nseq, :, :].rearrange("p a b -> p (a b)"), in1=bg[:, l, :], op=ALU.add),
                    reads=[t_pg, t_in], writes=[t_out], partial=True)
                k.op("dve", lambda e, l=l: e.tensor_tensor(
                    out=self.GP[:, l, :], in0=self.GP[:, l, :], in1=pr[:, l, :], op=ALU.mult),
                    reads=[t_out, t_in], writes=[t_out], partial=True)
            k.barrier()

    def alloc_front(self, st):
        f = {}
        f["xt"] = self.sb(st, "f_xt", [128, 2, 4, D], F32)
        f["t_xt"] = [T("xt0"), T("xt1")]
        f["junk"] = self.sb(st, "f_junk", [128, D], BF16)
        f["t_junk"] = T("junk")
        f["ss"] = self.sb(st, "f_ss", [128, 2, 4], F32)
        f["sd"] = self.sb(st, "f_sd", [128, 2, 4], F32)
        f["rs"] = self.sb(st, "f_rs", [128, 2, 4], F32)
        f["t_ss"] = [T("ss0"), T("ss1")]
        f["t_sd"] = [T("sd0"), T("sd1")]
        f["t_rs"] = [T("rs0"), T("rs1")]
        f["xn"] = self.sb(st, "f_xn", [128, 4, D], BF16)
        f["t_xn"] = [T("xn%d" % i) for i in range(4)]
        f["hT"] = self.sb(st, "f_hT", [128, 2, 8, 512], BF16)
        f["t_hT"] = [T("hT0"), T("hT1")]
        f["ptr"] = self.ps(st, "f_ptr", [128, 2, 1024], BF16)
        f["t_ptr"] = [T("ptr0"), T("ptr1")]
        f["n"] = 0
        return f

    def load_x(self, f, xin, s, c, idx):
        b = idx % 2
        xv = xin[s].rearrange("(i p) d -> p i d", p=128)
        self.k.op("sp", lambda e: e.dma_start(out=f["xt"][:, b, :, :], in_=xv[:, 4 * c:4 * c + 4, :]),
                  writes=[f["t_xt"][b]], dma=True)

    def front_a(self, f, idx):
        k = self.k
        b = idx % 2
        xt = f["xt"]
        for i in range(4):
            k.op("act", lambda e, i=i: e.activation(out=f["junk"][:, :], in_=xt[:, b, i, :], func=AF.Square,
                                                     accum_out=f["ss"][:, b, i:i + 1]),
                 reads=[f["t_xt"][b]], writes=[f["t_junk"], f["t_ss"][b]], partial=(i > 0))
        k.op("act", lambda e: e.activation(out=f["sd"][:, b, :], in_=f["ss"][:, b, :], func=AF.Sqrt,
                                           scale=1.0 / D, bias=self.EPS[:, 0:1]),
             reads=[f["t_ss"][b]], writes=[f["t_sd"][b]])
        k.op("dve", lambda e: e.reciprocal(out=f["rs"][:, b, :], in_=f["sd"][:, b, :]),
             reads=[f["t_sd"][b]], writes=[f["t_rs"][b]])
        for i in range(4):
            k.op("dve", lambda e, i=i: e.tensor_scalar(out=f["xn"][:, i, :], in0=xt[:, b, i, :],
                                                        scalar1=f["rs"][:, b, i:i + 1], scalar2=None, op0=ALU.mult),
                 reads=[f["t_xt"][b], f["t_rs"][b]], writes=[f["t_xn"][i]])

    def front_b(self, f, l, s, idx):
        k = self.k
        b = idx % 2
        ident = self.CP[:, CP_IDENT:CP_IDENT + 128]
        for fc in range(8):
            pb = f["n"] % 2
            f["n"] += 1
            for i in range(4):
                k.op("pe", lambda e, i=i, fc=fc, pb=pb: e.transpose(
                    out=f["ptr"][:, pb, i * 128:(i + 1) * 128], in_=f["xn"][:, i, fc * 128:(fc + 1) * 128], identity=ident),
                    reads=[f["t_xn"][i]], writes=[f["t_ptr"][pb]], signal=(i == 3), partial=(i > 0))
            k.op("act", lambda e, fc=fc, pb=pb: e.activation(
                out=f["hT"][:, b, fc, :], in_=f["ptr"][:, pb, 0:512], func=AF.Identity,
                scale=self.SC[:, l, s, fc:fc + 1], bias=self.SH[:, l, s, fc:fc + 1]),
                reads=[f["t_ptr"][pb]], writes=[f["t_hT"][b]], partial=(fc > 0))
        return b

    def load_w(self, dst, src_view, t):
        self.k.op("pool", lambda e: e.dma_start(out=dst, in_=src_view), writes=[t], dma=True, partial=True)

    def run_skewed(self, tasks):
        maxage = max(len(t) for t in tasks)
        for n in range(len(tasks) + maxage):
            for age in range(maxage - 1, -1, -1):
                m = n - age
                if 0 <= m < len(tasks) and age < len(tasks[m]) and tasks[m][age] is not None:
                    tasks[m][age]()

    def phase_P(self, l, xin):
        k, nseq = self.k, self.nseq
        ai = l // 2
        with ExitStack() as st:
            f = self.alloc_front(st)
            W = self.sb(st, "P_W", [128, 8, 3328], BF16)
            t_W = T("W")
            wv = self.attn_w_in[ai].rearrange("(kc p) n -> p kc n", p=128)
            for kc in range(8):
                self.load_w(W[:, kc, :], wv[:, kc, :], t_W)
            rp = self.sb(st, "P_rope", [128, 2, 4, 512], F32)
            t_rp = [T("rp0"), T("rp1")]
            NPJ = 4
            pj = self.ps(st, "P_pj", [128, NPJ, 512], F32)
            t_pj = [T("pj%d" % i) for i in range(NPJ)]
            px = self.ps(st, "P_px", [128, 2, 512], F32)
            t_px = [T("px0"), T("px1")]
            NSTG = 6
            stg = self.sb(st, "P_stg", [128, NSTG, 512], BF16)
            t_stg = [T("stg%d" % i) for i in range(NSTG)]
            NR = 3
            q16 = self.sb(st, "P_q16", [128, NR, 512], BF16)
            sq = self.sb(st, "P_sq", [128, NR, 512], BF16)
            sdv = self.sb(st, "P_sdv", [128, NR, 512], F32)
            rsv = self.sb(st, "P_rsv", [128, NR, 512], F32)
            t1 = self.sb(st, "P_t1", [128, NR, 512], F32)
            t2 = self.sb(st, "P_t2", [128, NR, 512], F32)
            t_q16 = [T("q16_%d" % i) for i in range(NR)]
            t_sq = [T("sq%d" % i) for i in range(NR)]
            t_sdv = [T("sdv%d" % i) for i in range(NR)]
            t_rsv = [T("rsv%d" % i) for i in range(NR)]
            t_t1 = [T("t1_%d" % i) for i in range(NR)]
            t_t2 = [T("t2_%d" % i) for i in range(NR)]
            vst = self.sb(st, "P_vst", [128, 2, 4, 3, 64], BF16)
            vbst = self.sb(st, "P_vbst", [128, 2, 2, 3, 64], BF16)
            t_vst = [T("vst0"), T("vst1")]
            t_vbst = [T("vbst0"), T("vbst1")]
            for b in range(2):
                k.op("pool", lambda e, b=b: e.memset(vst[:, b, :, :, :], 1.0), writes=[t_vst[b]])
                k.op("pool", lambda e, b=b: e.memset(vbst[:, b, :, :, :], 1.0), writes=[t_vbst[b]])
            RA = self.CP[:, CP_RA:CP_RA + 128]
            RB = self.CP[:, CP_RB:CP_RB + 128]
            BD = self.CP[:, CP_BD:CP_BD + 128]
            ident = self.CP[:, CP_IDENT:CP_IDENT + 128]
            cnt = {"pj": 0, "stg": 0, "a": 0, "v": 0}
            items = [(s, c) for s in range(nseq) for c in range(self.NC)]
            hT = f["hT"]

            def new_pj():
                j = cnt["pj"] % NPJ
                cnt["pj"] += 1
                return j

            def new_stg():
                si = cnt["stg"] % NSTG
                cnt["stg"] += 1
                return si

            def projT(j, col0, b):
                for kc in range(8):
                    k.op("pe", lambda e, kc=kc: e.matmul(pj[:, j, :], lhsT=W[:, kc, col0:col0 + 128],
                                                          rhs=hT[:, b, kc, :], start=(kc == 0), stop=(kc == 7)),
                         reads=[t_W, f["t_hT"][b]], writes=[t_pj[j]], signal=(kc == 7), partial=(kc > 0))

            def store(si, dram_ap):
                k.op("sp", lambda e: e.dma_start(out=dram_ap, in_=stg[:, si, :]), reads=[t_stg[si]], dma=True)

            def gate_task(s, c, b, fcg):
                stt_ = {}

                def s0():
                    stt_["j"] = new_pj()
                    projT(stt_["j"], 2304 + fcg * 128, b)

                def s1():
                    j, si = stt_["j"], new_stg()
                    k.op("act", lambda e: e.activation(out=stg[:, si, :], in_=pj[:, j, :], func=AF.Silu),
                         reads=[t_pj[j]], writes=[t_stg[si]])
                    store(si, self.gateT[s, fcg * 128:(fcg + 1) * 128, c * 512:(c + 1) * 512])
                return [s0, s1]

            def fb_task(idx2, fc):
                s2 = items[idx2][0]
                b2 = idx2 % 2
                stt_ = {}

                def s0():
                    pb = f["n"] % 2
                    f["n"] += 1
                    stt_["pb"] = pb
                    for i in range(4):
                        k.op("pe", lambda e, i=i: e.transpose(
                            out=f["ptr"][:, pb, i * 128:(i + 1) * 128], in_=f["xn"][:, i, fc * 128:(fc + 1) * 128], identity=ident),
                            reads=[f["t_xn"][i]], writes=[f["t_ptr"][pb]], signal=(i == 3), partial=(i > 0))

                def s1():
                    pb = stt_["pb"]
                    k.op("act", lambda e: e.activation(
                        out=hT[:, b2, fc, :], in_=f["ptr"][:, pb, 0:512], func=AF.Identity,
                        scale=self.SC[:, l, s2, fc:fc + 1], bias=self.SH[:, l, s2, fc:fc + 1]),
                        reads=[f["t_ptr"][pb]], writes=[f["t_hT"][b2]], partial=(fc > 0))
                return [s0, s1]

            def rope_stages(stt_, b, R, ctab, stab, dram_ap):
                def rot():
                    a = stt_["a"]
                    k.op("pe", lambda e: e.matmul(px[:, 1, :], lhsT=R, rhs=q16[:, a, :], start=True, stop=True),
                         reads=[t_q16[a]], writes=[t_px[1]])
                    k.op("pool", lambda e: e.tensor_tensor(out=t1[:, a, :], in0=q16[:, a, :], in1=rp[:, b, ctab, :], op=ALU.mult),
                         reads=[t_q16[a], t_rp[b]], writes=[t_t1[a]])

                def comb():
                    a = stt_["a"]
                    k.op("dve", lambda e: e.tensor_tensor(out=t2[:, a, :], in0=px[:, 1, :], in1=rp[:, b, stab, :], op=ALU.mult),
                         reads=[t_px[1], t_rp[b]], writes=[t_t2[a]])
                    si = new_stg()
                    k.op("pool", lambda e: e.tensor_tensor(out=stg[:, si, :], in0=t1[:, a, :], in1=t2[:, a, :], op=ALU.add),
                         reads=[t_t1[a], t_t2[a]], writes=[t_stg[si]])
                    store(si, dram_ap)
                return rot, comb

            def a_task(s, c, b, tq):
                stt_ = {}

                def s0():
                    stt_["j"] = new_pj()
                    projT(stt_["j"], tq * 128, b)

                def s1():
                    j = stt_["j"]
                    a = cnt["a"] % NR
                    cnt["a"] += 1
                    stt_["a"] = a
                    k.op("act", lambda e: e.activation(out=q16[:, a, :], in_=pj[:, j, :], func=AF.Copy),
                         reads=[t_pj[j]], writes=[t_q16[a]])
                rot, comb = rope_stages(stt_, b, RA, 0, 1, self.qkaT[s, tq * 128:(tq + 1) * 128, c * 512:(c + 1) * 512])
                return [s0, s1, rot, comb]

            def b_task(s, c, b, tq):
                stt_ = {}
                gi = 0 if tq < 4 else 1

                def s0():
                    stt_["j"] = new_pj()
                    projT(stt_["j"], 1536 + tq * 128, b)

                def s1():
                    j = stt_["j"]
                    a = cnt["a"] % NR
                    cnt["a"] += 1
                    stt_["a"] = a
                    k.op("act", lambda e: e.activation(out=sq[:, a, :], in_=pj[:, j, :], func=AF.Square),
                         reads=[t_pj[j]], writes=[t_sq[a]])
                    k.op("pe", lambda e: e.matmul(px[:, 0, :], lhsT=BD, rhs=sq[:, a, :], start=True, stop=True),
                         reads=[t_sq[a]], writes=[t_px[0]])

                def s2():
                    j, a = stt_["j"], stt_["a"]
                    k.op("act", lambda e: e.activation(out=sdv[:, a, :], in_=px[:, 0, :], func=AF.Sqrt,
                                                       scale=1.0 / HD, bias=self.EPS[:, 0:1]),
                         reads=[t_px[0]], writes=[t_sdv[a]])
                    k.op("dve", lambda e: e.reciprocal(out=rsv[:, a, :], in_=sdv[:, a, :]),
                         reads=[t_sdv[a]], writes=[t_rsv[a]])
                    k.op("dve", lambda e: e.scalar_tensor_tensor(
                        out=q16[:, a, :], in0=pj[:, j, :], scalar=self.QKG[:, ai, gi:gi + 1], in1=rsv[:, a, :],
                        op0=ALU.mult, op1=ALU.mult),
                        reads=[t_pj[j], t_rsv[a]], writes=[t_q16[a]])
                rot, comb = rope_stages(stt_, b, RB, 2, 3, self.qkbT[s, tq * 128:(tq + 1) * 128, c * 512:(c + 1) * 512])
                return [s0, s1, s2, rot, comb]

            def v_task(s, c, b, i):
                stt_ = {}

                def s0():
                    j, j2 = new_pj(), new_pj()
                    stt_["j"], stt_["j2"] = j, j2
                    for kc in range(8):
                        k.op("pe", lambda e, kc=kc: e.matmul(
                            pj[:, j, :], lhsT=hT[:, b, kc, i * 128:(i + 1) * 128], rhs=W[:, kc, 1024:1536],
                            start=(kc == 0), stop=(kc == 7)),
                            reads=[t_W, f["t_hT"][b]], writes=[t_pj[j]], signal=(kc == 7), partial=(kc > 0))
                    for kc in range(8):
                        k.op("pe", lambda e, kc=kc: e.matmul(
                            pj[:, j2, 0:128], lhsT=hT[:, b, kc, i * 128:(i + 1) * 128], rhs=W[:, kc, 2176:2304],
                            start=(kc == 0), stop=(kc == 7)),
                            reads=[t_W, f["t_hT"][b]], writes=[t_pj[j2]], signal=(kc == 7), partial=(kc > 0))

                def s1():
                    j, j2 = stt_["j"], stt_["j2"]
                    vbuf = cnt["v"] % 2
                    cnt["v"] += 1
                    k.op("dve", lambda e: e.tensor_copy(
                        out=vst[:, vbuf, :, 0:3:2, :], in_=pj[:, j, :].rearrange("p (h t d) -> p h t d", h=4, t=2)),
                        reads=[t_pj[j]], writes=[t_vst[vbuf]])
                    k.op("dve", lambda e: e.tensor_copy(
                        out=vbst[:, vbuf, :, 1, :], in_=pj[:, j2, 0:128].rearrange("p (h d) -> p h d", h=2)),
                        reads=[t_pj[j2]], writes=[t_vbst[vbuf]])
                    tok0 = c * 512 + i * 128
                    k.op("sp", lambda e: e.dma_start(
                        out=self.va[s, tok0:tok0 + 128, :, :].rearrange("t h c -> t (h c)"),
                        in_=vst[:, vbuf, :, :, :].rearrange("p h a d -> p (h a d)")),
                        reads=[t_vst[vbuf]], dma=True)
                    k.op("sp", lambda e: e.dma_start(
                        out=self.vb[s, tok0:tok0 + 128, :, :].rearrange("t h c -> t (h c)"),
                        in_=vbst[:, vbuf, :, :, :].rearrange("p h a d -> p (h a d)")),
                        reads=[t_vbst[vbuf]], dma=True)
                return [s0, s1]

            def prefetch_r(idx):
                c = items[idx][1]
                b = idx % 2
                k.op("sp", lambda e: e.dma_start(out=rp[:, b, :, :], in_=self.rope[:, :, c * 512:(c + 1) * 512]),
                     writes=[t_rp[b]], dma=True)

            def head_task(idx):
                def s0():
                    if idx + 2 < len(items):
                        self.load_x(f, xin, items[idx + 2][0], items[idx + 2][1], idx + 2)
                    if idx + 1 < len(items):
                        prefetch_r(idx + 1)
                        self.front_a(f, idx + 1)
                return [s0]

            self.load_x(f, xin, items[0][0], items[0][1], 0)
            prefetch_r(0)
            if len(items) > 1:
                self.load_x(f, xin, items[1][0], items[1][1], 1)
            self.front_a(f, 0)
            self.front_b(f, l, items[0][0], 0)
            tasks = []
            for idx, (s, c) in enumerate(items):
                b = idx % 2
                tasks.append(head_task(idx))
                nxt = idx + 1 < len(items)
                fbq = [fb_task(idx + 1, fc) for fc in range(8)] if nxt else []
                for fcg in range(8):
                    tasks.append(gate_task(s, c, b, fcg))
                    if fcg >= 4 and fbq:
                        tasks.append(fbq.pop(0))
                        tasks.append(fbq.pop(0))
                for tq in range(8):
                    tasks.append(a_task(s, c, b, tq))
                for tq in range(5):
                    tasks.append(b_task(s, c, b, tq))
                for i in range(4):
                    tasks.append(v_task(s, c, b, i))
            self.run_skewed(tasks)

    def alloc_fin(self, st):
        g = {}
        g["R"] = self.sb(st, "fin_R", [128, 2, 512], F32)
        g["RS"] = self.sb(st, "fin_RS", [128, 2, 512], F32)
        g["tg"] = self.sb(st, "fin_tg", [128, 2, 512], F32)
        g["yst"] = self.sb(st, "fin_y", [128, 2, 512], BF16)
        for nm in ("R", "RS", "tg", "yst"):
            g["t_" + nm] = [T(nm + "0"), T(nm + "1")]
        g["n"] = 0
        return g

    def finalize(self, g, srcA, srcB, t_srcs, Gap, t_G, dram_ap):
        k = self.k
        b = g["n"] % 2
        g["n"] += 1
        R, RS, tg, yst = g["R"], g["RS"], g["tg"], g["yst"]
        k.op("dve", lambda e: e.reciprocal(out=R[0:64, b, :], in_=srcB[0:64, :]), reads=t_srcs, writes=[g["t_R"][b]])
        k.op("dve", lambda e: e.reciprocal(out=R[64:128, b, :], in_=srcA[64:128, :]), reads=t_srcs,
             writes=[g["t_R"][b]], partial=True)
        k.op("sp", lambda e: e.dma_start(out=RS[0:64, b, :], in_=R[64:128, b, :]), reads=[g["t_R"][b]],
             writes=[g["t_RS"][b]], dma=True)
        k.op("sp", lambda e: e.dma_start(out=RS[64:128, b, :], in_=R[0:64, b, :]), reads=[g["t_R"][b]],
             writes=[g["t_RS"][b]], dma=True, partial=True)
        k.op("dve", lambda e: e.tensor_tensor(out=tg[0:64, b, :], in0=srcA[0:64, :], in1=Gap[0:64, :], op=ALU.mult),
             reads=t_srcs + [t_G], writes=[g["t_tg"][b]])
        k.op("dve", lambda e: e.tensor_tensor(out=tg[64:128, b, :], in0=srcB[64:128, :], in1=Gap[64:128, :], op=ALU.mult),
             reads=t_srcs + [t_G], writes=[g["t_tg"][b]], partial=True)
        k.op("pool", lambda e: e.tensor_tensor(out=yst[:, b, :], in0=tg[:, b, :], in1=RS[:, b, :], op=ALU.mult),
             reads=[g["t_tg"][b], g["t_RS"][b]], writes=[g["t_yst"][b]])
        k.op("sp", lambda e: e.dma_start(out=dram_ap, in_=yst[:, b, :]), reads=[g["t_yst"][b]], dma=True)

    def phase_A(self, l):
        k, nseq, S = self.k, self.nseq, self.S
        with ExitStack() as st:
            QT = self.sb(st, "A_QT", [128, 2, S], BF16)
            KT = self.sb(st, "A_KT", [128, 2, S], BF16)
            G = self.sb(st, "A_G", [128, S], BF16)
            t_QT = [T("QT0"), T("QT1")]
            t_KT = [T("KT0"), T("KT1")]
            t_G = T("G")
            NVB = 3
            VD = self.sb(st, "A_VD", [128, NVB, self.NT, 192], BF16)
            t_VD = [T("VD%d" % i) for i in range(NVB)]
            OA = self.sb(st, "A_OA", [128, 2, 2, S], F32)
            t_OA = [[T("OA00"), T("OA01")], [T("OA10"), T("OA11")]]
            DT = self.sb(st, "A_DT", [128, S // 64], F32)
            RT = self.sb(st, "A_RT", [128, S // 64], F32)
            RROW = self.sb(st, "A_RROW", [2, S], F32)
            t_DT, t_RT, t_RROW = T("DT"), T("RT"), T("RROW")
            tg = self.sb(st, "A_tg", [128, 2, 512], F32)
            yst = self.sb(st, "A_yst", [128, 2, 512], BF16)
            t_tg = [T("tg0"), T("tg1")]
            t_yst = [T("yst0"), T("yst1")]
            pR = self.ps(st, "A_pR", [128, 512], F32)
            t_pR = T("pR")
            pending = []
            fin_n = [0]
            NPS = 5
            pS = self.ps(st, "A_pS", [128, NPS, 512], F32)
            t_pS = [T("pS%d" % i) for i in range(NPS)]
            pO = self.ps(st, "A_pO", [128, 2, 512], F32)
            t_pO = [T("pO0"), T("pO1")]
            NPT = 6
            pT = self.sb(st, "A_pT", [128, NPT, 512], BF16)
            pTm = self.sb(st, "A_pTm", [128, NPT, 512], BF16)
            t_pT = [T("pT%d" % i) for i in range(NPT)]
            t_pTm = [T("pTm%d" % i) for i in range(NPT)]
            MASK = self.CP[:, CP_MASK:CP_MASK + 512]
            LA = 3
            for i in range(NPS):
                k.op("dve", lambda e, i=i: e.memset(pS[:, i, :], 0.0), writes=[t_pS[i]])
            items = [(s, hp) for s in range(nseq) for hp in range(4)]

            def load_qk(idx):
                s, hp = items[idx]
                b = idx % 2
                k.op("sp", lambda e: e.dma_start(out=QT[:, b, :], in_=self.qkaT[s, hp * 128:(hp + 1) * 128, :]),
                     writes=[t_QT[b]], dma=True)
                k.op("sp", lambda e: e.dma_start(out=KT[:, b, :], in_=self.qkaT[s, 512 + hp * 128:512 + (hp + 1) * 128, :]),
                     writes=[t_KT[b]], dma=True)

            vloads = [(idx, d) for idx in range(len(items)) for d in PATTERNS]

            def load_v(vi):
                idx, d = vloads[vi]
                s, hp = items[idx]
                vb_ = vi % NVB
                nt = (S // d) // 128
                vv = self.va[s, :, hp, :].rearrange("(m p r) c -> r p m c", p=128, r=d)
                for r in range(d):
                    k.op("sp", lambda e, r=r: e.dma_start(out=VD[:, vb_, r * nt:(r + 1) * nt, :], in_=vv[r]),
                         writes=[t_VD[vb_]], dma=True, partial=(r > 0))

            groups = []
            vi = 0
            for idx, (s, hp) in enumerate(items):
                b = idx % 2
                first_of_item = True
                for d in PATTERNS:
                    vb_ = vi % NVB
                    first_of_pat = True
                    L = S // d
                    nt = L // 128
                    for hh in range(2):
                        qtiles = [(r, m) for r in range(d) for m in range(nt + 1)]
                        for g0 in range(0, len(qtiles), 2):
                            gd = dict(idx=idx, s=s, hp=hp, b=b, d=d, hh=hh, vb=vb_, nt=nt, L=L,
                                      tiles=qtiles[g0:g0 + 2], pre=[], last=False, n=len(groups))
                            if first_of_pat and vi + 2 < len(vloads):
                                gd["pre"].append(("v", vi + 2))
                            first_of_item = False
                            first_of_pat = False
                            groups.append(gd)
                    vi += 1
                groups[-1]["last"] = True
            oa_first = {}

            def plan(gd):
                mm = []
                segs = []
                for ti, (r, m) in enumerate(gd["tiles"]):
                    jlo = max(128 * m - 64, 0)
                    jhi = min(128 * m + 64, gd["L"])
                    n = jhi - jlo
                    qoff = jlo + 64 - 128 * m
                    if segs and segs[-1][0] == r and segs[-1][1] + segs[-1][2] == jlo:
                        segs[-1][2] += n
                    else:
                        segs.append([r, jlo, n, ti * 128 + qoff])
                    for kind, mk in ((0, m - 1), (1, m)):
                        if 0 <= mk < gd["nt"]:
                            mm.append((ti, kind, mk, jlo, n, qoff, r))
                return mm, segs

            def emit_qk(gd):
                mm, _ = plan(gd)
                b, d, hh = gd["b"], gd["d"], gd["hh"]
                sb_ = gd["n"] % NPS
                rows = slice(hh * 64, (hh + 1) * 64)
                for ii, (ti, kind, mk, jlo, n, qoff, r) in enumerate(mm):
                    c0 = (2 * ti + kind) * 128 + qoff
                    k.op("pe", lambda e, mk=mk, jlo=jlo, n=n, c0=c0, r=r: e.matmul(
                        pS[:, sb_, c0:c0 + n], lhsT=KT[rows, b, sl(128 * mk * d + r, 128, d)],
                        rhs=QT[rows, b, sl(jlo * d + r, n, d)], start=True, stop=True),
                        reads=[t_KT[b], t_QT[b]], writes=[t_pS[sb_]], signal=(ii == len(mm) - 1), partial=(ii > 0))

            def emit_rest(gd):
                for what, arg in gd["pre"]:
                    if what == "qk":
                        load_qk(arg)
                    else:
                        load_v(arg)
                mm, segs = plan(gd)
                b, d, hh, vb_, nt = gd["b"], gd["d"], gd["hh"], gd["vb"], gd["nt"]
                sb_ = gd["n"] % NPS
                pb = gd["n"] % NPT
                ob = gd["n"] % 2
                vcol = slice(0, 128) if hh == 0 else slice(64, 192)
                k.op("act", lambda e: e.activation(out=pT[:, pb, :], in_=pS[:, sb_, :], func=AF.Exp, scale=0.125),
                     reads=[t_pS[sb_]], writes=[t_pT[pb]])
                meng = "dve" if gd["n"] % 2 == 0 else "pool"
                k.op(meng, lambda e: e.tensor_tensor(out=pTm[:, pb, :], in0=pT[:, pb, :], in1=MASK, op=ALU.mult),
                     reads=[t_pT[pb]], writes=[t_pTm[pb]])
                for ii, (ti, kind, mk, jlo, n, qoff, r) in enumerate(mm):
                    c0 = (2 * ti + kind) * 128 + qoff
                    first = (ii == 0) or (mm[ii - 1][0] != ti)
                    lastk = (ii == len(mm) - 1) or (mm[ii + 1][0] != ti)
                    k.op("pe", lambda e, mk=mk, n=n, c0=c0, ti=ti, qoff=qoff, first=first, lastk=lastk, r=r: e.matmul(
                        pO[:, ob, ti * 128 + qoff:ti * 128 + qoff + n],
                        lhsT=VD[:, vb_, r * nt + mk, vcol], rhs=pTm[:, pb, c0:c0 + n], start=first, stop=lastk),
                        reads=[t_VD[vb_], t_pTm[pb]], writes=[t_pO[ob]], signal=(ii == len(mm) - 1), partial=(ii > 0))
                ib = gd["idx"] % 2
                for (r, jlo, n, po0) in segs:
                    dst = OA[:, ib, hh, sl(jlo * d + r, n, d)]
                    if d == PATTERNS[0]:
                        fk = (gd["idx"], hh)
                        k.op("dve", lambda e, dst=dst, po0=po0, n=n: e.tensor_copy(out=dst, in_=pO[:, ob, po0:po0 + n]),
                             reads=[t_pO[ob]], writes=[t_OA[ib][hh]], partial=(fk in oa_first))
                        oa_first[fk] = True
                    else:
                        k.op("dve", lambda e, dst=dst, po0=po0, n=n: e.tensor_tensor(
                            out=dst, in0=pO[:, ob, po0:po0 + n], in1=dst, op=ALU.add),
                            reads=[t_pO[ob], t_OA[ib][hh]], writes=[t_OA[ib][hh]], partial=True)
                if gd["last"]:
                    queue_finalize(gd)
                elif pending:
                    fn = pending.pop(0)
                    if fn is not None:
                        fn()

            def queue_finalize(gd):
                s, hp, idx, b = gd["s"], gd["hp"], gd["idx"], gd["b"]
                ib = idx % 2
                tA, tB = t_OA[ib][0], t_OA[ib][1]
                k.op("sp", lambda e: e.dma_start(out=DT[0:64, :], in_=OA[64:65, ib, 0, :].rearrange("p (a b) -> p a b", b=S // 64)),
                     reads=[tA], writes=[t_DT], dma=True)
                k.op("sp", lambda e: e.dma_start(out=DT[64:128, :], in_=OA[0:1, ib, 1, :].rearrange("p (a b) -> p a b", b=S // 64)),
                     reads=[tB], writes=[t_DT], dma=True, partial=True)

                def stage_b():
                    k.op("sp", lambda e: e.dma_start(out=G[:, :], in_=self.gateT[s, hp * 128:(hp + 1) * 128, :]),
                         writes=[t_G], dma=True)
                    k.op("dve", lambda e: e.reciprocal(out=RT[:, :], in_=DT[:, :]), reads=[t_DT], writes=[t_RT])
                    k.op("sp", lambda e: e.dma_start(out=RROW[0:1, :].rearrange("p (a b) -> p a b", b=S // 64), in_=RT[0:64, :]),
                         reads=[t_RT], writes=[t_RROW], dma=True)
                    k.op("sp", lambda e: e.dma_start(out=RROW[1:2, :].rearrange("p (a b) -> p a b", b=S // 64), in_=RT[64:128, :]),
                         reads=[t_RT], writes=[t_RROW], dma=True, partial=True)

                def stage_c(c):
                    def run():
                        cols = slice(c * 512, (c + 1) * 512)
                        fb = fin_n[0] % 2
                        fin_n[0] += 1
                        k.op("pe", lambda e: e.matmul(pR[:, :], lhsT=self.SEL2[:, :], rhs=RROW[:, cols], start=True, stop=True),
                             reads=[t_RROW], writes=[t_pR])
                        k.op("dve", lambda e: e.tensor_tensor(out=tg[0:64, fb, :], in0=OA[0:64, ib, 0, cols], in1=G[0:64, cols],
                                                              op=ALU.mult),
                             reads=[tA, t_G], writes=[t_tg[fb]])
                        k.op("dve", lambda e: e.tensor_tensor(out=tg[64:128, fb, :], in0=OA[64:128, ib, 1, cols],
                                                              in1=G[64:128, cols], op=ALU.mult),
                             reads=[tB, t_G], writes=[t_tg[fb]], partial=True)
                        k.op("dve", lambda e: e.tensor_tensor(out=yst[:, fb, :], in0=tg[:, fb, :], in1=pR[:, :], op=ALU.mult),
                             reads=[t_tg[fb], t_pR], writes=[t_yst[fb]])
                        k.op("sp", lambda e: e.dma_start(out=self.yT[s, hp * 128:(hp + 1) * 128, cols], in_=yst[:, fb, :]),
                             reads=[t_yst[fb]], dma=True)
                    return run
                pending.append(None)
                pending.append(stage_b)
                pending.append(None)
                for c in range(self.NC):
                    pending.append(stage_c(c))
                if idx + 2 < len(items):
                    pending.append(lambda: load_qk(idx + 2))

            load_qk(0)
            if len(items) > 1:
                load_qk(1)
            load_v(0)
            load_v(1)
            for n in range(len(groups) + LA):
                if n < len(groups):
                    emit_qk(groups[n])
                if n - LA >= 0:
                    emit_rest(groups[n - LA])
            while pending:
                fn = pending.pop(0)
                if fn is not None:
                    fn()

    def phase_B(self, l):
        k, nseq, S = self.k, self.nseq, self.S
        with ExitStack() as st:
            g = self.alloc_fin(st)
            QT = self.sb(st, "B_QT", [128, 2, S], BF16)
            KT = self.sb(st, "B_KT", [128, 2, S], BF16)
            G = self.sb(st, "B_G", [128, 2, S], BF16)
            V1 = self.sb(st, "B_V1", [128, 2, self.NT, 192], BF16)
            t_QT = [T("QT0"), T("QT1")]
            t_KT = [T("KT0"), T("KT1")]
            t_G = [T("G0"), T("G1")]
            t_V1 = [T("V10"), T("V11")]
            pS = self.ps(st, "B_pS", [128, 2, 1024], F32)
            t_pS = [T("pS0"), T("pS1")]
            pO = self.ps(st, "B_pO", [128, 2, 2, 512], F32)
            t_pO = [T("pO0"), T("pO1")]
            NPT = 3
            pT = self.sb(st, "B_pT", [128, NPT, 1024], BF16)
            t_pT = [T("pT%d" % i) for i in range(NPT)]
            items = [(s, hp) for s in range(nseq) for hp in range(4)]

            def load(idx):
                s, hp = items[idx]
                b = idx % 2
                kv = hp // 2
                k.op("sp", lambda e: e.dma_start(out=QT[:, b, :], in_=self.qkbT[s, hp * 128:(hp + 1) * 128, :]),
                     writes=[t_QT[b]], dma=True)
                k.op("sp", lambda e: e.dma_start(out=KT[0:64, b, :], in_=self.qkbT[s, 512 + kv * 64:512 + (kv + 1) * 64, :]),
                     writes=[t_KT[b]], dma=True)
                k.op("sp", lambda e: e.dma_start(out=KT[64:128, b, :], in_=self.qkbT[s, 512 + kv * 64:512 + (kv + 1) * 64, :]),
                     writes=[t_KT[b]], dma=True, partial=True)
                k.op("sp", lambda e: e.dma_start(out=G[:, b, :], in_=self.gateT[s, 512 + hp * 128:512 + (hp + 1) * 128, :]),
                     writes=[t_G[b]], dma=True)
                k.op("sp", lambda e: e.dma_start(out=V1[:, b, :, :],
                                                 in_=self.vb[s, :, kv, :].rearrange("(m p) c -> p m c", p=128)),
                     writes=[t_V1[b]], dma=True)

            steps = []
            for idx, (s, hp) in enumerate(items):
                for qc in range(self.NC):
                    for kt in range(self.NT):
                        steps.append(dict(idx=idx, s=s, hp=hp, b=idx % 2, qc=qc, kt=kt, n=len(steps),
                                          ob=(idx * self.NC + qc) % 2))

            def emit_qk(sd):
                b, qc, kt = sd["b"], sd["qc"], sd["kt"]
                sb_ = sd["n"] % 2
                for hh in range(2):
                    rows = slice(hh * 64, (hh + 1) * 64)
                    k.op("pe", lambda e, hh=hh, rows=rows: e.matmul(
                        pS[:, sb_, hh * 512:(hh + 1) * 512], lhsT=KT[rows, b, kt * 128:(kt + 1) * 128],
                        rhs=QT[rows, b, qc * 512:(qc + 1) * 512], start=True, stop=True),
                        reads=[t_KT[b], t_QT[b]], writes=[t_pS[sb_]], signal=(hh == 1), partial=(hh == 1))

            def emit_exp(sd):
                if sd["qc"] == 0 and sd["kt"] == 0 and sd["idx"] + 1 < len(items):
                    load(sd["idx"] + 1)
                sb_ = sd["n"] % 2
                pb = sd["n"] % NPT
                k.op("act", lambda e: e.activation(out=pT[:, pb, :], in_=pS[:, sb_, :], func=AF.Exp, scale=0.125),
                     reads=[t_pS[sb_]], writes=[t_pT[pb]])

            def emit_pv(sd):
                b, qc, kt, ob = sd["b"], sd["qc"], sd["kt"], sd["ob"]
                pb = sd["n"] % NPT
                for hh in range(2):
                    vcol = slice(64, 192) if hh == 0 else slice(0, 128)
                    k.op("pe", lambda e, hh=hh, vcol=vcol: e.matmul(
                        pO[:, ob, hh, :], lhsT=V1[:, b, kt, vcol], rhs=pT[:, pb, hh * 512:(hh + 1) * 512],
                        start=(kt == 0), stop=(kt == self.NT - 1)),
                        reads=[t_V1[b], t_pT[pb]], writes=[t_pO[ob]], signal=(hh == 1), partial=not (kt == 0 and hh == 0))
                if kt == self.NT - 1:
                    s, hp = sd["s"], sd["hp"]
                    qcols = slice(qc * 512, (qc + 1) * 512)
                    self.finalize(g, pO[:, ob, 0, :], pO[:, ob, 1, :], [t_pO[ob]], G[:, b, qcols], t_G[b],
                                  self.yT[s, 512 + hp * 128:512 + (hp + 1) * 128, qcols])

            load(0)
            emit_qk(steps[0])
            if len(steps) > 1:
                emit_qk(steps[1])
            for n in range(len(steps)):
                emit_exp(steps[n])
                if n + 2 < len(steps):
                    emit_qk(steps[n + 2])
                emit_pv(steps[n])

    def phase_U(self, l, xin):
        k, nseq = self.k, self.nseq
        pi = l // 2
        with ExitStack() as st:
            f = self.alloc_front(st)
            W = self.sb(st, "U_W", [128, 8, 2048], BF16)
            t_W = T("W")
            wv = self.pool_w_in[pi].rearrange("(kc p) n -> p kc n", p=128)
            for kc in range(8):
                self.load_w(W[:, kc, :], wv[:, kc, :], t_W)
            pj = self.ps(st, "U_pj", [128, 4, 512], F32)
            t_pj = [T("pj%d" % i) for i in range(4)]
            NSTG = 4
            stg = self.sb(st, "U_stg", [128, NSTG, 512], BF16)
            t_stg = [T("stg%d" % i) for i in range(NSTG)]
            ust = self.sb(st, "U_ust", [128, 2, D], BF16)
            t_ust = [T("ust0"), T("ust1")]
            cnt = {"pj": 0, "stg": 0, "u": 0}
            items = [(s, c) for s in range(nseq) for c in range(self.NC)]
            self.load_x(f, xin, items[0][0], items[0][1], 0)
            if len(items) > 1:
                self.load_x(f, xin, items[1][0], items[1][1], 1)
            self.front_a(f, 0)
            self.front_b(f, l, items[0][0], 0)
            for idx, (s, c) in enumerate(items):
                if idx + 2 < len(items):
                    self.load_x(f, xin, items[idx + 2][0], items[idx + 2][1], idx + 2)
                if idx + 1 < len(items):
                    self.front_a(f, idx + 1)
                b = idx % 2
                hT, t_hT = f["hT"], f["t_hT"][b]
                cols = slice(c * 512, (c + 1) * 512)
                for fcg in range(8):
                    j = cnt["pj"] % 4
                    cnt["pj"] += 1
                    for kc in range(8):
                        k.op("pe", lambda e, kc=kc, fcg=fcg, j=j: e.matmul(
                            pj[:, j, :], lhsT=W[:, kc, 1024 + fcg * 128:1024 + (fcg + 1) * 128], rhs=hT[:, b, kc, :],
                            start=(kc == 0), stop=(kc == 7)),
                            reads=[t_W, t_hT], writes=[t_pj[j]], signal=(kc == 7), partial=(kc > 0))
                    si = cnt["stg"] % NSTG
                    cnt["stg"] += 1
                    k.op("act", lambda e, j=j, si=si: e.activation(out=stg[:, si, :], in_=pj[:, j, :], func=AF.Silu),
                         reads=[t_pj[j]], writes=[t_stg[si]])
                    k.op("sp", lambda e, si=si, fcg=fcg: e.dma_start(out=self.gateT[s, fcg * 128:(fcg + 1) * 128, cols],
                                                                   in_=stg[:, si, :]), reads=[t_stg[si]], dma=True)
                if idx + 1 < len(items):
                    self.front_b(f, l, items[idx + 1][0], idx + 1)
                for i in range(4):
                    ub = cnt["u"] % 2
                    cnt["u"] += 1
                    for half in range(2):
                        j = cnt["pj"] % 4
                        cnt["pj"] += 1
                        for kc in range(8):
                            k.op("pe", lambda e, kc=kc, i=i, half=half, j=j: e.matmul(
                                pj[:, j, :], lhsT=hT[:, b, kc, i * 128:(i + 1) * 128], rhs=W[:, kc, half * 512:(half + 1) * 512],
                                start=(kc == 0), stop=(kc == 7)),
                                reads=[t_W, t_hT], writes=[t_pj[j]], signal=(kc == 7), partial=(kc > 0))
                        k.op("dve", lambda e, j=j, ub=ub, half=half: e.tensor_copy(out=ust[:, ub, half * 512:(half + 1) * 512],
                                                                                 in_=pj[:, j, :]),
                             reads=[t_pj[j]], writes=[t_ust[ub]], partial=(half == 1))
                    tok0 = c * 512 + i * 128
                    k.op("sp", lambda e, ub=ub, tok0=tok0: e.dma_start(out=self.u[s, tok0:tok0 + 128, :], in_=ust[:, ub, :]),
                         reads=[t_ust[ub]], dma=True)

    def phase_O(self, l, xin, xout, attn):
        k, nseq = self.k, self.nseq
        wi = l // 2
        with ExitStack() as st:
            Wo = self.sb(st, "O_W", [128, 8, D], BF16)
            t_W = T("Wo")
            wsrc = self.attn_w_out[wi] if attn else self.pool_w_out[wi]
            wv = wsrc.rearrange("(kc p) n -> p kc n", p=128)
            for kc in range(0, 8, 2):
                self.load_w(Wo[:, kc:kc + 2, :], wv[:, kc:kc + 2, :], t_W)
            yTc = self.sb(st, "O_yT", [128, 2, 8, 512], BF16)
            t_yT = [T("yT0"), T("yT1")]
            xt = self.sb(st, "O_xt", [128, 2, 4, D], F32)
            t_xt = [T("xt0"), T("xt1")]
            xo = self.sb(st, "O_xo", [128, 2, 4, D], F32)
            t_xo = [T("xo0"), T("xo1")]
            gpb = self.sb(st, "O_gpb", [128, D], F32)
            t_gpb = T("gpb")
            tt = self.sb(st, "O_tt", [128, 2, D], F32)
            t_tt = [T("tt0"), T("tt1")]
            junk = self.sb(st, "O_junk", [128, D], BF16)
            t_junk = T("junk")
            ss = self.sb(st, "O_ss", [128, 2], F32)
            sd = self.sb(st, "O_sd", [128, 2], F32)
            rs = self.sb(st, "O_rs", [128, 2], F32)
            t_ss = [T("ss0"), T("ss1")]
            t_sd = [T("sd0"), T("sd1")]
            t_rs = [T("rs0"), T("rs1")]
            pm = self.ps(st, "O_pm", [128, 2, 1024], F32)
            t_pm = [T("pm0"), T("pm1")]
            cnt = {"m": 0, "pp": 0, "pg": 0}
            if not attn:
                Wg = self.sb(st, "V_Wg", [128, 4, 2, 256], BF16)
                t_Wg = T("Wg")
                self.load_w(Wg[:, :, :, :], self.pool_w_grp[wi].rearrange("g (cc p) d -> p g cc d", p=128), t_Wg)
                uc = self.sb(st, "V_uc", [128, 2, 4, D], BF16)
                t_uc = [T("uc0"), T("uc1")]
                HPt = self.sb(st, "V_HP", [128, 2, D], BF16)
                HNt = self.sb(st, "V_HN", [128, 2, D], BF16)
                t_HP = [T("HP0"), T("HP1")]
                t_HN = [T("HN0"), T("HN1")]
                for hb in range(2):
                    k.op("pool", lambda e, hb=hb: e.memset(HPt[:, hb, :], 0.0), writes=[t_HP[hb]])
                    k.op("pool", lambda e, hb=hb: e.memset(HNt[:, hb, :], 0.0), writes=[t_HN[hb]])
                gc = self.sb(st, "V_gc", [128, 2, 8, 512], BF16)
                t_gc = [T("gc0"), T("gc1")]
                pooled = self.sb(st, "V_pooled", [128, 8, 512], BF16)
                t_pooled = [T("pooled%d" % i) for i in range(8)]
                pp = self.ps(st, "V_pp", [128, 2, 512], F32)
                t_pp = [T("pp0"), T("pp1")]
                pg = self.ps(st, "V_pg", [128, 2, 512], F32)
                t_pg = [T("pg0"), T("pg1")]
            pgp = self.ps(st, "O_pgp", [128, 2, 512], F32) if attn else None
            t_pgp = T("pgp")
            items = [(s, c) for s in range(nseq) for c in range(self.NC)]

            def prefetch(idx):
                s, c = items[idx]
                b = idx % 2
                xv = xin[s].rearrange("(i p) d -> p i d", p=128)
                k.op("sp", lambda e: e.dma_start(out=xt[:, b, :, :], in_=xv[:, 4 * c:4 * c + 4, :]), writes=[t_xt[b]], dma=True)
                if attn:
                    yv = self.yT[s].rearrange("(kc p) t -> p kc t", p=128)
                    k.op("sp", lambda e: e.dma_start(out=yTc[:, b, :, :], in_=yv[:, :, c * 512:(c + 1) * 512]),
                         writes=[t_yT[b]], dma=True)
                else:
                    uv = self.u[s].rearrange("(i p) d -> p i d", p=128)
                    k.op("sp", lambda e: e.dma_start(out=uc[:, b, :, :], in_=uv[:, 4 * c:4 * c + 4, :]), writes=[t_uc[b]], dma=True)
                    if c > 0:
                        k.op("sp", lambda e: e.dma_start(out=HPt[96:128, b, :], in_=self.u[s, c * 512 - 32:c * 512, :]),
                             writes=[t_HP[b]], dma=True, partial=True)
                    if c < self.NC - 1:
                        k.op("sp", lambda e: e.dma_start(out=HNt[0:32, b, :], in_=self.u[s, (c + 1) * 512:(c + 1) * 512 + 32, :]),
                             writes=[t_HN[b]], dma=True, partial=True)
                    gv = self.gateT[s].rearrange("(kc p) t -> p kc t", p=128)
                    k.op("sp", lambda e: e.dma_start(out=gc[:, b, :, :], in_=gv[:, :, c * 512:(c + 1) * 512]),
                         writes=[t_gc[b]], dma=True)

            prefetch(0)
            for idx, (s, c) in enumerate(items):
                b = idx % 2
                if idx + 1 < len(items):
                    prefetch(idx + 1)
                if c == 0:
                    pgx = pgp if attn else pg
                    t_pgx = [t_pgp, t_pgp] if attn else t_pg
                    for half in range(2):
                        k.op("pe", lambda e, half=half: e.matmul(pgx[:, half, :], lhsT=self.SEL[:, s, :],
                                                                  rhs=self.GP[:, l, half * 512:(half + 1) * 512],
                                                                  start=True, stop=True),
                             writes=[t_pgx[half]])
                        k.op("act", lambda e, half=half: e.activation(out=gpb[:, half * 512:(half + 1) * 512],
                                                                      in_=pgx[:, half, :], func=AF.Copy),
                             reads=[t_pgx[half]], writes=[t_gpb], partial=(half == 1))
                if not attn:
                    for fc in range(8):
                        gI = fc // 2
                        ppb = cnt["pp"] % 2
                        cnt["pp"] += 1
                        ops = []
                        for i in range(4):
                            Tg = 4 * c + i
                            var = 1 if Tg == 0 else (2 if Tg == self.NT - 1 else 0)
                            fcols = slice(fc * 128, (fc + 1) * 128)
                            oc = slice(i * 128, (i + 1) * 128)
                            bc0 = CP_BC + (gI * 3 + var) * 128
                            lst = [(uc[:, b, i, fcols], self.CP[:, bc0:bc0 + 128], [t_uc[b]])]
                            if Tg > 0:
                                bp0 = CP_BP + gI * 128
                                if i > 0:
                                    lst.append((uc[64:128, b, i - 1, fcols], self.CP[64:128, bp0:bp0 + 128], [t_uc[b]]))
                                else:
                                    lst.append((HPt[64:128, b, fcols], self.CP[64:128, bp0:bp0 + 128], [t_HP[b]]))
                            if Tg < self.NT - 1:
                                bn0 = CP_BN + gI * 128
                                if i < 3:
                                    lst.append((uc[0:32, b, i + 1, fcols], self.CP[0:32, bn0:bn0 + 128], [t_uc[b]]))
                                else:
                                    lst.append((HNt[0:32, b, fcols], self.CP[0:32, bn0:bn0 + 128], [t_HN[b]]))
                            for q, (lh, rh, rd) in enumerate(lst):
                                ops.append((oc, lh, rh, rd, q == 0, q == len(lst) - 1))
                        for q, (oc, lh, rh, rd, stt, stp) in enumerate(ops):
                            k.op("pe", lambda e, oc=oc, lh=lh, rh=rh, stt=stt, stp=stp: e.matmul(
                                pp[:, ppb, oc], lhsT=lh, rhs=rh, start=stt, stop=stp),
                                reads=rd, writes=[t_pp[ppb]], signal=(q == len(ops) - 1), partial=(q > 0))
                        k.op("act", lambda e, fc=fc, ppb=ppb: e.activation(out=pooled[:, fc, :], in_=pp[:, ppb, :], func=AF.Copy),
                             reads=[t_pp[ppb]], writes=[t_pooled[fc]])
                    for dc in range(8):
                        gI = dc // 2
                        pgb = cnt["pg"] % 2
                        cnt["pg"] += 1
                        for cc in range(2):
                            k.op("pe", lambda e, dc=dc, gI=gI, cc=cc, pgb=pgb: e.matmul(
                                pg[:, pgb, :], lhsT=Wg[:, gI, cc, (dc % 2) * 128:(dc % 2 + 1) * 128],
                                rhs=pooled[:, 2 * gI + cc, :], start=(cc == 0), stop=(cc == 1)),
                                reads=[t_Wg, t_pooled[2 * gI + cc]], writes=[t_pg[pgb]], signal=(cc == 1), partial=(cc == 1))
                        k.op("dve", lambda e, dc=dc, pgb=pgb: e.scalar_tensor_tensor(
                            out=yTc[:, b, dc, :], in0=pg[:, pgb, :], scalar=self.PSC[:, wi, dc:dc + 1], in1=gc[:, b, dc, :],
                            op0=ALU.mult, op1=ALU.mult),
                            reads=[t_pg[pgb], t_gc[b]], writes=[t_yT[b]], partial=(dc > 0))
                for i in range(4):
                    mb = cnt["m"] % 2
                    cnt["m"] += 1
                    for half in range(2):
                        for kc in range(8):
                            k.op("pe", lambda e, i=i, half=half, kc=kc, mb=mb: e.matmul(
                                pm[:, mb, half * 512:(half + 1) * 512], lhsT=yTc[:, b, kc, i * 128:(i + 1) * 128],
                                rhs=Wo[:, kc, half * 512:(half + 1) * 512], start=(kc == 0), stop=(kc == 7)),
                                reads=[t_W, t_yT[b]], writes=[t_pm[mb]], signal=(kc == 7 and half == 1),
                                partial=not (kc == 0 and half == 0))
                    k.op("act", lambda e, mb=mb: e.activation(out=junk[:, :], in_=pm[:, mb, :], func=AF.Square,
                                                              accum_out=ss[:, mb:mb + 1]),
                         reads=[t_pm[mb]], writes=[t_junk, t_ss[mb]])
                    k.op("act", lambda e, mb=mb: e.activation(out=sd[:, mb:mb + 1], in_=ss[:, mb:mb + 1], func=AF.Sqrt,
                                                              scale=1.0 / D, bias=self.EPS[:, 0:1]),
                         reads=[t_ss[mb]], writes=[t_sd[mb]])
                    k.op("dve", lambda e, mb=mb: e.reciprocal(out=rs[:, mb:mb + 1], in_=sd[:, mb:mb + 1]),
                         reads=[t_sd[mb]], writes=[t_rs[mb]])
                    k.op("dve", lambda e, mb=mb: e.scalar_tensor_tensor(
                        out=tt[:, mb, :], in0=pm[:, mb, :], scalar=rs[:, mb:mb + 1], in1=gpb[:, :],
                        op0=ALU.mult, op1=ALU.mult),
                        reads=[t_pm[mb], t_rs[mb], t_gpb], writes=[t_tt[mb]])
                    k.op("pool", lambda e, i=i, mb=mb: e.tensor_tensor(out=xo[:, b, i, :], in0=tt[:, mb, :], in1=xt[:, b, i, :],
                                                                      op=ALU.add),
                         reads=[t_tt[mb], t_xt[b]], writes=[t_xo[b]], partial=(i > 0))
                ov = xout[s].rearrange("(i p) d -> p i d", p=128)
                k.op("sp", lambda e: e.dma_start(out=ov[:, 4 * c:4 * c + 4, :], in_=xo[:, b, :, :]), reads=[t_xo[b]], dma=True)


_PROG_CACHE = {}


def _get_prog(nseq, S, layers=(0, 1, 2, 3)):
    key = (nseq, S, tuple(layers))
    if key not in _PROG_CACHE:
        _PROG_CACHE[key] = Prog(nseq, S, layers)
    return _PROG_CACHE[key]


def _shared_inputs(nseq, S, ada_w, ada_b, pre_norm, post_norm, attn_w_in, attn_q_norm, attn_k_norm,
                   attn_w_out, pool_w_in, pool_w_grp, pool_scale, pool_w_out):
    f = lambda a: np.ascontiguousarray(np.asarray(a, dtype=np.float32))
    ada_b = f(ada_b)
    m = {}
    m["ada_w"] = f(ada_w)
    m["ada_bT"] = np.ascontiguousarray(ada_b[:, :2048].reshape(4, 16, 128).transpose(2, 0, 1))
    m["ada_bg"] = np.ascontiguousarray(np.broadcast_to(ada_b[None, :, 2048:], (nseq, 4, D)))
    m["pre_T"] = np.ascontiguousarray(f(pre_norm).reshape(4, 8, 128).transpose(2, 0, 1))
    m["post_rep"] = np.ascontiguousarray(np.broadcast_to(f(post_norm)[None], (nseq, 4, D)))
    m["attn_w_in"] = f(attn_w_in)
    qg = np.stack([f(attn_q_norm), f(attn_k_norm)], axis=-1)
    m["qk_g"] = np.ascontiguousarray(np.concatenate([qg, qg], axis=1).transpose(1, 0, 2))
    m["attn_w_out"] = f(attn_w_out)
    m["pool_w_in"] = f(pool_w_in)
    m["pool_w_grp"] = f(pool_w_grp)
    m["pool_scT"] = np.ascontiguousarray(f(pool_scale).reshape(2, 8, 128).transpose(2, 0, 1))
    m["pool_w_out"] = f(pool_w_out)
    m.update(_consts(S, nseq))
    return m


def run_trunk(xs, cs, weights, n_cores, layers=(0, 1, 2, 3)):
    nseq, S = xs.shape[1], xs.shape[2]
    prog = _get_prog(nseq, S, layers)
    shared = _shared_inputs(nseq, S, **weights)
    in_maps = []
    for i in range(n_cores):
        m = dict(shared)
        m["x"] = np.ascontiguousarray(xs[i])
        m["cT"] = np.ascontiguousarray(cs[i].reshape(nseq, 8, 128).transpose(2, 1, 0))
        in_maps.append(m)
    res = run_bass_kernel_spmd(prog.nc, in_maps, core_ids=list(range(n_cores)))
    return np.stack([np.asarray(r["y"]) for r in res.results], axis=0)


def kernel(x_prompt, x_sample, c_prompt, c_sample, ada_w, ada_b, pre_norm, post_norm,
           attn_w_in, attn_q_norm, attn_k_norm, attn_w_out,
           pool_w_in, pool_w_grp, pool_scale, pool_w_out):
    x_prompt = np.asarray(x_prompt, dtype=np.float32)
    x_sample = np.asarray(x_sample, dtype=np.float32)
    c_prompt = np.asarray(c_prompt, dtype=np.float32)
    c_sample = np.asarray(c_sample, dtype=np.float32)
    nb_p, nb_s = x_prompt.shape[0], x_sample.shape[0]
    nreal = nb_p + nb_s
    nslots = N_CORES * NSEQ_FULL
    xs = np.empty((nslots, S_FULL, D), np.float32)
    cs = np.empty((nslots, D), np.float32)
    xs[:nb_p] = x_prompt
    xs[nb_p:nreal] = x_sample
    cs[:nb_p] = c_prompt
    cs[nb_p:nreal] = c_sample
    for j in range(nreal, nslots):
        xs[j] = xs[j - nreal]
        cs[j] = cs[j - nreal]
    weights = dict(ada_w=ada_w, ada_b=ada_b, pre_norm=pre_norm, post_norm=post_norm, attn_w_in=attn_w_in,
                   attn_q_norm=attn_q_norm, attn_k_norm=attn_k_norm, attn_w_out=attn_w_out, pool_w_in=pool_w_in,
                   pool_w_grp=pool_w_grp, pool_scale=pool_scale, pool_w_out=pool_w_out)
    y = run_trunk(xs.reshape(N_CORES, NSEQ_FULL, S_FULL, D), cs.reshape(N_CORES, NSEQ_FULL, D), weights, N_CORES)
    y = y.reshape(nslots, S_FULL, D)
    return (np.ascontiguousarray(y[:nb_p]), np.ascontiguousarray(y[nb_p:nreal]))
```

```python
import math
from contextlib import ExitStack

import numpy as np
import ml_dtypes

import concourse.bass as bass
import concourse.mybir as mybir
from concourse.bass_utils import run_bass_kernel_spmd

F32 = mybir.dt.float32
BF16 = mybir.dt.bfloat16
AF = mybir.ActivationFunctionType
ALU = mybir.AluOpType
NPBF = ml_dtypes.bfloat16

D = 1024
HD = 64
EPS = 1e-6
PATTERNS = (1, 4, 16)
POOL_HALF = (1, 2, 4, 8)
N_CORES = 8
NSEQ_FULL = 3
S_FULL = 4096


def sl(start, count, step=1):
    return slice(start, start + (count - 1) * step + 1, step)


class Ref:
    __slots__ = ("kind", "eng", "sem", "value")

    def __init__(self, kind, eng, sem, value):
        self.kind, self.eng, self.sem, self.value = kind, eng, sem, value


class T:
    __slots__ = ("name", "writers", "readers")

    def __init__(self, name):
        self.name = name
        self.writers = []
        self.readers = {}


class EngS:
    def __init__(self, name, h, sem):
        self.name, self.h, self.sem = name, h, sem
        self.count = 0
        self.pending = []
        self.waited = {}


class KB:
    def __init__(self, nc, stack, n_dma_sems=40):
        self.nc = nc
        self.E = {}
        for name, h in (("pe", nc.tensor), ("act", nc.scalar), ("dve", nc.vector),
                        ("pool", nc.gpsimd), ("sp", nc.sync)):
            sem = stack.enter_context(nc.semaphore("sem_" + name))
            self.E[name] = EngS(name, h, sem)
        self.dsems = [[stack.enter_context(nc.semaphore("dsem%d" % i)), 0] for i in range(n_dma_sems)]
        self.dnext = 0
        self.swsems = [[stack.enter_context(nc.semaphore("swsem%d" % i)), 0] for i in range(8)]
        self.swnext = 0
        self.nops = 0

    def _wait(self, E, sem, val):
        key = id(sem)
        if E.waited.get(key, 0) >= val:
            return
        E.h.wait_ge(sem, val)
        E.waited[key] = val

    def op(self, eng, fn, reads=(), writes=(), signal=True, partial=False, dma=False):
        E = self.E[eng]
        self.nops += 1
        for t in reads:
            for r in t.writers:
                self._dep(E, eng, r, "raw")
        for t in writes:
            for r in t.readers.values():
                self._dep(E, eng, r, "war")
            for r in t.writers:
                self._dep(E, eng, r, "waw")
        if dma:
            if eng == "pool":
                slot = self.swsems[self.swnext]
                self.swnext = (self.swnext + 1) % len(self.swsems)
            else:
                slot = self.dsems[self.dnext]
                self.dnext = (self.dnext + 1) % len(self.dsems)
            if slot[1] > 0:
                self._wait(E, slot[0], slot[1])
            ins = fn(E.h)
            ins.then_inc(slot[0], 16)
            slot[1] += 16
            ref = Ref("dma", eng, slot[0], slot[1])
        else:
            ins = fn(E.h)
            ref = Ref("eng", eng, E.sem, None)
            if signal:
                E.count += 1
                ins.then_inc(E.sem, 1)
                ref.value = E.count
                for p in E.pending:
                    p.value = E.count
                E.pending = []
            else:
                E.pending.append(ref)
        for t in reads:
            t.readers[eng if not dma else ("dma", id(ref))] = ref
        for t in writes:
            if partial:
                t.writers.append(ref)
            else:
                t.writers = [ref]
                t.readers = {}
        return ref

    def _dep(self, E, eng, r, kind):
        if r.kind == "eng" and r.eng == eng:
            if eng == "pe":
                return
        assert r.value is not None, "dependency on an unsignaled op (engine %s)" % r.eng
        self._wait(E, r.sem, r.value)

    def barrier(self):
        for E in self.E.values():
            assert not E.pending, "pending unsignaled ops on %s at barrier" % E.name
        for E in self.E.values():
            for F in self.E.values():
                if F is not E and F.count > 0:
                    self._wait(E, F.sem, F.count)
            for sem, cnt in self.dsems + self.swsems:
                if cnt > 0:
                    self._wait(E, sem, cnt)


def _rope_tables(S):
    t = np.arange(S, dtype=np.float32)
    invA = (np.float32(500000.0) ** (-np.arange(0, 16, 2, dtype=np.float32) / np.float32(16))).astype(np.float32)
    angA = (t[:, None] * invA[None, :]).astype(np.float32)
    CA = np.ones((128, S), np.float32)
    SA = np.zeros((128, S), np.float32)
    for p in range(128):
        dd = p % 64
        if dd < 16:
            CA[p] = np.cos(angA[:, dd % 8])
            SA[p] = np.sin(angA[:, dd % 8])
    invB = (np.float32(10000.0) ** (-np.arange(0, 32, 2, dtype=np.float32) / np.float32(32))).astype(np.float32)
    row = (np.arange(S) // 64).astype(np.float32)
    col = (np.arange(S) % 64).astype(np.float32)
    CB = np.zeros((128, S), np.float32)
    SB = np.zeros((128, S), np.float32)
    for p in range(128):
        dd = p % 64
        pos = row if dd < 32 else col
        f = (dd % 32) % 16
        ang = (pos * invB[f]).astype(np.float32)
        CB[p] = np.cos(ang)
        SB[p] = np.sin(ang)
    return CA, SA, CB, SB


def _rot_mats():
    RA = np.zeros((128, 128), np.float32)
    RB = np.zeros((128, 128), np.float32)
    for m in range(128):
        dd = m % 64
        if dd < 8:
            RA[m + 8, m] = -1.0
        elif dd < 16:
            RA[m - 8, m] = 1.0
        e = dd % 32
        if e < 16:
            RB[m + 16, m] = -1.0
        else:
            RB[m - 16, m] = 1.0
    return RA, RB


def _band_mats():
    BC = np.zeros((128, 4, 3, 128), np.float32)
    BP = np.zeros((128, 4, 128), np.float32)
    BN = np.zeros((128, 4, 128), np.float32)
    for g, h in enumerate(POOL_HALF):
        for var in range(3):
            for tout in range(128):
                lo, hi = tout - h, tout + h
                clo = max(lo, 0) if var == 1 else lo
                chi = min(hi, 128) if var == 2 else hi
                cnt = chi - clo
                for tin in range(max(clo, 0), min(chi, 128)):
                    BC[tin, g, var, tout] += 1.0 / cnt
                BC[tout, g, var, tout] -= 1.0
        for tout in range(128):
            for i in range(32):
                if -32 + i >= tout - h:
                    BP[96 + i, g, tout] = 1.0 / (2 * h)
                if 128 + i < tout + h:
                    BN[i, g, tout] = 1.0 / (2 * h)
    return BC, BP, BN


def _consts(S, nseq):
    c = {}
    CA, SA, CB, SB = _rope_tables(S)
    c["rope"] = np.stack([CA, SA, CB, SB], axis=1).astype(np.float32)
    RA, RB = _rot_mats()
    ident = np.eye(128, dtype=np.float32)
    bd = np.zeros((128, 128), np.float32)
    bd[:64, :64] = 1.0
    bd[64:, 64:] = 1.0
    kj = np.arange(128)[:, None]
    qi = np.arange(128)[None, :]
    mlo = (kj >= qi).astype(np.float32)
    mhi = (kj <= qi).astype(np.float32)
    mask = np.concatenate([mlo, mhi] * 4, axis=1)
    BC, BP, BN = _band_mats()
    pack = np.concatenate([ident, RA, RB, bd, mask, BC.reshape(128, -1), BP.reshape(128, -1),
                           BN.reshape(128, -1)], axis=1)
    c["cpack"] = pack.astype(NPBF)
    sel = np.zeros((nseq, nseq, 128), np.float32)
    for s in range(nseq):
        sel[s, s, :] = 1.0
    c["sel"] = sel
    sel2 = np.zeros((2, 128), np.float32)
    sel2[0, :64] = 1.0
    sel2[1, 64:] = 1.0
    c["sel2"] = sel2
    return c


CP_IDENT, CP_RA, CP_RB, CP_BD, CP_MASK = 0, 128, 256, 384, 512
CP_BC = 512 + 1024
CP_BP = CP_BC + 4 * 3 * 128
CP_BN = CP_BP + 4 * 128
CP_END = CP_BN + 4 * 128


class Prog:
    def __init__(self, nseq, S, layers=(0, 1, 2, 3)):
        self.nseq, self.S, self.layers = nseq, S, tuple(layers)
        self.NT = S // 128
        self.NC = S // 512
        nc = bass.Bass("TRN2", target_bir_lowering=False)
        self.nc = nc
        dt = nc.dram_tensor
        self.x_in = dt("x", [nseq, S, D], F32, kind="ExternalInput")
        self.cT = dt("cT", [128, 8, nseq], F32, kind="ExternalInput")
        self.ada_w = dt("ada_w", [4, D, 3 * D], F32, kind="ExternalInput")
        self.ada_bT = dt("ada_bT", [128, 4, 16], F32, kind="ExternalInput")
        self.ada_bg = dt("ada_bg", [nseq, 4, D], F32, kind="ExternalInput")
        self.pre_T = dt("pre_T", [128, 4, 8], F32, kind="ExternalInput")
        self.post_rep = dt("post_rep", [nseq, 4, D], F32, kind="ExternalInput")
        self.attn_w_in = dt("attn_w_in", [2, D, 3328], F32, kind="ExternalInput")
        self.qk_g = dt("qk_g", [128, 2, 2], F32, kind="ExternalInput")
        self.attn_w_out = dt("attn_w_out", [2, D, D], F32, kind="ExternalInput")
        self.pool_w_in = dt("pool_w_in", [2, D, 2 * D], F32, kind="ExternalInput")
        self.pool_w_grp = dt("pool_w_grp", [2, 4, 256, 256], F32, kind="ExternalInput")
        self.pool_scT = dt("pool_scT", [128, 2, 8], F32, kind="ExternalInput")
        self.pool_w_out = dt("pool_w_out", [2, D, D], F32, kind="ExternalInput")
        self.rope = dt("rope", [128, 4, S], F32, kind="ExternalInput")
        self.cpack = dt("cpack", [128, CP_END], BF16, kind="ExternalInput")
        self.sel = dt("sel", [nseq, nseq, 128], F32, kind="ExternalInput")
        self.sel2 = dt("sel2", [2, 128], F32, kind="ExternalInput")
        self.y_out = dt("y", [nseq, S, D], F32, kind="ExternalOutput")
        self.xs = [dt("xs0", [nseq, S, D], F32, kind="Internal"), dt("xs1", [nseq, S, D], F32, kind="Internal")]
        self.qkaT = dt("qkaT", [nseq, 1024, S], BF16, kind="Internal")
        self.qkbT = dt("qkbT", [nseq, 640, S], BF16, kind="Internal")
        self.gateT = dt("gateT", [nseq, 1024, S], BF16, kind="Internal")
        self.va = dt("va", [nseq, S, 4, 192], BF16, kind="Internal")
        self.vb = dt("vb", [nseq, S, 2, 192], BF16, kind="Internal")
        self.yT = dt("yT", [nseq, 1024, S], BF16, kind="Internal")
        self.u = dt("u", [nseq, S, D], BF16, kind="Internal")
        self.build()

    def sb(self, st, name, shape, dtype):
        self._uid = getattr(self, "_uid", 0) + 1
        return st.enter_context(self.nc.sbuf_tensor("%s_%d" % (name, self._uid), list(shape), dtype))

    def ps(self, st, name, shape, dtype):
        self._uid = getattr(self, "_uid", 0) + 1
        return st.enter_context(self.nc.psum_tensor("%s_%d" % (name, self._uid), list(shape), dtype))

    def build(self):
        nc = self.nc
        with ExitStack() as st:
            k = KB(nc, st)
            self.k = k
            self.CP = self.sb(st, "cpk", [128, CP_END], BF16)
            self.SC = self.sb(st, "SC", [128, 4, self.nseq, 8], F32)
            self.SH = self.sb(st, "SH", [128, 4, self.nseq, 8], F32)
            self.GP = self.sb(st, "GP", [self.nseq, 4, D], F32)
            self.SEL = self.sb(st, "SEL", [self.nseq, self.nseq, 128], F32)
            self.SEL2 = self.sb(st, "SEL2", [2, 128], F32)
            self.QKG = self.sb(st, "QKG", [128, 2, 2], F32)
            self.PSC = self.sb(st, "PSC", [128, 2, 8], F32)
            self.EPS = self.sb(st, "EPSt", [128, 1], F32)
            tc = T("consts")
            k.op("sp", lambda e: e.dma_start(out=self.CP[:, :], in_=self.cpack[:, :]), writes=[tc], dma=True)
            k.op("sp", lambda e: e.dma_start(out=self.SEL[:, :, :], in_=self.sel[:, :, :]), writes=[tc], partial=True, dma=True)
            k.op("sp", lambda e: e.dma_start(out=self.QKG[:, :, :], in_=self.qk_g[:, :, :]), writes=[tc], partial=True, dma=True)
            k.op("sp", lambda e: e.dma_start(out=self.SEL2[:, :], in_=self.sel2[:, :]), writes=[tc], partial=True, dma=True)
            k.op("sp", lambda e: e.dma_start(out=self.PSC[:, :, :], in_=self.pool_scT[:, :, :]), writes=[tc], partial=True, dma=True)
            k.op("dve", lambda e: e.memset(self.EPS[:, :], EPS), writes=[tc], partial=True)
            self.prologue()
            k.barrier()
            xin = self.x_in
            for li, l in enumerate(self.layers):
                last = li == len(self.layers) - 1
                xout = self.y_out if last else self.xs[li % 2]
                if l % 2 == 0:
                    self.phase_P(l, xin)
                    k.barrier()
                    self.phase_A(l)
                    k.barrier()
                    self.phase_B(l)
                    k.barrier()
                    self.phase_O(l, xin, xout, attn=True)
                    k.barrier()
                else:
                    self.phase_U(l, xin)
                    k.barrier()
                    self.phase_O(l, xin, xout, attn=False)
                    k.barrier()
                xin = xout
            k.barrier()

    def prologue(self):
        k, nseq = self.k, self.nseq
        with ExitStack() as st:
            cT = self.sb(st, "cTs", [128, 8, nseq], F32)
            scT = self.sb(st, "scT", [128, 8, nseq], F32)
            bT = self.sb(st, "bT", [128, 4, 16], F32)
            preT = self.sb(st, "preT", [128, 4, 8], F32)
            bg = self.sb(st, "bg", [nseq, 4, D], F32)
            pr = self.sb(st, "pr", [nseq, 4, D], F32)
            blk = self.sb(st, "blk", [128, 2, 8, 512], F32)
            modT = self.sb(st, "modT", [128, 16, nseq], F32)
            pm = self.ps(st, "pm", [128, 2, 512], F32)
            pg = self.ps(st, "pg", [128, 2, 512], F32)
            t_in = T("pin")
            k.op("sp", lambda e: e.dma_start(out=cT[:, :, :], in_=self.cT[:, :, :]), writes=[t_in], dma=True)
            for dst, src in ((bT, self.ada_bT), (preT, self.pre_T), (bg, self.ada_bg), (pr, self.post_rep)):
                k.op("sp", lambda e, dst=dst, src=src: e.dma_start(out=dst[:, :, :], in_=src[:, :, :]),
                     writes=[t_in], partial=True, dma=True)
            t_sc = T("scT")
            k.op("act", lambda e: e.activation(out=scT[:, :, :], in_=cT[:, :, :], func=AF.Silu),
                 reads=[t_in], writes=[t_sc])
            t_blk = [T("blk0"), T("blk1")]
            t_pm = [T("pm0"), T("pm1")]
            t_pg = T("pg")
            t_mod = T("modT")
            t_out = T("modout")
            nb_i = 0
            for l in self.layers:
                wv = self.ada_w[l].rearrange("(kc p) n -> p kc n", p=128)
                for nb in range(6):
                    b = nb_i % 2
                    nb_i += 1
                    k.op("sp", lambda e, b=b, nb=nb, wv=wv: e.dma_start(out=blk[:, b, :, :], in_=wv[:, :, nb * 512:(nb + 1) * 512]),
                         writes=[t_blk[b]], dma=True)
                    if nb < 4:
                        pb = nb % 2
                        for j in range(4):
                            for kc in range(8):
                                k.op("pe", lambda e, b=b, j=j, kc=kc, pb=pb: e.matmul(
                                    pm[:, pb, j * nseq:(j + 1) * nseq], lhsT=blk[:, b, kc, j * 128:(j + 1) * 128],
                                    rhs=scT[:, kc, :], start=(kc == 0), stop=(kc == 7)),
                                    reads=[t_blk[b], t_sc], writes=[t_pm[pb]], signal=(kc == 7 and j == 3),
                                    partial=not (j == 0 and kc == 0))
                        k.op("dve", lambda e, nb=nb, pb=pb: e.tensor_copy(
                            out=modT[:, nb * 4:(nb + 1) * 4, :],
                            in_=pm[:, pb, 0:4 * nseq].rearrange("p (j s) -> p j s", s=nseq)),
                            reads=[t_pm[pb]], writes=[t_mod], partial=(nb > 0))
                    else:
                        half = nb - 4
                        for kc in range(8):
                            k.op("pe", lambda e, b=b, kc=kc, half=half: e.matmul(
                                pg[0:nseq, half, :], lhsT=scT[:, kc, :], rhs=blk[:, b, kc, :],
                                start=(kc == 0), stop=(kc == 7)),
                                reads=[t_blk[b], t_sc], writes=[t_pg], signal=(kc == 7),
                                partial=not (half == 0 and kc == 0))
                for s in range(nseq):
                    k.op("dve", lambda e, l=l, s=s: e.tensor_tensor(
                        out=self.SH[:, l, s, :], in0=modT[:, 0:8, s], in1=bT[:, l, 0:8], op=ALU.add),
                        reads=[t_mod, t_in], writes=[t_out], partial=True)
                    k.op("dve", lambda e, l=l, s=s: e.tensor_tensor(
                        out=self.SC[:, l, s, :], in0=modT[:, 8:16, s], in1=bT[:, l, 8:16], op=ALU.add),
                        reads=[t_mod, t_in], writes=[t_out], partial=True)
                    k.op("dve", lambda e, l=l, s=s: e.scalar_tensor_tensor(
                        out=self.SC[:, l, s, :], in0=self.SC[:, l, s, :], scalar=1.0, in1=preT[:, l, :],
                        op0=ALU.add, op1=ALU.mult),
                        reads=[t_out, t_in], writes=[t_out], partial=True)
                k.op("dve", lambda e, l=l: e.tensor_tensor(
                    out=self.GP[:, l, :], in0=pg[0:nseq, :, :].rearrange("p a b -> p (a b)"), in1=bg[:, l, :], op=ALU.add),
                    reads=[t_pg, t_in], writes=[t_out], partial=True)
                k.op("dve", lambda e, l=l: e.tensor_tensor(
                    out=self.GP[:, l, :], in0=self.GP[:, l, :], in1=pr[:, l, :], op=ALU.mult),
                    reads=[t_out, t_in], writes=[t_out], partial=True)
            k.barrier()

    def alloc_front(self, st):
        f = {}
        f["xt"] = self.sb(st, "f_xt", [128, 2, 4, D], F32)
        f["t_xt"] = [T("xt0"), T("xt1")]
        f["junk"] = self.sb(st, "f_junk", [128, D], BF16)
        f["t_junk"] = T("junk")
        f["ss"] = self.sb(st, "f_ss", [128, 2, 4], F32)
        f["sd"] = self.sb(st, "f_sd", [128, 2, 4], F32)
        f["rs"] = self.sb(st, "f_rs", [128, 2, 4], F32)
        f["t_ss"] = [T("ss0"), T("ss1")]
        f["t_sd"] = [T("sd0"), T("sd1")]
        f["t_rs"] = [T("rs0"), T("rs1")]
        f["xn"] = self.sb(st, "f_xn", [128, 4, D], BF16)
        f["t_xn"] = [T("xn%d" % i) for i in range(4)]
        f["hT"] = self.sb(st, "f_hT", [128, 2, 8, 512], BF16)
        f["t_hT"] = [T("hT0"), T("hT1")]
        f["ptr"] = self.ps(st, "f_ptr", [128, 2, 1024], BF16)
        f["t_ptr"] = [T("ptr0"), T("ptr1")]
        f["n"] = 0
        return f

    def load_x(self, f, xin, s, c, idx):
        b = idx % 2
        xv = xin[s].rearrange("(i p) d -> p i d", p=128)
        self.k.op("sp", lambda e: e.dma_start(out=f["xt"][:, b, :, :], in_=xv[:, 4 * c:4 * c + 4, :]),
                  writes=[f["t_xt"][b]], dma=True)

    def front_a(self, f, idx):
        k = self.k
        b = idx % 2
        xt = f["xt"]
        for i in range(4):
            k.op("act", lambda e, i=i: e.activation(out=f["junk"][:, :], in_=xt[:, b, i, :], func=AF.Square,
                                                     accum_out=f["ss"][:, b, i:i + 1]),
                 reads=[f["t_xt"][b]], writes=[f["t_junk"], f["t_ss"][b]], partial=(i > 0))
        k.op("act", lambda e: e.activation(out=f["sd"][:, b, :], in_=f["ss"][:, b, :], func=AF.Sqrt,
                                           scale=1.0 / D, bias=self.EPS[:, 0:1]),
             reads=[f["t_ss"][b]], writes=[f["t_sd"][b]])
        k.op("dve", lambda e: e.reciprocal(out=f["rs"][:, b, :], in_=f["sd"][:, b, :]),
             reads=[f["t_sd"][b]], writes=[f["t_rs"][b]])
        for i in range(4):
            k.op("dve", lambda e, i=i: e.tensor_scalar(out=f["xn"][:, i, :], in0=xt[:, b, i, :],
                                                        scalar1=f["rs"][:, b, i:i + 1], scalar2=None, op0=ALU.mult),
                 reads=[f["t_xt"][b], f["t_rs"][b]], writes=[f["t_xn"][i]])

    def front_b(self, f, l, s, idx):
        k = self.k
        b = idx % 2
        ident = self.CP[:, CP_IDENT:CP_IDENT + 128]
        for fc in range(8):
            pb = f["n"] % 2
            f["n"] += 1
            for i in range(4):
                k.op("pe", lambda e, i=i, fc=fc, pb=pb: e.transpose(
                    out=f["ptr"][:, pb, i * 128:(i + 1) * 128], in_=f["xn"][:, i, fc * 128:(fc + 1) * 128], identity=ident),
                    reads=[f["t_xn"][i]], writes=[f["t_ptr"][pb]], signal=(i == 3), partial=(i > 0))
            k.op("act", lambda e, fc=fc, pb=pb: e.activation(
                out=f["hT"][:, b, fc, :], in_=f["ptr"][:, pb, 0:512], func=AF.Identity,
                scale=self.SC[:, l, s, fc:fc + 1], bias=self.SH[:, l, s, fc:fc + 1]),
                reads=[f["t_ptr"][pb]], writes=[f["t_hT"][b]], partial=(fc > 0))
        return b

    def load_w(self, dst, src_view, t):
        self.k.op("pool", lambda e: e.dma_start(out=dst, in_=src_view), writes=[t], dma=True, partial=True)

    def run_skewed(self, tasks):
        maxage = max(len(t) for t in tasks)
        for n in range(len(tasks) + maxage):
            for age in range(maxage - 1, -1, -1):
                m = n - age
                if 0 <= m < len(tasks) and age < len(tasks[m]) and tasks[m][age] is not None:
                    tasks[m][age]()

    def phase_P(self, l, xin):
        k, nseq = self.k, self.nseq
        ai = l // 2
        with ExitStack() as st:
            f = self.alloc_front(st)
            W = self.sb(st, "P_W", [128, 8, 3328], BF16)
            t_W = T("W")
            wv = self.attn_w_in[ai].rearrange("(kc p) n -> p kc n", p=128)
            for kc in range(8):
                self.load_w(W[:, kc, :], wv[:, kc, :], t_W)
            rp = self.sb(st, "P_rope", [128, 2, 4, 512], F32)
            t_rp = [T("rp0"), T("rp1")]
            NPJ = 4
            pj = self.ps(st, "P_pj", [128, NPJ, 512], F32)
            t_pj = [T("pj%d" % i) for i in range(NPJ)]
            px = self.ps(st, "P_px", [128, 2, 512], F32)
            t_px = [T("px0"), T("px1")]
            NSTG = 6
            stg = self.sb(st, "P_stg", [128, NSTG, 512], BF16)
            t_stg = [T("stg%d" % i) for i in range(NSTG)]
            NR = 3
            q16 = self.sb(st, "P_q16", [128, NR, 512], BF16)
            sq = self.sb(st, "P_sq", [128, NR, 512], BF16)
            sdv = self.sb(st, "P_sdv", [128, NR, 512], F32)
            rsv = self.sb(st, "P_rsv", [128, NR, 512], F32)
            t1 = self.sb(st, "P_t1", [128, NR, 512], F32)
            t2 = self.sb(st, "P_t2", [128, NR, 512], F32)
            t_q16 = [T("q16_%d" % i) for i in range(NR)]
            t_sq = [T("sq%d" % i) for i in range(NR)]
            t_sdv = [T("sdv%d" % i) for i in range(NR)]
            t_rsv = [T("rsv%d" % i) for i in range(NR)]
            t_t1 = [T("t1_%d" % i) for i in range(NR)]
            t_t2 = [T("t2_%d" % i) for i in range(NR)]
            vst = self.sb(st, "P_vst", [128, 2, 4, 3, 64], BF16)
            vbst = self.sb(st, "P_vbst", [128, 2, 2, 3, 64], BF16)
            t_vst = [T("vst0"), T("vst1")]
            t_vbst = [T("vbst0"), T("vbst1")]
            for b in range(2):
                k.op("pool", lambda e, b=b: e.memset(vst[:, b, :, :, :], 1.0), writes=[t_vst[b]])
                k.op("pool", lambda e, b=b: e.memset(vbst[:, b, :, :, :], 1.0), writes=[t_vbst[b]])
            RA = self.CP[:, CP_RA:CP_RA + 128]
            RB = self.CP[:, CP_RB:CP_RB + 128]
            BD = self.CP[:, CP_BD:CP_BD + 128]
            ident = self.CP[:, CP_IDENT:CP_IDENT + 128]
            cnt = {"pj": 0, "stg": 0, "a": 0, "v": 0}
            items = [(s, c) for s in range(nseq) for c in range(self.NC)]
            hT = f["hT"]

            def new_pj():
                j = cnt["pj"] % NPJ
                cnt["pj"] += 1
                return j

            def new_stg():
                si = cnt["stg"] % NSTG
                cnt["stg"] += 1
                return si

            def projT(j, col0, b):
                for kc in range(8):
                    k.op("pe", lambda e, kc=kc: e.matmul(pj[:, j, :], lhsT=W[:, kc, col0:col0 + 128],
                                                          rhs=hT[:, b, kc, :], start=(kc == 0), stop=(kc == 7)),
                         reads=[t_W, f["t_hT"][b]], writes=[t_pj[j]], signal=(kc == 7), partial=(kc > 0))

            def store(si, dram_ap):
                k.op("sp", lambda e: e.dma_start(out=dram_ap, in_=stg[:, si, :]), reads=[t_stg[si]], dma=True)

            def gate_task(s, c, b, fcg):
                stt_ = {}

                def s0():
                    stt_["j"] = new_pj()
                    projT(stt_["j"], 2304 + fcg * 128, b)

                def s1():
                    j, si = stt_["j"], new_stg()
                    k.op("act", lambda e: e.activation(out=stg[:, si, :], in_=pj[:, j, :], func=AF.Silu),
                         reads=[t_pj[j]], writes=[t_stg[si]])
                    store(si, self.gateT[s, fcg * 128:(fcg + 1) * 128, c * 512:(c + 1) * 512])
                return [s0, s1]

            def fb_task(idx2, fc):
                s2 = items[idx2][0]
                b2 = idx2 % 2
                stt_ = {}

                def s0():
                    pb = f["n"] % 2
                    f["n"] += 1
                    stt_["pb"] = pb
                    for i in range(4):
                        k.op("pe", lambda e, i=i: e.transpose(
                            out=f["ptr"][:, pb, i * 128:(i + 1) * 128], in_=f["xn"][:, i, fc * 128:(fc + 1) * 128], identity=ident),
                            reads=[f["t_xn"][i]], writes=[f["t_ptr"][pb]], signal=(i == 3), partial=(i > 0))

                def s1():
                    pb = stt_["pb"]
                    k.op("act", lambda e: e.activation(
                        out=hT[:, b2, fc, :], in_=f["ptr"][:, pb, 0:512], func=AF.Identity,
                        scale=self.SC[:, l, s2, fc:fc + 1], bias=self.SH[:, l, s2, fc:fc + 1]),
                        reads=[f["t_ptr"][pb]], writes=[f["t_hT"][b2]], partial=(fc > 0))
                return [s0, s1]

            def rope_stages(stt_, b, R, ctab, stab, dram_ap):
                def rot():
                    a = stt_["a"]
                    k.op("pe", lambda e: e.matmul(px[:, 1, :], lhsT=R, rhs=q16[:, a, :], start=True, stop=True),
                         reads=[t_q16[a]], writes=[t_px[1]])
                    k.op("pool", lambda e: e.tensor_tensor(out=t1[:, a, :], in0=q16[:, a, :], in1=rp[:, b, ctab, :], op=ALU.mult),
                         reads=[t_q16[a], t_rp[b]], writes=[t_t1[a]])

                def comb():
                    a = stt_["a"]
                    k.op("dve", lambda e: e.tensor_tensor(out=t2[:, a, :], in0=px[:, 1, :], in1=rp[:, b, stab, :], op=ALU.mult),
                         reads=[t_px[1], t_rp[b]], writes=[t_t2[a]])
                    si = new_stg()
                    k.op("pool", lambda e: e.tensor_tensor(out=stg[:, si, :], in0=t1[:, a, :], in1=t2[:, a, :], op=ALU.add),
                         reads=[t_t1[a], t_t2[a]], writes=[t_stg[si]])
                    store(si, dram_ap)
                return rot, comb

            def a_task(s, c, b, tq):
                stt_ = {}

                def s0():
                    stt_["j"] = new_pj()
                    projT(stt_["j"], tq * 128, b)

                def s1():
                    j = stt_["j"]
                    a = cnt["a"] % NR
                    cnt["a"] += 1
                    stt_["a"] = a
                    k.op("act", lambda e: e.activation(out=q16[:, a, :], in_=pj[:, j, :], func=AF.Copy),
                         reads=[t_pj[j]], writes=[t_q16[a]])
                rot, comb = rope_stages(stt_, b, RA, 0, 1, self.qkaT[s, tq * 128:(tq + 1) * 128, c * 512:(c + 1) * 512])
                return [s0, s1, rot, comb]

            def b_task(s, c, b, tq):
                stt_ = {}
                gi = 0 if tq < 4 else 1

                def s0():
                    stt_["j"] = new_pj()
                    projT(stt_["j"], 1536 + tq * 128, b)

                def s1():
                    j = stt_["j"]
                    a = cnt["a"] % NR
                    cnt["a"] += 1
                    stt_["a"] = a
                    k.op("act", lambda e: e.activation(out=sq[:, a, :], in_=pj[:, j, :], func=AF.Square),
                         reads=[t_pj[j]], writes=[t_sq[a]])
                    k.op("pe", lambda e: e.matmul(px[:, 0, :], lhsT=BD, rhs=sq[:, a, :], start=True, stop=True),
                         reads=[t_sq[a]], writes=[t_px[0]])

                def s2():
                    j, a = stt_["j"], stt_["a"]
                    k.op("act", lambda e: e.activation(out=sdv[:, a, :], in_=px[:, 0, :], func=AF.Sqrt,
                                                       scale=1.0 / HD, bias=self.EPS[:, 0:1]),
                         reads=[t_px[0]], writes=[t_sdv[a]])
                    k.op("dve", lambda e: e.reciprocal(out=rsv[:, a, :], in_=sdv[:, a, :]),
                         reads=[t_sdv[a]], writes=[t_rsv[a]])
                    k.op("dve", lambda e: e.scalar_tensor_tensor(
                        out=q16[:, a, :], in0=pj[:, j, :], scalar=self.QKG[:, ai, gi:gi + 1], in1=rsv[:, a, :],
                        op0=ALU.mult, op1=ALU.mult),
                        reads=[t_pj[j], t_rsv[a]], writes=[t_q16[a]])
                rot, comb = rope_stages(stt_, b, RB, 2, 3, self.qkbT[s, tq * 128:(tq + 1) * 128, c * 512:(c + 1) * 512])
                return [s0, s1, s2, rot, comb]

            def v_task(s, c, b, i):
                stt_ = {}

                def s0():
                    j, j2 = new_pj(), new_pj()
                    stt_["j"], stt_["j2"] = j, j2
                    for kc in range(8):
                        k.op("pe", lambda e, kc=kc: e.matmul(
                            pj[:, j, :], lhsT=hT[:, b, kc, i * 128:(i + 1) * 128], rhs=W[:, kc, 1024:1536],
                            start=(kc == 0), stop=(kc == 7)),
                            reads=[t_W, f["t_hT"][b]], writes=[t_pj[j]], signal=(kc == 7), partial=(kc > 0))
                    for kc in range(8):
                        k.op("pe", lambda e, kc=kc: e.matmul(
                            pj[:, j2, 0:128], lhsT=hT[:, b, kc, i * 128:(i + 1) * 128], rhs=W[:, kc, 2176:2304],
                            start=(kc == 0), stop=(kc == 7)),
                            reads=[t_W, f["t_hT"][b]], writes=[t_pj[j2]], signal=(kc == 7), partial=(kc > 0))

                def s1():
                    j, j2 = stt_["j"], stt_["j2"]
                    vbuf = cnt["v"] % 2
                    cnt["v"] += 1
                    k.op("dve", lambda e: e.tensor_copy(
                        out=vst[:, vbuf, :, 0:3:2, :], in_=pj[:, j, :].rearrange("p (h t d) -> p h t d", h=4, t=2)),
                        reads=[t_pj[j]], writes=[t_vst[vbuf]])
                    k.op("dve", lambda e: e.tensor_copy(
                        out=vbst[:, vbuf, :, 1, :], in_=pj[:, j2, 0:128].rearrange("p (h d) -> p h d", h=2)),
                        reads=[t_pj[j2]], writes=[t_vbst[vbuf]])
                    tok0 = c * 512 + i * 128
                    k.op("sp", lambda e: e.dma_start(
                        out=self.va[s, tok0:tok0 + 128, :, :].rearrange("t h c -> t (h c)"),
                        in_=vst[:, vbuf, :, :, :].rearrange("p h a d -> p (h a d)")),
                        reads=[t_vst[vbuf]], dma=True)
                    k.op("sp", lambda e: e.dma_start(
                        out=self.vb[s, tok0:tok0 + 128, :, :].rearrange("t h c -> t (h c)"),
                        in_=vbst[:, vbuf, :, :, :].rearrange("p h a d -> p (h a d)")),
                        reads=[t_vbst[vbuf]], dma=True)
                return [s0, s1]

            def prefetch_r(idx):
                c = items[idx][1]
                b = idx % 2
                k.op("sp", lambda e: e.dma_start(out=rp[:, b, :, :], in_=self.rope[:, :, c * 512:(c + 1) * 512]),
                     writes=[t_rp[b]], dma=True)

            def head_task(idx):
                def s0():
                    if idx + 2 < len(items):
                        self.load_x(f, xin, items[idx + 2][0], items[idx + 2][1], idx + 2)
                    if idx + 1 < len(items):
                        prefetch_r(idx + 1)
                        self.front_a(f, idx + 1)
                return [s0]

            self.load_x(f, xin, items[0][0], items[0][1], 0)
            prefetch_r(0)
            if len(items) > 1:
                self.load_x(f, xin, items[1][0], items[1][1], 1)
            self.front_a(f, 0)
            self.front_b(f, l, items[0][0], 0)
            tasks = []
            for idx, (s, c) in enumerate(items):
                b = idx % 2
                tasks.append(head_task(idx))
                nxt = idx + 1 < len(items)
                fbq = [fb_task(idx + 1, fc) for fc in range(8)] if nxt else []
                for fcg in range(8):
                    tasks.append(gate_task(s, c, b, fcg))
                    if fcg >= 4 and fbq:
                        tasks.append(fbq.pop(0))
                        tasks.append(fbq.pop(0))
                for tq in range(8):
                    tasks.append(a_task(s, c, b, tq))
                for tq in range(5):
                    tasks.append(b_task(s, c, b, tq))
                for i in range(4):
                    tasks.append(v_task(s, c, b, i))
            self.run_skewed(tasks)

    def alloc_fin(self, st):
        g = {}
        g["R"] = self.sb(st, "fin_R", [128, 2, 512], F32)
        g["RS"] = self.sb(st, "fin_RS", [128, 2, 512], F32)
        g["tg"] = self.sb(st, "fin_tg", [128, 2, 512], F32)
        g["yst"] = self.sb(st, "fin_y", [128, 2, 512], BF16)
        for nm in ("R", "RS", "tg", "yst"):
            g["t_" + nm] = [T(nm + "0"), T(nm + "1")]
        g["n"] = 0
        return g

    def finalize(self, g, srcA, srcB, t_srcs, Gap, t_G, dram_ap):
        k = self.k
        b = g["n"] % 2
        g["n"] += 1
        R, RS, tg, yst = g["R"], g["RS"], g["tg"], g["yst"]
        k.op("dve", lambda e: e.reciprocal(out=R[0:64, b, :], in_=srcB[0:64, :]), reads=t_srcs, writes=[g["t_R"][b]])
        k.op("dve", lambda e: e.reciprocal(out=R[64:128, b, :], in_=srcA[64:128, :]), reads=t_srcs,
             writes=[g["t_R"][b]], partial=True)
        k.op("sp", lambda e: e.dma_start(out=RS[0:64, b, :], in_=R[64:128, b, :]), reads=[g["t_R"][b]],
             writes=[g["t_RS"][b]], dma=True)
        k.op("sp", lambda e: e.dma_start(out=RS[64:128, b, :], in_=R[0:64, b, :]), reads=[g["t_R"][b]],
             writes=[g["t_RS"][b]], dma=True, partial=True)
        k.op("dve", lambda e: e.tensor_tensor(out=tg[0:64, b, :], in0=srcA[0:64, :], in1=Gap[0:64, :], op=ALU.mult),
             reads=t_srcs + [t_G], writes=[g["t_tg"][b]])
        k.op("dve", lambda e: e.tensor_tensor(out=tg[64:128, b, :], in0=srcB[64:128, :], in1=Gap[64:128, :], op=ALU.mult),
             reads=t_srcs + [t_G], writes=[g["t_tg"][b]], partial=True)
        k.op("pool", lambda e: e.tensor_tensor(out=yst[:, b, :], in0=tg[:, b, :], in1=RS[:, b, :], op=ALU.mult),
             reads=[g["t_tg"][b], g["t_RS"][b]], writes=[g["t_yst"][b]])
        k.op("sp", lambda e: e.dma_start(out=dram_ap, in_=yst[:, b, :]), reads=[g["t_yst"][b]], dma=True)

    def phase_A(self, l):
        k, nseq, S = self.k, self.nseq, self.S
        with ExitStack() as st:
            QT = self.sb(st, "A_QT", [128, 2, S], BF16)
            KT = self.sb(st, "A_KT", [128, 2, S], BF16)
            G = self.sb(st, "A_G", [128, S], BF16)
            t_QT = [T("QT0"), T("QT1")]
            t_KT = [T("KT0"), T("KT1")]
            t_G = T("G")
            NVB = 3
            VD = self.sb(st, "A_VD", [128, NVB, self.NT, 192], BF16)
            t_VD = [T("VD%d" % i) for i in range(NVB)]
            OA = self.sb(st, "A_OA", [128, 2, 2, S], F32)
            t_OA = [[T("OA00"), T("OA01")], [T("OA10"), T("OA11")]]
            DT = self.sb(st, "A_DT", [128, S // 64], F32)
            RT = self.sb(st, "A_RT", [128, S // 64], F32)
            RROW = self.sb(st, "A_RROW", [2, S], F32)
            t_DT, t_RT, t_RROW = T("DT"), T("RT"), T("RROW")
            tg = self.sb(st, "A_tg", [128, 2, 512], F32)
            yst = self.sb(st, "A_yst", [128, 2, 512], BF16)
            t_tg = [T("tg0"), T("tg1")]
            t_yst = [T("yst0"), T("yst1")]
            pR = self.ps(st, "A_pR", [128, 512], F32)
            t_pR = T("pR")
            pending = []
            fin_n = [0]
            NPS = 5
            pS = self.ps(st, "A_pS", [128, NPS, 512], F32)
            t_pS = [T("pS%d" % i) for i in range(NPS)]
            pO = self.ps(st, "A_pO", [128, 2, 512], F32)
            t_pO = [T("pO0"), T("pO1")]
            NPT = 6
            pT = self.sb(st, "A_pT", [128, NPT, 512], BF16)
            pTm = self.sb(st, "A_pTm", [128, NPT, 512], BF16)
            t_pT = [T("pT%d" % i) for i in range(NPT)]
            t_pTm = [T("pTm%d" % i) for i in range(NPT)]
            MASK = self.CP[:, CP_MASK:CP_MASK + 512]
            LA = 3
            for i in range(NPS):
                k.op("dve", lambda e, i=i: e.memset(pS[:, i, :], 0.0), writes=[t_pS[i]])
            items = [(s, hp) for s in range(nseq) for hp in range(4)]

            def load_qk(idx):
                s, hp = items[idx]
                b = idx % 2
                k.op("sp", lambda e: e.dma_start(out=QT[:, b, :], in_=self.qkaT[s, hp * 128:(hp + 1) * 128, :]),
                     writes=[t_QT[b]], dma=True)
                k.op("sp", lambda e: e.dma_start(out=KT[:, b, :], in_=self.qkaT[s, 512 + hp * 128:512 + (hp + 1) * 128, :]),
                     writes=[t_KT[b]], dma=True)

            vloads = [(idx, d) for idx in range(len(items)) for d in PATTERNS]

            def load_v(vi):
                idx, d = vloads[vi]
                s, hp = items[idx]
                vb_ = vi % NVB
                nt = (S // d) // 128
                vv = self.va[s, :, hp, :].rearrange("(m p r) c -> r p m c", p=128, r=d)
                for r in range(d):
                    k.op("sp", lambda e, r=r: e.dma_start(out=VD[:, vb_, r * nt:(r + 1) * nt, :], in_=vv[r]),
                         writes=[t_VD[vb_]], dma=True, partial=(r > 0))

            groups = []
            vi = 0
            for idx, (s, hp) in enumerate(items):
                b = idx % 2
                first_of_item = True
                for d in PATTERNS:
                    vb_ = vi % NVB
                    first_of_pat = True
                    L = S // d
                    nt = L // 128
                    for hh in range(2):
                        qtiles = [(r, m) for r in range(d) for m in range(nt + 1)]
                        for g0 in range(0, len(qtiles), 2):
                            gd = dict(idx=idx, s=s, hp=hp, b=b, d=d, hh=hh, vb=vb_, nt=nt, L=L,
                                      tiles=qtiles[g0:g0 + 2], pre=[], last=False, n=len(groups))
                            if first_of_pat and vi + 2 < len(vloads):
                                gd["pre"].append(("v", vi + 2))
                            first_of_item = False
                            first_of_pat = False
                            groups.append(gd)
                    vi += 1
                groups[-1]["last"] = True
            oa_first = {}

            def plan(gd):
                mm = []
                segs = []
                for ti, (r, m) in enumerate(gd["tiles"]):
                    jlo = max(128 * m - 64, 0)
                    jhi = min(128 * m + 64, gd["L"])
                    n = jhi - jlo
                    qoff = jlo + 64 - 128 * m
                    if segs and segs[-1][0] == r and segs[-1][1] + segs[-1][2] == jlo:
                        segs[-1][2] += n
                    else:
                        segs.append([r, jlo, n, ti * 128 + qoff])
                    for kind, mk in ((0, m - 1), (1, m)):
                        if 0 <= mk < gd["nt"]:
                            mm.append((ti, kind, mk, jlo, n, qoff, r))
                return mm, segs

            def emit_qk(gd):
                mm, _ = plan(gd)
                b, d, hh = gd["b"], gd["d"], gd["hh"]
                sb_ = gd["n"] % NPS
                rows = slice(hh * 64, (hh + 1) * 64)
                for ii, (ti, kind, mk, jlo, n, qoff, r) in enumerate(mm):
                    c0 = (2 * ti + kind) * 128 + qoff
                    k.op("pe", lambda e, mk=mk, jlo=jlo, n=n, c0=c0, r=r: e.matmul(
                        pS[:, sb_, c0:c0 + n], lhsT=KT[rows, b, sl(128 * mk * d + r, 128, d)],
                        rhs=QT[rows, b, sl(jlo * d + r, n, d)], start=True, stop=True),
                        reads=[t_KT[b], t_QT[b]], writes=[t_pS[sb_]], signal=(ii == len(mm) - 1), partial=(ii > 0))

            def emit_em(gd):
                sb_ = gd["n"] % NPS
                pb = gd["n"] % NPT
                k.op("act", lambda e: e.activation(out=pT[:, pb, :], in_=pS[:, sb_, :], func=AF.Exp, scale=0.125),
                     reads=[t_pS[sb_]], writes=[t_pT[pb]])
                meng = "dve" if gd["n"] % 2 == 0 else "pool"
                k.op(meng, lambda e: e.tensor_tensor(out=pTm[:, pb, :], in0=pT[:, pb, :], in1=MASK, op=ALU.mult),
                     reads=[t_pT[pb]], writes=[t_pTm[pb]])

            def emit_pv(gd):
                for what, arg in gd["pre"]:
                    if what == "qk":
                        load_qk(arg)
                    else:
                        load_v(arg)
                mm, segs = plan(gd)
                b, d, hh, vb_, nt = gd["b"], gd["d"], gd["hh"], gd["vb"], gd["nt"]
                pb = gd["n"] % NPT
                ob = gd["n"] % 2
                vcol = slice(0, 128) if hh == 0 else slice(64, 192)
                for ii, (ti, kind, mk, jlo, n, qoff, r) in enumerate(mm):
                    c0 = (2 * ti + kind) * 128 + qoff
                    first = (ii == 0) or (mm[ii - 1][0] != ti)
                    lastk = (ii == len(mm) - 1) or (mm[ii + 1][0] != ti)
                    k.op("pe", lambda e, mk=mk, n=n, c0=c0, ti=ti, qoff=qoff, first=first, lastk=lastk, r=r: e.matmul(
                        pO[:, ob, ti * 128 + qoff:ti * 128 + qoff + n],
                        lhsT=VD[:, vb_, r * nt + mk, vcol], rhs=pTm[:, pb, c0:c0 + n], start=first, stop=lastk),
                        reads=[t_VD[vb_], t_pTm[pb]], writes=[t_pO[ob]], signal=(ii == len(mm) - 1), partial=(ii > 0))
                ib = gd["idx"] % 2
                for (r, jlo, n, po0) in segs:
                    dst = OA[:, ib, hh, sl(jlo * d + r, n, d)]
                    if d == PATTERNS[0]:
                        fk = (gd["idx"], hh)
                        k.op("dve", lambda e, dst=dst, po0=po0, n=n: e.tensor_copy(out=dst, in_=pO[:, ob, po0:po0 + n]),
                             reads=[t_pO[ob]], writes=[t_OA[ib][hh]], partial=(fk in oa_first))
                        oa_first[fk] = True
                    else:
                        k.op("dve", lambda e, dst=dst, po0=po0, n=n: e.tensor_tensor(
                            out=dst, in0=pO[:, ob, po0:po0 + n], in1=dst, op=ALU.add),
                            reads=[t_pO[ob], t_OA[ib][hh]], writes=[t_OA[ib][hh]], partial=True)
                if gd["last"]:
                    queue_finalize(gd)
                elif pending:
                    fn = pending.pop(0)
                    if fn is not None:
                        fn()

            def queue_finalize(gd):
                s, hp, idx, b = gd["s"], gd["hp"], gd["idx"], gd["b"]
                ib = idx % 2
                tA, tB = t_OA[ib][0], t_OA[ib][1]
                k.op("sp", lambda e: e.dma_start(out=DT[0:64, :], in_=OA[64:65, ib, 0, :].rearrange("p (a b) -> p a b", b=S // 64)),
                     reads=[tA], writes=[t_DT], dma=True)
                k.op("sp", lambda e: e.dma_start(out=DT[64:128, :], in_=OA[0:1, ib, 1, :].rearrange("p (a b) -> p a b", b=S // 64)),
                     reads=[tB], writes=[t_DT], dma=True, partial=True)

                def stage_b():
                    k.op("sp", lambda e: e.dma_start(out=G[:, :], in_=self.gateT[s, hp * 128:(hp + 1) * 128, :]),
                         writes=[t_G], dma=True)
                    k.op("dve", lambda e: e.reciprocal(out=RT[:, :], in_=DT[:, :]), reads=[t_DT], writes=[t_RT])
                    k.op("sp", lambda e: e.dma_start(out=RROW[0:1, :].rearrange("p (a b) -> p a b", b=S // 64), in_=RT[0:64, :]),
                         reads=[t_RT], writes=[t_RROW], dma=True)
                    k.op("sp", lambda e: e.dma_start(out=RROW[1:2, :].rearrange("p (a b) -> p a b", b=S // 64), in_=RT[64:128, :]),
                         reads=[t_RT], writes=[t_RROW], dma=True, partial=True)

                def stage_c(c):
                    def run():
                        cols = slice(c * 512, (c + 1) * 512)
                        fb = fin_n[0] % 2
                        fin_n[0] += 1
                        k.op("pe", lambda e: e.matmul(pR[:, :], lhsT=self.SEL2[:, :], rhs=RROW[:, cols], start=True, stop=True),
                             reads=[t_RROW], writes=[t_pR])
                        k.op("dve", lambda e: e.tensor_tensor(out=tg[0:64, fb, :], in0=OA[0:64, ib, 0, cols], in1=G[0:64, cols],
                                                              op=ALU.mult),
                             reads=[tA, t_G], writes=[t_tg[fb]])
                        k.op("dve", lambda e: e.tensor_tensor(out=tg[64:128, fb, :], in0=OA[64:128, ib, 1, cols],
                                                              in1=G[64:128, cols], op=ALU.mult),
                             reads=[tB, t_G], writes=[t_tg[fb]], partial=True)
                        k.op("dve", lambda e: e.tensor_tensor(out=yst[:, fb, :], in0=tg[:, fb, :], in1=pR[:, :], op=ALU.mult),
                             reads=[t_tg[fb], t_pR], writes=[t_yst[fb]])
                        k.op("sp", lambda e: e.dma_start(out=self.yT[s, hp * 128:(hp + 1) * 128, cols], in_=yst[:, fb, :]),
                             reads=[t_yst[fb]], dma=True)
                    return run
                pending.append(None)
                pending.append(stage_b)
                pending.append(None)
                for c in range(self.NC):
                    pending.append(stage_c(c))
                if idx + 2 < len(items):
                    pending.append(lambda: load_qk(idx + 2))

            load_qk(0)
            if len(items) > 1:
                load_qk(1)
            load_v(0)
            load_v(1)
            for i in range(-LA, len(groups)):
                if 0 <= i + LA < len(groups):
                    emit_qk(groups[i + LA])
                if 0 <= i + 1 < len(groups):
                    emit_em(groups[i + 1])
                if 0 <= i:
                    emit_pv(groups[i])
            while pending:
                fn = pending.pop(0)
                if fn is not None:
                    fn()

    def phase_B(self, l):
        k, nseq, S = self.k, self.nseq, self.S
        with ExitStack() as st:
            g = self.alloc_fin(st)
            QT = self.sb(st, "B_QT", [128, 2, S], BF16)
            KT = self.sb(st, "B_KT", [128, 2, S], BF16)
            G = self.sb(st, "B_G", [128, 2, S], BF16)
            V1 = self.sb(st, "B_V1", [128, 2, self.NT, 192], BF16)
            t_QT = [T("QT0"), T("QT1")]
            t_KT = [T("KT0"), T("KT1")]
            t_G = [T("G0"), T("G1")]
            t_V1 = [T("V10"), T("V11")]
            pS = self.ps(st, "B_pS", [128, 2, 1024], F32)
            t_pS = [T("pS0"), T("pS1")]
            pO = self.ps(st, "B_pO", [128, 2, 2, 512], F32)
            t_pO = [T("pO0"), T("pO1")]
            NPT = 3
            pT = self.sb(st, "B_pT", [128, NPT, 1024], BF16)
            t_pT = [T("pT%d" % i) for i in range(NPT)]
            items = [(s, hp) for s in range(nseq) for hp in range(4)]

            def load(idx):
                s, hp = items[idx]
                b = idx % 2
                kv = hp // 2
                k.op("sp", lambda e: e.dma_start(out=QT[:, b, :], in_=self.qkbT[s, hp * 128:(hp + 1) * 128, :]),
                     writes=[t_QT[b]], dma=True)
                k.op("sp", lambda e: e.dma_start(out=KT[0:64, b, :], in_=self.qkbT[s, 512 + kv * 64:512 + (kv + 1) * 64, :]),
                     writes=[t_KT[b]], dma=True)
                k.op("sp", lambda e: e.dma_start(out=KT[64:128, b, :], in_=self.qkbT[s, 512 + kv * 64:512 + (kv + 1) * 64, :]),
                     writes=[t_KT[b]], dma=True, partial=True)
                k.op("sp", lambda e: e.dma_start(out=G[:, b, :], in_=self.gateT[s, 512 + hp * 128:512 + (hp + 1) * 128, :]),
                     writes=[t_G[b]], dma=True)
                k.op("sp", lambda e: e.dma_start(out=V1[:, b, :, :],
                                                 in_=self.vb[s, :, kv, :].rearrange("(m p) c -> p m c", p=128)),
                     writes=[t_V1[b]], dma=True)

            steps = []
            for idx, (s, hp) in enumerate(items):
                for qc in range(self.NC):
                    for kt in range(self.NT):
                        steps.append(dict(idx=idx, s=s, hp=hp, b=idx % 2, qc=qc, kt=kt, n=len(steps),
                                          ob=(idx * self.NC + qc) % 2))

            def emit_qk(sd):
                b, qc, kt = sd["b"], sd["qc"], sd["kt"]
                sb_ = sd["n"] % 2
                for hh in range(2):
                    rows = slice(hh * 64, (hh + 1) * 64)
                    k.op("pe", lambda e, hh=hh, rows=rows: e.matmul(
                        pS[:, sb_, hh * 512:(hh + 1) * 512], lhsT=KT[rows, b, kt * 128:(kt + 1) * 128],
                        rhs=QT[rows, b, qc * 512:(qc + 1) * 512], start=True, stop=True),
                        reads=[t_KT[b], t_QT[b]], writes=[t_pS[sb_]], signal=(hh == 1), partial=(hh == 1))

            def emit_exp(sd):
                if sd["qc"] == 0 and sd["kt"] == 0 and sd["idx"] + 1 < len(items):
                    load(sd["idx"] + 1)
                sb_ = sd["n"] % 2
                pb = sd["n"] % NPT
                k.op("act", lambda e: e.activation(out=pT[:, pb, :], in_=pS[:, sb_, :], func=AF.Exp, scale=0.125),
                     reads=[t_pS[sb_]], writes=[t_pT[pb]])

            def emit_pv(sd):
                b, qc, kt, ob = sd["b"], sd["qc"], sd["kt"], sd["ob"]
                pb = sd["n"] % NPT
                for hh in range(2):
                    vcol = slice(64, 192) if hh == 0 else slice(0, 128)
                    k.op("pe", lambda e, hh=hh, vcol=vcol: e.matmul(
                        pO[:, ob, hh, :], lhsT=V1[:, b, kt, vcol], rhs=pT[:, pb, hh * 512:(hh + 1) * 512],
                        start=(kt == 0), stop=(kt == self.NT - 1)),
                        reads=[t_V1[b], t_pT[pb]], writes=[t_pO[ob]], signal=(hh == 1), partial=not (kt == 0 and hh == 0))
                if kt == self.NT - 1:
                    s, hp = sd["s"], sd["hp"]
                    qcols = slice(qc * 512, (qc + 1) * 512)
                    self.finalize(g, pO[:, ob, 0, :], pO[:, ob, 1, :], [t_pO[ob]], G[:, b, qcols], t_G[b],
                                  self.yT[s, 512 + hp * 128:512 + (hp + 1) * 128, qcols])

            load(0)
            emit_qk(steps[0])
            if len(steps) > 1:
                emit_qk(steps[1])
            for n in range(len(steps)):
                emit_exp(steps[n])
                if n + 2 < len(steps):
                    emit_qk(steps[n + 2])
                emit_pv(steps[n])

    def phase_U(self, l, xin):
        k, nseq = self.k, self.nseq
        pi = l // 2
        with ExitStack() as st:
            f = self.alloc_front(st)
            W = self.sb(st, "U_W", [128, 8, 2048], BF16)
            t_W = T("W")
            wv = self.pool_w_in[pi].rearrange("(kc p) n -> p kc n", p=128)
            for kc in range(8):
                self.load_w(W[:, kc, :], wv[:, kc, :], t_W)
            pj = self.ps(st, "U_pj", [128, 4, 512], F32)
            t_pj = [T("pj%d" % i) for i in range(4)]
            NSTG = 4
            stg = self.sb(st, "U_stg", [128, NSTG, 512], BF16)
            t_stg = [T("stg%d" % i) for i in range(NSTG)]
            ust = self.sb(st, "U_ust", [128, 2, D], BF16)
            t_ust = [T("ust0"), T("ust1")]
            cnt = {"pj": 0, "stg": 0, "u": 0}
            items = [(s, c) for s in range(nseq) for c in range(self.NC)]
            self.load_x(f, xin, items[0][0], items[0][1], 0)
            if len(items) > 1:
                self.load_x(f, xin, items[1][0], items[1][1], 1)
            self.front_a(f, 0)
            self.front_b(f, l, items[0][0], 0)
            for idx, (s, c) in enumerate(items):
                if idx + 2 < len(items):
                    self.load_x(f, xin, items[idx + 2][0], items[idx + 2][1], idx + 2)
                if idx + 1 < len(items):
                    self.front_a(f, idx + 1)
                b = idx % 2
                hT, t_hT = f["hT"], f["t_hT"][b]
                cols = slice(c * 512, (c + 1) * 512)
                for fcg in range(8):
                    j = cnt["pj"] % 4
                    cnt["pj"] += 1
                    for kc in range(8):
                        k.op("pe", lambda e, kc=kc, fcg=fcg, j=j: e.matmul(
                            pj[:, j, :], lhsT=W[:, kc, 1024 + fcg * 128:1024 + (fcg + 1) * 128], rhs=hT[:, b, kc, :],
                            start=(kc == 0), stop=(kc == 7)),
                            reads=[t_W, t_hT], writes=[t_pj[j]], signal=(kc == 7), partial=(kc > 0))
                    si = cnt["stg"] % NSTG
                    cnt["stg"] += 1
                    k.op("act", lambda e, j=j, si=si: e.activation(out=stg[:, si, :], in_=pj[:, j, :], func=AF.Silu),
                         reads=[t_pj[j]], writes=[t_stg[si]])
                    k.op("sp", lambda e, si=si, fcg=fcg: e.dma_start(out=self.gateT[s, fcg * 128:(fcg + 1) * 128, cols],
                                                                   in_=stg[:, si, :]), reads=[t_stg[si]], dma=True)
                if idx + 1 < len(items):
                    self.front_b(f, l, items[idx + 1][0], idx + 1)
                for i in range(4):
                    ub = cnt["u"] % 2
                    cnt["u"] += 1
                    for half in range(2):
                        j = cnt["pj"] % 4
                        cnt["pj"] += 1
                        for kc in range(8):
                            k.op("pe", lambda e, kc=kc, i=i, half=half, j=j: e.matmul(
                                pj[:, j, :], lhsT=hT[:, b, kc, i * 128:(i + 1) * 128], rhs=W[:, kc, half * 512:(half + 1) * 512],
                                start=(kc == 0), stop=(kc == 7)),
                                reads=[t_W, t_hT], writes=[t_pj[j]], signal=(kc == 7), partial=(kc > 0))
                        k.op("dve", lambda e, j=j, ub=ub, half=half: e.tensor_copy(out=ust[:, ub, half * 512:(half + 1) * 512],
                                                                                 in_=pj[:, j, :]),
                             reads=[t_pj[j]], writes=[t_ust[ub]], partial=(half == 1))
                    tok0 = c * 512 + i * 128
                    k.op("sp", lambda e, ub=ub, tok0=tok0: e.dma_start(out=self.u[s, tok0:tok0 + 128, :], in_=ust[:, ub, :]),
                         reads=[t_ust[ub]], dma=True)

    def phase_O(self, l, xin, xout, attn):
        k, nseq = self.k, self.nseq
        wi = l // 2
        with ExitStack() as st:
            Wo = self.sb(st, "O_W", [128, 8, D], BF16)
            t_W = T("Wo")
            wsrc = self.attn_w_out[wi] if attn else self.pool_w_out[wi]
            wv = wsrc.rearrange("(kc p) n -> p kc n", p=128)
            for kc in range(0, 8, 2):
                self.load_w(Wo[:, kc:kc + 2, :], wv[:, kc:kc + 2, :], t_W)
            yTc = self.sb(st, "O_yT", [128, 2, 8, 512], BF16)
            t_yT = [T("yT0"), T("yT1")]
            xt = self.sb(st, "O_xt", [128, 2, 4, D], F32)
            t_xt = [T("xt0"), T("xt1")]
            xo = self.sb(st, "O_xo", [128, 2, 4, D], F32)
            t_xo = [T("xo0"), T("xo1")]
            gpb = self.sb(st, "O_gpb", [128, D], F32)
            t_gpb = T("gpb")
            tt = self.sb(st, "O_tt", [128, 2, D], F32)
            t_tt = [T("tt0"), T("tt1")]
            junk = self.sb(st, "O_junk", [128, D], BF16)
            t_junk = T("junk")
            ss = self.sb(st, "O_ss", [128, 2], F32)
            sd = self.sb(st, "O_sd", [128, 2], F32)
            rs = self.sb(st, "O_rs", [128, 2], F32)
            t_ss = [T("ss0"), T("ss1")]
            t_sd = [T("sd0"), T("sd1")]
            t_rs = [T("rs0"), T("rs1")]
            pm = self.ps(st, "O_pm", [128, 2, 1024], F32)
            t_pm = [T("pm0"), T("pm1")]
            cnt = {"m": 0, "pp": 0, "pg": 0}
            if not attn:
                Wg = self.sb(st, "V_Wg", [128, 4, 2, 256], BF16)
                t_Wg = T("Wg")
                self.load_w(Wg[:, :, :, :], self.pool_w_grp[wi].rearrange("g (cc p) d -> p g cc d", p=128), t_Wg)
                uc = self.sb(st, "V_uc", [128, 2, 4, D], BF16)
                t_uc = [T("uc0"), T("uc1")]
                HPt = self.sb(st, "V_HP", [128, 2, D], BF16)
                HNt = self.sb(st, "V_HN", [128, 2, D], BF16)
                t_HP = [T("HP0"), T("HP1")]
                t_HN = [T("HN0"), T("HN1")]
                for hb in range(2):
                    k.op("pool", lambda e, hb=hb: e.memset(HPt[:, hb, :], 0.0), writes=[t_HP[hb]])
                    k.op("pool", lambda e, hb=hb: e.memset(HNt[:, hb, :], 0.0), writes=[t_HN[hb]])
                gc = self.sb(st, "V_gc", [128, 2, 8, 512], BF16)
                t_gc = [T("gc0"), T("gc1")]
                pooled = self.sb(st, "V_pooled", [128, 8, 512], BF16)
                t_pooled = [T("pooled%d" % i) for i in range(8)]
                pp = self.ps(st, "V_pp", [128, 2, 512], F32)
                t_pp = [T("pp0"), T("pp1")]
                pg = self.ps(st, "V_pg", [128, 2, 512], F32)
                t_pg = [T("pg0"), T("pg1")]
            pgp = self.ps(st, "O_pgp", [128, 2, 512], F32) if attn else None
            t_pgp = T("pgp")
            items = [(s, c) for s in range(nseq) for c in range(self.NC)]

            def prefetch(idx):
                s, c = items[idx]
                b = idx % 2
                xv = xin[s].rearrange("(i p) d -> p i d", p=128)
                k.op("sp", lambda e: e.dma_start(out=xt[:, b, :, :], in_=xv[:, 4 * c:4 * c + 4, :]), writes=[t_xt[b]], dma=True)
                if attn:
                    yv = self.yT[s].rearrange("(kc p) t -> p kc t", p=128)
                    k.op("sp", lambda e: e.dma_start(out=yTc[:, b, :, :], in_=yv[:, :, c * 512:(c + 1) * 512]),
                         writes=[t_yT[b]], dma=True)
                else:
                    uv = self.u[s].rearrange("(i p) d -> p i d", p=128)
                    k.op("sp", lambda e: e.dma_start(out=uc[:, b, :, :], in_=uv[:, 4 * c:4 * c + 4, :]), writes=[t_uc[b]], dma=True)
                    if c > 0:
                        k.op("sp", lambda e: e.dma_start(out=HPt[96:128, b, :], in_=self.u[s, c * 512 - 32:c * 512, :]),
                             writes=[t_HP[b]], dma=True, partial=True)
                    if c < self.NC - 1:
                        k.op("sp", lambda e: e.dma_start(out=HNt[0:32, b, :], in_=self.u[s, (c + 1) * 512:(c + 1) * 512 + 32, :]),
                             writes=[t_HN[b]], dma=True, partial=True)
                    gv = self.gateT[s].rearrange("(kc p) t -> p kc t", p=128)
                    k.op("sp", lambda e: e.dma_start(out=gc[:, b, :, :], in_=gv[:, :, c * 512:(c + 1) * 512]),
                         writes=[t_gc[b]], dma=True)

            prefetch(0)
            for idx, (s, c) in enumerate(items):
                b = idx % 2
                if idx + 1 < len(items):
                    prefetch(idx + 1)
                if c == 0:
                    pgx = pgp if attn else pg
                    t_pgx = [t_pgp, t_pgp] if attn else t_pg
                    for half in range(2):
                        k.op("pe", lambda e, half=half: e.matmul(pgx[:, half, :], lhsT=self.SEL[:, s, :],
                                                                  rhs=self.GP[:, l, half * 512:(half + 1) * 512],
                                                                  start=True, stop=True),
                             writes=[t_pgx[half]])
                        k.op("act", lambda e, half=half: e.activation(out=gpb[:, half * 512:(half + 1) * 512],
                                                                      in_=pgx[:, half, :], func=AF.Copy),
                             reads=[t_pgx[half]], writes=[t_gpb], partial=(half == 1))
                if not attn:
                    for fc in range(8):
                        gI = fc // 2
                        ppb = cnt["pp"] % 2
                        cnt["pp"] += 1
                        ops = []
                        for i in range(4):
                            Tg = 4 * c + i
                            var = 1 if Tg == 0 else (2 if Tg == self.NT - 1 else 0)
                            fcols = slice(fc * 128, (fc + 1) * 128)
                            oc = slice(i * 128, (i + 1) * 128)
                            bc0 = CP_BC + (gI * 3 + var) * 128
                            lst = [(uc[:, b, i, fcols], self.CP[:, bc0:bc0 + 128], [t_uc[b]])]
                            if Tg > 0:
                                bp0 = CP_BP + gI * 128
                                if i > 0:
                                    lst.append((uc[64:128, b, i - 1, fcols], self.CP[64:128, bp0:bp0 + 128], [t_uc[b]]))
                                else:
                                    lst.append((HPt[64:128, b, fcols], self.CP[64:128, bp0:bp0 + 128], [t_HP[b]]))
                            if Tg < self.NT - 1:
                                bn0 = CP_BN + gI * 128
                                if i < 3:
                                    lst.append((uc[0:32, b, i + 1, fcols], self.CP[0:32, bn0:bn0 + 128], [t_uc[b]]))
                                else:
                                    lst.append((HNt[0:32, b, fcols], self.CP[0:32, bn0:bn0 + 128], [t_HN[b]]))
                            for q, (lh, rh, rd) in enumerate(lst):
                                ops.append((oc, lh, rh, rd, q == 0, q == len(lst) - 1))
                        for q, (oc, lh, rh, rd, stt, stp) in enumerate(ops):
                            k.op("pe", lambda e, oc=oc, lh=lh, rh=rh, stt=stt, stp=stp: e.matmul(
                                pp[:, ppb, oc], lhsT=lh, rhs=rh, start=stt, stop=stp),
                                reads=rd, writes=[t_pp[ppb]], signal=(q == len(ops) - 1), partial=(q > 0))
                        k.op("act", lambda e, fc=fc, ppb=ppb: e.activation(out=pooled[:, fc, :], in_=pp[:, ppb, :], func=AF.Copy),
                             reads=[t_pp[ppb]], writes=[t_pooled[fc]])
                    for dc in range(8):
                        gI = dc // 2
                        pgb = cnt["pg"] % 2
                        cnt["pg"] += 1
                        for cc in range(2):
                            k.op("pe", lambda e, dc=dc, gI=gI, cc=cc, pgb=pgb: e.matmul(
                                pg[:, pgb, :], lhsT=Wg[:, gI, cc, (dc % 2) * 128:(dc % 2 + 1) * 128],
                                rhs=pooled[:, 2 * gI + cc, :], start=(cc == 0), stop=(cc == 1)),
                                reads=[t_Wg, t_pooled[2 * gI + cc]], writes=[t_pg[pgb]], signal=(cc == 1), partial=(cc == 1))
                        k.op("dve", lambda e, dc=dc, pgb=pgb: e.scalar_tensor_tensor(
                            out=yTc[:, b, dc, :], in0=pg[:, pgb, :], scalar=self.PSC[:, wi, dc:dc + 1], in1=gc[:, b, dc, :],
                            op0=ALU.mult, op1=ALU.mult),
                            reads=[t_pg[pgb], t_gc[b]], writes=[t_yT[b]], partial=(dc > 0))
                for i in range(4):
                    mb = cnt["m"] % 2
                    cnt["m"] += 1
                    for half in range(2):
                        for kc in range(8):
                            k.op("pe", lambda e, i=i, half=half, kc=kc, mb=mb: e.matmul(
                                pm[:, mb, half * 512:(half + 1) * 512], lhsT=yTc[:, b, kc, i * 128:(i + 1) * 128],
                                rhs=Wo[:, kc, half * 512:(half + 1) * 512], start=(kc == 0), stop=(kc == 7)),
                                reads=[t_W, t_yT[b]], writes=[t_pm[mb]], signal=(kc == 7 and half == 1),
                                partial=not (kc == 0 and half == 0))
                    k.op("act", lambda e, mb=mb: e.activation(out=junk[:, :], in_=pm[:, mb, :], func=AF.Square,
                                                              accum_out=ss[:, mb:mb + 1]),
                         reads=[t_pm[mb]], writes=[t_junk, t_ss[mb]])
                    k.op("act", lambda e, mb=mb: e.activation(out=sd[:, mb:mb + 1], in_=ss[:, mb:mb + 1], func=AF.Sqrt,
                                                              scale=1.0 / D, bias=self.EPS[:, 0:1]),
                         reads=[t_ss[mb]], writes=[t_sd[mb]])
                    k.op("dve", lambda e, mb=mb: e.reciprocal(out=rs[:, mb:mb + 1], in_=sd[:, mb:mb + 1]),
                         reads=[t_sd[mb]], writes=[t_rs[mb]])
                    k.op("dve", lambda e, mb=mb: e.scalar_tensor_tensor(
                        out=tt[:, mb, :], in0=pm[:, mb, :], scalar=rs[:, mb:mb + 1], in1=gpb[:, :],
                        op0=ALU.mult, op1=ALU.mult),
                        reads=[t_pm[mb], t_rs[mb], t_gpb], writes=[t_tt[mb]])
                    k.op("pool", lambda e, i=i, mb=mb: e.tensor_tensor(out=xo[:, b, i, :], in0=tt[:, mb, :], in1=xt[:, b, i, :],
                                                                      op=ALU.add),
                         reads=[t_tt[mb], t_xt[b]], writes=[t_xo[b]], partial=(i > 0))
                ov = xout[s].rearrange("(i p) d -> p i d", p=128)
                k.op("sp", lambda e: e.dma_start(out=ov[:, 4 * c:4 * c + 4, :], in_=xo[:, b, :, :]), reads=[t_xo[b]], dma=True)


_PROG_CACHE = {}


def _get_prog(nseq, S, layers=(0, 1, 2, 3)):
    key = (nseq, S, tuple(layers))
    if key not in _PROG_CACHE:
        _PROG_CACHE[key] = Prog(nseq, S, layers)
    return _PROG_CACHE[key]


def _shared_inputs(nseq, S, ada_w, ada_b, pre_norm, post_norm, attn_w_in, attn_q_norm, attn_k_norm,
                   attn_w_out, pool_w_in, pool_w_grp, pool_scale, pool_w_out):
    f = lambda a: np.ascontiguousarray(np.asarray(a, dtype=np.float32))
    ada_b = f(ada_b)
    m = {}
    m["ada_w"] = f(ada_w)
    m["ada_bT"] = np.ascontiguousarray(ada_b[:, :2048].reshape(4, 16, 128).transpose(2, 0, 1))
    m["ada_bg"] = np.ascontiguousarray(np.broadcast_to(ada_b[None, :, 2048:], (nseq, 4, D)))
    m["pre_T"] = np.ascontiguousarray(f(pre_norm).reshape(4, 8, 128).transpose(2, 0, 1))
    m["post_rep"] = np.ascontiguousarray(np.broadcast_to(f(post_norm)[None], (nseq, 4, D)))
    m["attn_w_in"] = f(attn_w_in)
    qg = np.stack([f(attn_q_norm), f(attn_k_norm)], axis=-1)
    m["qk_g"] = np.ascontiguousarray(np.concatenate([qg, qg], axis=1).transpose(1, 0, 2))
    m["attn_w_out"] = f(attn_w_out)
    m["pool_w_in"] = f(pool_w_in)
    m["pool_w_grp"] = f(pool_w_grp)
    m["pool_scT"] = np.ascontiguousarray(f(pool_scale).reshape(2, 8, 128).transpose(2, 0, 1))
    m["pool_w_out"] = f(pool_w_out)
    m.update(_consts(S, nseq))
    return m


def run_trunk(xs, cs, weights, n_cores, layers=(0, 1, 2, 3)):
    nseq, S = xs.shape[1], xs.shape[2]
    prog = _get_prog(nseq, S, layers)
    shared = _shared_inputs(nseq, S, **weights)
    in_maps = []
    for i in range(n_cores):
        m = dict(shared)
        m["x"] = np.ascontiguousarray(xs[i])
        m["cT"] = np.ascontiguousarray(cs[i].reshape(nseq, 8, 128).transpose(2, 1, 0))
        in_maps.append(m)
    res = run_bass_kernel_spmd(prog.nc, in_maps, core_ids=list(range(n_cores)))
    return np.stack([np.asarray(r["y"]) for r in res.results], axis=0)


def kernel(x_prompt, x_sample, c_prompt, c_sample, ada_w, ada_b, pre_norm, post_norm,
           attn_w_in, attn_q_norm, attn_k_norm, attn_w_out,
           pool_w_in, pool_w_grp, pool_scale, pool_w_out):
    x_prompt = np.asarray(x_prompt, dtype=np.float32)
    x_sample = np.asarray(x_sample, dtype=np.float32)
    c_prompt = np.asarray(c_prompt, dtype=np.float32)
    c_sample = np.asarray(c_sample, dtype=np.float32)
    nb_p, nb_s = x_prompt.shape[0], x_sample.shape[0]
    nreal = nb_p + nb_s
    nslots = N_CORES * NSEQ_FULL
    xs = np.empty((nslots, S_FULL, D), np.float32)
    cs = np.empty((nslots, D), np.float32)
    xs[:nb_p] = x_prompt
    xs[nb_p:nreal] = x_sample
    cs[:nb_p] = c_prompt
    cs[nb_p:nreal] = c_sample
    for j in range(nreal, nslots):
        xs[j] = xs[j - nreal]
        cs[j] = cs[j - nreal]
    weights = dict(ada_w=ada_w, ada_b=ada_b, pre_norm=pre_norm, post_norm=post_norm, attn_w_in=attn_w_in,
                   attn_q_norm=attn_q_norm, attn_k_norm=attn_k_norm, attn_w_out=attn_w_out, pool_w_in=pool_w_in,
                   pool_w_grp=pool_w_grp, pool_scale=pool_scale, pool_w_out=pool_w_out)
    y = run_trunk(xs.reshape(N_CORES, NSEQ_FULL, S_FULL, D), cs.reshape(N_CORES, NSEQ_FULL, D), weights, N_CORES)
    y = y.reshape(nslots, S_FULL, D)
    return (np.ascontiguousarray(y[:nb_p]), np.ascontiguousarray(y[nb_p:nreal]))
```

```python
import math
from contextlib import ExitStack

import numpy as np
import ml_dtypes

import concourse.bass as bass
import concourse.mybir as mybir
from concourse.bass_utils import run_bass_kernel_spmd

F32 = mybir.dt.float32
BF16 = mybir.dt.bfloat16
AF = mybir.ActivationFunctionType
ALU = mybir.AluOpType
NPBF = ml_dtypes.bfloat16

D = 1024
HD = 64
EPS = 1e-6
PATTERNS = (1, 4, 16)
POOL_HALF = (1, 2, 4, 8)
N_CORES = 8
NSEQ_FULL = 3
S_FULL = 4096


def sl(start, count, step=1):
    return slice(start, start + (count - 1) * step + 1, step)


class Ref:
    __slots__ = ("kind", "eng", "sem", "value")

    def __init__(self, kind, eng, sem, value):
        self.kind, self.eng, self.sem, self.value = kind, eng, sem, value


class T:
    __slots__ = ("name", "writers", "readers")

    def __init__(self, name):
        self.name = name
        self.writers = []
        self.readers = {}


class EngS:
    def __init__(self, name, h, sem):
        self.name, self.h, self.sem = name, h, sem
        self.count = 0
        self.pending = []
        self.waited = {}


class KB:
    def __init__(self, nc, stack, n_dma_sems=40):
        self.nc = nc
        self.E = {}
        for name, h in (("pe", nc.tensor), ("act", nc.scalar), ("dve", nc.vector),
                        ("pool", nc.gpsimd), ("sp", nc.sync)):
            sem = stack.enter_context(nc.semaphore("sem_" + name))
            self.E[name] = EngS(name, h, sem)
        self.dsems = [[stack.enter_context(nc.semaphore("dsem%d" % i)), 0] for i in range(n_dma_sems)]
        self.dnext = 0
        self.swsems = [[stack.enter_context(nc.semaphore("swsem%d" % i)), 0] for i in range(8)]
        self.swnext = 0
        self.nops = 0

    def _wait(self, E, sem, val):
        key = id(sem)
        if E.waited.get(key, 0) >= val:
            return
        E.h.wait_ge(sem, val)
        E.waited[key] = val

    def op(self, eng, fn, reads=(), writes=(), signal=True, partial=False, dma=False):
        E = self.E[eng]
        self.nops += 1
        for t in reads:
            for r in t.writers:
                self._dep(E, eng, r, "raw")
        for t in writes:
            for r in t.readers.values():
                self._dep(E, eng, r, "war")
            for r in t.writers:
                self._dep(E, eng, r, "waw")
        if dma:
            if eng == "pool":
                slot = self.swsems[self.swnext]
                self.swnext = (self.swnext + 1) % len(self.swsems)
            else:
                slot = self.dsems[self.dnext]
                self.dnext = (self.dnext + 1) % len(self.dsems)
            if slot[1] > 0:
                self._wait(E, slot[0], slot[1])
            ins = fn(E.h)
            ins.then_inc(slot[0], 16)
            slot[1] += 16
            ref = Ref("dma", eng, slot[0], slot[1])
        else:
            ins = fn(E.h)
            ref = Ref("eng", eng, E.sem, None)
            if signal:
                E.count += 1
                ins.then_inc(E.sem, 1)
                ref.value = E.count
                for p in E.pending:
                    p.value = E.count
                E.pending = []
            else:
                E.pending.append(ref)
        for t in reads:
            t.readers[eng if not dma else ("dma", id(ref))] = ref
        for t in writes:
            if partial:
                t.writers.append(ref)
            else:
                t.writers = [ref]
                t.readers = {}
        return ref

    def _dep(self, E, eng, r, kind):
        if r.kind == "eng" and r.eng == eng:
            if eng == "pe":
                return
        assert r.value is not None, "dependency on an unsignaled op (engine %s)" % r.eng
        self._wait(E, r.sem, r.value)

    def barrier(self):
        for E in self.E.values():
            assert not E.pending, "pending unsignaled ops on %s at barrier" % E.name
        for E in self.E.values():
            for F in self.E.values():
                if F is not E and F.count > 0:
                    self._wait(E, F.sem, F.count)
            for sem, cnt in self.dsems + self.swsems:
                if cnt > 0:
                    self._wait(E, sem, cnt)


def _rope_tables(S):
    t = np.arange(S, dtype=np.float32)
    invA = (np.float32(500000.0) ** (-np.arange(0, 16, 2, dtype=np.float32) / np.float32(16))).astype(np.float32)
    angA = (t[:, None] * invA[None, :]).astype(np.float32)
    CA = np.ones((128, S), np.float32)
    SA = np.zeros((128, S), np.float32)
    for p in range(128):
        dd = p % 64
        if dd < 16:
            CA[p] = np.cos(angA[:, dd % 8])
            SA[p] = np.sin(angA[:, dd % 8])
    invB = (np.float32(10000.0) ** (-np.arange(0, 32, 2, dtype=np.float32) / np.float32(32))).astype(np.float32)
    row = (np.arange(S) // 64).astype(np.float32)
    col = (np.arange(S) % 64).astype(np.float32)
    CB = np.zeros((128, S), np.float32)
    SB = np.zeros((128, S), np.float32)
    for p in range(128):
        dd = p % 64
        pos = row if dd < 32 else col
        f = (dd % 32) % 16
        ang = (pos * invB[f]).astype(np.float32)
        CB[p] = np.cos(ang)
        SB[p] = np.sin(ang)
    return CA, SA, CB, SB


def _rot_mats():
    RA = np.zeros((128, 128), np.float32)
    RB = np.zeros((128, 128), np.float32)
    for m in range(128):
        dd = m % 64
        if dd < 8:
            RA[m + 8, m] = -1.0
        elif dd < 16:
            RA[m - 8, m] = 1.0
        e = dd % 32
        if e < 16:
            RB[m + 16, m] = -1.0
        else:
            RB[m - 16, m] = 1.0
    return RA, RB


def _band_mats():
    BC = np.zeros((128, 4, 3, 128), np.float32)
    BP = np.zeros((128, 4, 128), np.float32)
    BN = np.zeros((128, 4, 128), np.float32)
    for g, h in enumerate(POOL_HALF):
        for var in range(3):
            for tout in range(128):
                lo, hi = tout - h, tout + h
                clo = max(lo, 0) if var == 1 else lo
                chi = min(hi, 128) if var == 2 else hi
                cnt = chi - clo
                for tin in range(max(clo, 0), min(chi, 128)):
                    BC[tin, g, var, tout] += 1.0 / cnt
                BC[tout, g, var, tout] -= 1.0
        for tout in range(128):
            for i in range(32):
                if -32 + i >= tout - h:
                    BP[96 + i, g, tout] = 1.0 / (2 * h)
                if 128 + i < tout + h:
                    BN[i, g, tout] = 1.0 / (2 * h)
    return BC, BP, BN


def _consts(S, nseq):
    c = {}
    CA, SA, CB, SB = _rope_tables(S)
    c["rope"] = np.stack([CA, SA, CB, SB], axis=1).astype(np.float32)
    RA, RB = _rot_mats()
    ident = np.eye(128, dtype=np.float32)
    bd = np.zeros((128, 128), np.float32)
    bd[:64, :64] = 1.0
    bd[64:, 64:] = 1.0
    kj = np.arange(128)[:, None]
    qi = np.arange(128)[None, :]
    mlo = (kj >= qi).astype(np.float32)
    mhi = (kj <= qi).astype(np.float32)
    mask = np.concatenate([mlo, mhi] * 4, axis=1)
    BC, BP, BN = _band_mats()
    pack = np.concatenate([ident, RA, RB, bd, mask, BC.reshape(128, -1), BP.reshape(128, -1),
                           BN.reshape(128, -1)], axis=1)
    c["cpack"] = pack.astype(NPBF)
    sel = np.zeros((nseq, nseq, 128), np.float32)
    for s in range(nseq):
        sel[s, s, :] = 1.0
    c["sel"] = sel
    sel2 = np.zeros((2, 128), np.float32)
    sel2[0, :64] = 1.0
    sel2[1, 64:] = 1.0
    c["sel2"] = sel2
    return c


CP_IDENT, CP_RA, CP_RB, CP_BD, CP_MASK = 0, 128, 256, 384, 512
CP_BC = 512 + 1024
CP_BP = CP_BC + 4 * 3 * 128
CP_BN = CP_BP + 4 * 128
CP_END = CP_BN + 4 * 128


class Prog:
    def __init__(self, nseq, S, layers=(0, 1, 2, 3)):
        self.nseq, self.S, self.layers = nseq, S, tuple(layers)
        self.NT = S // 128
        self.NC = S // 512
        nc = bass.Bass("TRN2", target_bir_lowering=False)
        self.nc = nc
        dt = nc.dram_tensor
        self.x_in = dt("x", [nseq, S, D], F32, kind="ExternalInput")
        self.cT = dt("cT", [128, 8, nseq], F32, kind="ExternalInput")
        self.ada_w = dt("ada_w", [4, D, 3 * D], F32, kind="ExternalInput")
        self.ada_bT = dt("ada_bT", [128, 4, 16], F32, kind="ExternalInput")
        self.ada_bg = dt("ada_bg", [nseq, 4, D], F32, kind="ExternalInput")
        self.pre_T = dt("pre_T", [128, 4, 8], F32, kind="ExternalInput")
        self.post_rep = dt("post_rep", [nseq, 4, D], F32, kind="ExternalInput")
        self.attn_w_in = dt("attn_w_in", [2, D, 3328], F32, kind="ExternalInput")
        self.qk_g = dt("qk_g", [128, 2, 2], F32, kind="ExternalInput")
        self.attn_w_out = dt("attn_w_out", [2, D, D], F32, kind="ExternalInput")
        self.pool_w_in = dt("pool_w_in", [2, D, 2 * D], F32, kind="ExternalInput")
        self.pool_w_grp = dt("pool_w_grp", [2, 4, 256, 256], F32, kind="ExternalInput")
        self.pool_scT = dt("pool_scT", [128, 2, 8], F32, kind="ExternalInput")
        self.pool_w_out = dt("pool_w_out", [2, D, D], F32, kind="ExternalInput")
        self.rope = dt("rope", [128, 4, S], F32, kind="ExternalInput")
        self.cpack = dt("cpack", [128, CP_END], BF16, kind="ExternalInput")
        self.sel = dt("sel", [nseq, nseq, 128], F32, kind="ExternalInput")
        self.sel2 = dt("sel2", [2, 128], F32, kind="ExternalInput")
        self.y_out = dt("y", [nseq, S, D], F32, kind="ExternalOutput")
        self.xs = [dt("xs0", [nseq, S, D], F32, kind="Internal"), dt("xs1", [nseq, S, D], F32, kind="Internal")]
        self.qkaT = dt("qkaT", [nseq, 1024, S], BF16, kind="Internal")
        self.qkbT = dt("qkbT", [nseq, 640, S], BF16, kind="Internal")
        self.gateT = dt("gateT", [nseq, 1024, S], BF16, kind="Internal")
        self.va = dt("va", [nseq, S, 4, 192], BF16, kind="Internal")
        self.vb = dt("vb", [nseq, S, 2, 192], BF16, kind="Internal")
        self.yT = dt("yT", [nseq, 1024, S], BF16, kind="Internal")
        self.u = dt("u", [nseq, S, D], BF16, kind="Internal")
        self.build()

    def sb(self, st, name, shape, dtype):
        self._uid = getattr(self, "_uid", 0) + 1
        return st.enter_context(self.nc.sbuf_tensor("%s_%d" % (name, self._uid), list(shape), dtype))

    def ps(self, st, name, shape, dtype):
        self._uid = getattr(self, "_uid", 0) + 1
        return st.enter_context(self.nc.psum_tensor("%s_%d" % (name, self._uid), list(shape), dtype))

    def build(self):
        nc = self.nc
        with ExitStack() as st:
            k = KB(nc, st)
            self.k = k
            self.CP = self.sb(st, "cpk", [128, CP_END], BF16)
            self.SC = self.sb(st, "SC", [128, 4, self.nseq, 8], F32)
            self.SH = self.sb(st, "SH", [128, 4, self.nseq, 8], F32)
            self.GP = self.sb(st, "GP", [self.nseq, 4, D], F32)
            self.SEL = self.sb(st, "SEL", [self.nseq, self.nseq, 128], F32)
            self.SEL2 = self.sb(st, "SEL2", [2, 128], F32)
            self.QKG = self.sb(st, "QKG", [128, 2, 2], F32)
            self.PSC = self.sb(st, "PSC", [128, 2, 8], F32)
            self.EPS = self.sb(st, "EPSt", [128, 1], F32)
            tc = T("consts")
            k.op("sp", lambda e: e.dma_start(out=self.CP[:, :], in_=self.cpack[:, :]), writes=[tc], dma=True)
            k.op("sp", lambda e: e.dma_start(out=self.SEL[:, :, :], in_=self.sel[:, :, :]), writes=[tc], partial=True, dma=True)
            k.op("sp", lambda e: e.dma_start(out=self.QKG[:, :, :], in_=self.qk_g[:, :, :]), writes=[tc], partial=True, dma=True)
            k.op("sp", lambda e: e.dma_start(out=self.SEL2[:, :], in_=self.sel2[:, :]), writes=[tc], partial=True, dma=True)
            k.op("sp", lambda e: e.dma_start(out=self.PSC[:, :, :], in_=self.pool_scT[:, :, :]), writes=[tc], partial=True, dma=True)
            k.op("dve", lambda e: e.memset(self.EPS[:, :], EPS), writes=[tc], partial=True)
            self.prologue()
            k.barrier()
            xin = self.x_in
            for li, l in enumerate(self.layers):
                last = li == len(self.layers) - 1
                xout = self.y_out if last else self.xs[li % 2]
                if l % 2 == 0:
                    self.phase_P(l, xin)
                    k.barrier()
                    self.phase_A(l)
                    k.barrier()
                    self.phase_B(l)
                    k.barrier()
                    self.phase_O(l, xin, xout, attn=True)
                    k.barrier()
                else:
                    self.phase_U(l, xin)
                    k.barrier()
                    self.phase_O(l, xin, xout, attn=False)
                    k.barrier()
                xin = xout
            k.barrier()

    def prologue(self):
        k, nseq = self.k, self.nseq
        with ExitStack() as st:
            cT = self.sb(st, "cTs", [128, 8, nseq], F32)
            scT = self.sb(st, "scT", [128, 8, nseq], BF16)
            bT = self.sb(st, "bT", [128, 4, 16], F32)
            preT = self.sb(st, "preT", [128, 4, 8], F32)
            bg = self.sb(st, "bg", [nseq, 4, D], F32)
            pr = self.sb(st, "pr", [nseq, 4, D], F32)
            blk32 = self.sb(st, "blk32", [128, 2, 8, 512], F32)
            blk = self.sb(st, "blk", [128, 2, 8, 512], BF16)
            t_b32 = [T("b32_0"), T("b32_1")]
            modT = self.sb(st, "modT", [128, 16, nseq], F32)
            pm = self.ps(st, "pm", [128, 2, 512], F32)
            pg = self.ps(st, "pg", [128, 2, 512], F32)
            t_in = T("pin")
            k.op("sp", lambda e: e.dma_start(out=cT[:, :, :], in_=self.cT[:, :, :]), writes=[t_in], dma=True)
            for dst, src in ((bT, self.ada_bT), (preT, self.pre_T), (bg, self.ada_bg), (pr, self.post_rep)):
                k.op("sp", lambda e, dst=dst, src=src: e.dma_start(out=dst[:, :, :], in_=src[:, :, :]),
                     writes=[t_in], partial=True, dma=True)
            t_sc = T("scT")
            k.op("act", lambda e: e.activation(out=scT[:, :, :], in_=cT[:, :, :], func=AF.Silu),
                 reads=[t_in], writes=[t_sc])
            t_blk = [T("blk0"), T("blk1")]
            t_pm = [T("pm0"), T("pm1")]
            t_pg = T("pg")
            t_mod = T("modT")
            t_out = T("modout")
            nb_i = 0
            for l in self.layers:
                wv = self.ada_w[l].rearrange("(kc p) n -> p kc n", p=128)
                for nb in range(6):
                    b = nb_i % 2
                    nb_i += 1
                    k.op("sp", lambda e, b=b, nb=nb, wv=wv: e.dma_start(out=blk32[:, b, :, :], in_=wv[:, :, nb * 512:(nb + 1) * 512]),
                         writes=[t_b32[b]], dma=True)
                    k.op("dve", lambda e, b=b: e.tensor_copy(out=blk[:, b, 0:4, :], in_=blk32[:, b, 0:4, :]),
                         reads=[t_b32[b]], writes=[t_blk[b]])
                    k.op("act", lambda e, b=b: e.activation(out=blk[:, b, 4:8, :], in_=blk32[:, b, 4:8, :], func=AF.Copy),
                         reads=[t_b32[b]], writes=[t_blk[b]], partial=True)
                    if nb < 4:
                        pb = nb % 2
                        for j in range(4):
                            for kc in range(8):
                                k.op("pe", lambda e, b=b, j=j, kc=kc, pb=pb: e.matmul(
                                    pm[:, pb, j * nseq:(j + 1) * nseq], lhsT=blk[:, b, kc, j * 128:(j + 1) * 128],
                                    rhs=scT[:, kc, :], start=(kc == 0), stop=(kc == 7)),
                                    reads=[t_blk[b], t_sc], writes=[t_pm[pb]], signal=(kc == 7 and j == 3),
                                    partial=not (j == 0 and kc == 0))
                        k.op("dve", lambda e, nb=nb, pb=pb: e.tensor_copy(
                            out=modT[:, nb * 4:(nb + 1) * 4, :],
                            in_=pm[:, pb, 0:4 * nseq].rearrange("p (j s) -> p j s", s=nseq)),
                            reads=[t_pm[pb]], writes=[t_mod], partial=(nb > 0))
                    else:
                        half = nb - 4
                        for kc in range(8):
                            k.op("pe", lambda e, b=b, kc=kc, half=half: e.matmul(
                                pg[0:nseq, half, :], lhsT=scT[:, kc, :], rhs=blk[:, b, kc, :],
                                start=(kc == 0), stop=(kc == 7)),
                                reads=[t_blk[b], t_sc], writes=[t_pg], signal=(kc == 7),
                                partial=not (half == 0 and kc == 0))
                for s in range(nseq):
                    k.op("dve", lambda e, l=l, s=s: e.tensor_tensor(
                        out=self.SH[:, l, s, :], in0=modT[:, 0:8, s], in1=bT[:, l, 0:8], op=ALU.add),
                        reads=[t_mod, t_in], writes=[t_out], partial=True)
                    k.op("dve", lambda e, l=l, s=s: e.tensor_tensor(
                        out=self.SC[:, l, s, :], in0=modT[:, 8:16, s], in1=bT[:, l, 8:16], op=ALU.add),
                        reads=[t_mod, t_in], writes=[t_out], partial=True)
                    k.op("dve", lambda e, l=l, s=s: e.scalar_tensor_tensor(
                        out=self.SC[:, l, s, :], in0=self.SC[:, l, s, :], scalar=1.0, in1=preT[:, l, :],
                        op0=ALU.add, op1=ALU.mult),
                        reads=[t_out, t_in], writes=[t_out], partial=True)
                k.op("dve", lambda e, l=l: e.tensor_tensor(
                    out=self.GP[:, l, :], in0=pg[0:nseq, :, :].rearrange("p a b -> p (a b)"), in1=bg[:, l, :], op=ALU.add),
                    reads=[t_pg, t_in], writes=[t_out], partial=True)
                k.op("dve", lambda e, l=l: e.tensor_tensor(
                    out=self.GP[:, l, :], in0=self.GP[:, l, :], in1=pr[:, l, :], op=ALU.mult),
                    reads=[t_out, t_in], writes=[t_out], partial=True)
            k.barrier()

    def alloc_front(self, st):
        f = {}
        f["xt"] = self.sb(st, "f_xt", [128, 2, 4, D], F32)
        f["t_xt"] = [T("xt0"), T("xt1")]
        f["junk"] = self.sb(st, "f_junk", [128, D], BF16)
        f["t_junk"] = T("junk")
        f["ss"] = self.sb(st, "f_ss", [128, 2, 4], F32)
        f["sd"] = self.sb(st, "f_sd", [128, 2, 4], F32)
        f["rs"] = self.sb(st, "f_rs", [128, 2, 4], F32)
        f["t_ss"] = [T("ss0"), T("ss1")]
        f["t_sd"] = [T("sd0"), T("sd1")]
        f["t_rs"] = [T("rs0"), T("rs1")]
        f["xn"] = self.sb(st, "f_xn", [128, 4, D], BF16)
        f["t_xn"] = [T("xn%d" % i) for i in range(4)]
        f["hT"] = self.sb(st, "f_hT", [128, 2, 8, 512], BF16)
        f["t_hT"] = [T("hT0"), T("hT1")]
        f["ptr"] = self.ps(st, "f_ptr", [128, 2, 1024], BF16)
        f["t_ptr"] = [T("ptr0"), T("ptr1")]
        f["n"] = 0
        return f

    def load_x(self, f, xin, s, c, idx):
        b = idx % 2
        xv = xin[s].rearrange("(i p) d -> p i d", p=128)
        self.k.op("sp", lambda e: e.dma_start(out=f["xt"][:, b, :, :], in_=xv[:, 4 * c:4 * c + 4, :]),
                  writes=[f["t_xt"][b]], dma=True)

    def front_a(self, f, idx):
        k = self.k
        b = idx % 2
        xt = f["xt"]
        for i in range(4):
            k.op("act", lambda e, i=i: e.activation(out=f["junk"][:, :], in_=xt[:, b, i, :], func=AF.Square,
                                                     accum_out=f["ss"][:, b, i:i + 1]),
                 reads=[f["t_xt"][b]], writes=[f["t_junk"], f["t_ss"][b]], partial=(i > 0))
        k.op("act", lambda e: e.activation(out=f["sd"][:, b, :], in_=f["ss"][:, b, :], func=AF.Sqrt,
                                           scale=1.0 / D, bias=self.EPS[:, 0:1]),
             reads=[f["t_ss"][b]], writes=[f["t_sd"][b]])
        k.op("dve", lambda e: e.reciprocal(out=f["rs"][:, b, :], in_=f["sd"][:, b, :]),
             reads=[f["t_sd"][b]], writes=[f["t_rs"][b]])
        for i in range(4):
            k.op("dve", lambda e, i=i: e.tensor_scalar(out=f["xn"][:, i, :], in0=xt[:, b, i, :],
                                                        scalar1=f["rs"][:, b, i:i + 1], scalar2=None, op0=ALU.mult),
                 reads=[f["t_xt"][b], f["t_rs"][b]], writes=[f["t_xn"][i]])

    def front_b(self, f, l, s, idx):
        k = self.k
        b = idx % 2
        ident = self.CP[:, CP_IDENT:CP_IDENT + 128]
        for fc in range(8):
            pb = f["n"] % 2
            f["n"] += 1
            for i in range(4):
                k.op("pe", lambda e, i=i, fc=fc, pb=pb: e.transpose(
                    out=f["ptr"][:, pb, i * 128:(i + 1) * 128], in_=f["xn"][:, i, fc * 128:(fc + 1) * 128], identity=ident),
                    reads=[f["t_xn"][i]], writes=[f["t_ptr"][pb]], signal=(i == 3), partial=(i > 0))
            k.op("act", lambda e, fc=fc, pb=pb: e.activation(
                out=f["hT"][:, b, fc, :], in_=f["ptr"][:, pb, 0:512], func=AF.Identity,
                scale=self.SC[:, l, s, fc:fc + 1], bias=self.SH[:, l, s, fc:fc + 1]),
                reads=[f["t_ptr"][pb]], writes=[f["t_hT"][b]], partial=(fc > 0))
        return b

    def load_w(self, dst, src_view, t):
        self.k.op("pool", lambda e: e.dma_start(out=dst, in_=src_view), writes=[t], dma=True, partial=True)

    def run_skewed(self, tasks):
        maxage = max(len(t) for t in tasks)
        for n in range(len(tasks) + maxage):
            for age in range(maxage - 1, -1, -1):
                m = n - age
                if 0 <= m < len(tasks) and age < len(tasks[m]) and tasks[m][age] is not None:
                    tasks[m][age]()

    def phase_P(self, l, xin):
        k, nseq = self.k, self.nseq
        ai = l // 2
        with ExitStack() as st:
            f = self.alloc_front(st)
            W = self.sb(st, "P_W", [128, 8, 3328], BF16)
            t_W = T("W")
            wv = self.attn_w_in[ai].rearrange("(kc p) n -> p kc n", p=128)
            for kc in range(8):
                self.load_w(W[:, kc, :], wv[:, kc, :], t_W)
            rp = self.sb(st, "P_rope", [128, 2, 4, 512], F32)
            t_rp = [T("rp0"), T("rp1")]
            NPJ = 4
            pj = self.ps(st, "P_pj", [128, NPJ, 512], F32)
            t_pj = [T("pj%d" % i) for i in range(NPJ)]
            px = self.ps(st, "P_px", [128, 2, 512], F32)
            t_px = [T("px0"), T("px1")]
            NSTG = 6
            stg = self.sb(st, "P_stg", [128, NSTG, 512], BF16)
            t_stg = [T("stg%d" % i) for i in range(NSTG)]
            NR = 3
            q16 = self.sb(st, "P_q16", [128, NR, 512], BF16)
            sq = self.sb(st, "P_sq", [128, NR, 512], BF16)
            sdv = self.sb(st, "P_sdv", [128, NR, 512], F32)
            rsv = self.sb(st, "P_rsv", [128, NR, 512], F32)
            t1 = self.sb(st, "P_t1", [128, NR, 512], F32)
            t2 = self.sb(st, "P_t2", [128, NR, 512], F32)
            t_q16 = [T("q16_%d" % i) for i in range(NR)]
            t_sq = [T("sq%d" % i) for i in range(NR)]
            t_sdv = [T("sdv%d" % i) for i in range(NR)]
            t_rsv = [T("rsv%d" % i) for i in range(NR)]
            t_t1 = [T("t1_%d" % i) for i in range(NR)]
            t_t2 = [T("t2_%d" % i) for i in range(NR)]
            vst = self.sb(st, "P_vst", [128, 2, 4, 3, 64], BF16)
            vbst = self.sb(st, "P_vbst", [128, 2, 2, 3, 64], BF16)
            t_vst = [T("vst0"), T("vst1")]
            t_vbst = [T("vbst0"), T("vbst1")]
            for b in range(2):
                k.op("pool", lambda e, b=b: e.memset(vst[:, b, :, :, :], 1.0), writes=[t_vst[b]])
                k.op("pool", lambda e, b=b: e.memset(vbst[:, b, :, :, :], 1.0), writes=[t_vbst[b]])
            RA = self.CP[:, CP_RA:CP_RA + 128]
            RB = self.CP[:, CP_RB:CP_RB + 128]
            BD = self.CP[:, CP_BD:CP_BD + 128]
            ident = self.CP[:, CP_IDENT:CP_IDENT + 128]
            cnt = {"pj": 0, "stg": 0, "a": 0, "v": 0}
            items = [(s, c) for s in range(nseq) for c in range(self.NC)]
            hT = f["hT"]

            def new_pj():
                j = cnt["pj"] % NPJ
                cnt["pj"] += 1
                return j

            def new_stg():
                si = cnt["stg"] % NSTG
                cnt["stg"] += 1
                return si

            def projT(j, col0, b):
                for kc in range(8):
                    k.op("pe", lambda e, kc=kc: e.matmul(pj[:, j, :], lhsT=W[:, kc, col0:col0 + 128],
                                                          rhs=hT[:, b, kc, :], start=(kc == 0), stop=(kc == 7)),
                         reads=[t_W, f["t_hT"][b]], writes=[t_pj[j]], signal=(kc == 7), partial=(kc > 0))

            def store(si, dram_ap):
                k.op("sp", lambda e: e.dma_start(out=dram_ap, in_=stg[:, si, :]), reads=[t_stg[si]], dma=True)

            def gate_task(s, c, b, fcg):
                stt_ = {}

                def s0():
                    stt_["j"] = new_pj()
                    projT(stt_["j"], 2304 + fcg * 128, b)

                def s1():
                    j, si = stt_["j"], new_stg()
                    k.op("act", lambda e: e.activation(out=stg[:, si, :], in_=pj[:, j, :], func=AF.Silu),
                         reads=[t_pj[j]], writes=[t_stg[si]])
                    store(si, self.gateT[s, fcg * 128:(fcg + 1) * 128, c * 512:(c + 1) * 512])
                return [s0, s1]

            def fb_task(idx2, fc):
                s2 = items[idx2][0]
                b2 = idx2 % 2
                stt_ = {}

                def s0():
                    pb = f["n"] % 2
                    f["n"] += 1
                    stt_["pb"] = pb
                    for i in range(4):
                        k.op("pe", lambda e, i=i: e.transpose(
                            out=f["ptr"][:, pb, i * 128:(i + 1) * 128], in_=f["xn"][:, i, fc * 128:(fc + 1) * 128], identity=ident),
                            reads=[f["t_xn"][i]], writes=[f["t_ptr"][pb]], signal=(i == 3), partial=(i > 0))

                def s1():
                    pb = stt_["pb"]
                    k.op("act", lambda e: e.activation(
                        out=hT[:, b2, fc, :], in_=f["ptr"][:, pb, 0:512], func=AF.Identity,
                        scale=self.SC[:, l, s2, fc:fc + 1], bias=self.SH[:, l, s2, fc:fc + 1]),
                        reads=[f["t_ptr"][pb]], writes=[f["t_hT"][b2]], partial=(fc > 0))
                return [s0, s1]

            def rope_stages(stt_, b, R, ctab, stab, dram_ap):
                def rot():
                    a = stt_["a"]
                    k.op("pe", lambda e: e.matmul(px[:, 1, :], lhsT=R, rhs=q16[:, a, :], start=True, stop=True),
                         reads=[t_q16[a]], writes=[t_px[1]])
                    k.op("pool", lambda e: e.tensor_tensor(out=t1[:, a, :], in0=q16[:, a, :], in1=rp[:, b, ctab, :], op=ALU.mult),
                         reads=[t_q16[a], t_rp[b]], writes=[t_t1[a]])

                def comb():
                    a = stt_["a"]
                    k.op("dve", lambda e: e.tensor_tensor(out=t2[:, a, :], in0=px[:, 1, :], in1=rp[:, b, stab, :], op=ALU.mult),
                         reads=[t_px[1], t_rp[b]], writes=[t_t2[a]])
                    si = new_stg()
                    k.op("pool", lambda e: e.tensor_tensor(out=stg[:, si, :], in0=t1[:, a, :], in1=t2[:, a, :], op=ALU.add),
                         reads=[t_t1[a], t_t2[a]], writes=[t_stg[si]])
                    store(si, dram_ap)
                return rot, comb

            def a_task(s, c, b, tq):
                stt_ = {}

                def s0():
                    stt_["j"] = new_pj()
                    projT(stt_["j"], tq * 128, b)

                def s1():
                    j = stt_["j"]
                    a = cnt["a"] % NR
                    cnt["a"] += 1
                    stt_["a"] = a
                    k.op("act", lambda e: e.activation(out=q16[:, a, :], in_=pj[:, j, :], func=AF.Copy),
                         reads=[t_pj[j]], writes=[t_q16[a]])
                rot, comb = rope_stages(stt_, b, RA, 0, 1, self.qkaT[s, tq * 128:(tq + 1) * 128, c * 512:(c + 1) * 512])
                return [s0, s1, rot, comb]

            def b_task(s, c, b, tq):
                stt_ = {}
                gi = 0 if tq < 4 else 1

                def s0():
                    stt_["j"] = new_pj()
                    projT(stt_["j"], 1536 + tq * 128, b)

                def s1():
                    j = stt_["j"]
                    a = cnt["a"] % NR
                    cnt["a"] += 1
                    stt_["a"] = a
                    k.op("act", lambda e: e.activation(out=sq[:, a, :], in_=pj[:, j, :], func=AF.Square),
                         reads=[t_pj[j]], writes=[t_sq[a]])
                    k.op("pe", lambda e: e.matmul(px[:, 0, :], lhsT=BD, rhs=sq[:, a, :], start=True, stop=True),
                         reads=[t_sq[a]], writes=[t_px[0]])

                def s2():
                    j, a = stt_["j"], stt_["a"]
                    k.op("act", lambda e: e.activation(out=sdv[:, a, :], in_=px[:, 0, :], func=AF.Sqrt,
                                                       scale=1.0 / HD, bias=self.EPS[:, 0:1]),
                         reads=[t_px[0]], writes=[t_sdv[a]])
                    k.op("dve", lambda e: e.reciprocal(out=rsv[:, a, :], in_=sdv[:, a, :]),
                         reads=[t_sdv[a]], writes=[t_rsv[a]])
                    k.op("dve", lambda e: e.scalar_tensor_tensor(
                        out=q16[:, a, :], in0=pj[:, j, :], scalar=self.QKG[:, ai, gi:gi + 1], in1=rsv[:, a, :],
                        op0=ALU.mult, op1=ALU.mult),
                        reads=[t_pj[j], t_rsv[a]], writes=[t_q16[a]])
                rot, comb = rope_stages(stt_, b, RB, 2, 3, self.qkbT[s, tq * 128:(tq + 1) * 128, c * 512:(c + 1) * 512])
                return [s0, s1, s2, rot, comb]

            def v_task(s, c, b, i):
                stt_ = {}

                def s0():
                    j, j2 = new_pj(), new_pj()
                    stt_["j"], stt_["j2"] = j, j2
                    for kc in range(8):
                        k.op("pe", lambda e, kc=kc: e.matmul(
                            pj[:, j, :], lhsT=hT[:, b, kc, i * 128:(i + 1) * 128], rhs=W[:, kc, 1024:1536],
                            start=(kc == 0), stop=(kc == 7)),
                            reads=[t_W, f["t_hT"][b]], writes=[t_pj[j]], signal=(kc == 7), partial=(kc > 0))
                    for kc in range(8):
                        k.op("pe", lambda e, kc=kc: e.matmul(
                            pj[:, j2, 0:128], lhsT=hT[:, b, kc, i * 128:(i + 1) * 128], rhs=W[:, kc, 2176:2304],
                            start=(kc == 0), stop=(kc == 7)),
                            reads=[t_W, f["t_hT"][b]], writes=[t_pj[j2]], signal=(kc == 7), partial=(kc > 0))

                def s1():
                    j, j2 = stt_["j"], stt_["j2"]
                    vbuf = cnt["v"] % 2
                    cnt["v"] += 1
                    k.op("dve", lambda e: e.tensor_copy(
                        out=vst[:, vbuf, :, 0:3:2, :], in_=pj[:, j, :].rearrange("p (h t d) -> p h t d", h=4, t=2)),
                        reads=[t_pj[j]], writes=[t_vst[vbuf]])
                    k.op("dve", lambda e: e.tensor_copy(
                        out=vbst[:, vbuf, :, 1, :], in_=pj[:, j2, 0:128].rearrange("p (h d) -> p h d", h=2)),
                        reads=[t_pj[j2]], writes=[t_vbst[vbuf]])
                    tok0 = c * 512 + i * 128
                    k.op("sp", lambda e: e.dma_start(
                        out=self.va[s, tok0:tok0 + 128, :, :].rearrange("t h c -> t (h c)"),
                        in_=vst[:, vbuf, :, :, :].rearrange("p h a d -> p (h a d)")),
                        reads=[t_vst[vbuf]], dma=True)
                    k.op("sp", lambda e: e.dma_start(
                        out=self.vb[s, tok0:tok0 + 128, :, :].rearrange("t h c -> t (h c)"),
                        in_=vbst[:, vbuf, :, :, :].rearrange("p h a d -> p (h a d)")),
                        reads=[t_vbst[vbuf]], dma=True)
                return [s0, s1]

            def prefetch_r(idx):
                c = items[idx][1]
                b = idx % 2
                k.op("sp", lambda e: e.dma_start(out=rp[:, b, :, :], in_=self.rope[:, :, c * 512:(c + 1) * 512]),
                     writes=[t_rp[b]], dma=True)

            def head_task(idx):
                def s0():
                    if idx + 2 < len(items):
                        self.load_x(f, xin, items[idx + 2][0], items[idx + 2][1], idx + 2)
                    if idx + 1 < len(items):
                        prefetch_r(idx + 1)
                        self.front_a(f, idx + 1)
                return [s0]

            self.load_x(f, xin, items[0][0], items[0][1], 0)
            prefetch_r(0)
            if len(items) > 1:
                self.load_x(f, xin, items[1][0], items[1][1], 1)
            self.front_a(f, 0)
            self.front_b(f, l, items[0][0], 0)
            tasks = []
            for idx, (s, c) in enumerate(items):
                b = idx % 2
                tasks.append(head_task(idx))
                nxt = idx + 1 < len(items)
                fbq = [fb_task(idx + 1, fc) for fc in range(8)] if nxt else []
                for fcg in range(8):
                    tasks.append(gate_task(s, c, b, fcg))
                    if fcg >= 4 and fbq:
                        tasks.append(fbq.pop(0))
                        tasks.append(fbq.pop(0))
                for tq in range(8):
                    tasks.append(a_task(s, c, b, tq))
                for tq in range(5):
                    tasks.append(b_task(s, c, b, tq))
                for i in range(4):
                    tasks.append(v_task(s, c, b, i))
            self.run_skewed(tasks)

    def alloc_fin(self, st):
        g = {}
        g["R"] = self.sb(st, "fin_R", [128, 2, 512], F32)
        g["RS"] = self.sb(st, "fin_RS", [128, 2, 512], F32)
        g["tg"] = self.sb(st, "fin_tg", [128, 2, 512], F32)
        g["yst"] = self.sb(st, "fin_y", [128, 2, 512], BF16)
        for nm in ("R", "RS", "tg", "yst"):
            g["t_" + nm] = [T(nm + "0"), T(nm + "1")]
        g["n"] = 0
        return g

    def finalize(self, g, srcA, srcB, t_srcs, Gap, t_G, dram_ap):
        k = self.k
        b = g["n"] % 2
        g["n"] += 1
        R, RS, tg, yst = g["R"], g["RS"], g["tg"], g["yst"]
        k.op("dve", lambda e: e.reciprocal(out=R[0:64, b, :], in_=srcB[0:64, :]), reads=t_srcs, writes=[g["t_R"][b]])
        k.op("dve", lambda e: e.reciprocal(out=R[64:128, b, :], in_=srcA[64:128, :]), reads=t_srcs,
             writes=[g["t_R"][b]], partial=True)
        k.op("sp", lambda e: e.dma_start(out=RS[0:64, b, :], in_=R[64:128, b, :]), reads=[g["t_R"][b]],
             writes=[g["t_RS"][b]], dma=True)
        k.op("sp", lambda e: e.dma_start(out=RS[64:128, b, :], in_=R[0:64, b, :]), reads=[g["t_R"][b]],
             writes=[g["t_RS"][b]], dma=True, partial=True)
        k.op("dve", lambda e: e.tensor_tensor(out=tg[0:64, b, :], in0=srcA[0:64, :], in1=Gap[0:64, :], op=ALU.mult),
             reads=t_srcs + [t_G], writes=[g["t_tg"][b]])
        k.op("dve", lambda e: e.tensor_tensor(out=tg[64:128, b, :], in0=srcB[64:128, :], in1=Gap[64:128, :], op=ALU.mult),
             reads=t_srcs + [t_G], writes=[g["t_tg"][b]], partial=True)
        k.op("pool", lambda e: e.tensor_tensor(out=yst[:, b, :], in0=tg[:, b, :], in1=RS[:, b, :], op=ALU.mult),
             reads=[g["t_tg"][b], g["t_RS"][b]], writes=[g["t_yst"][b]])
        k.op("sp", lambda e: e.dma_start(out=dram_ap, in_=yst[:, b, :]), reads=[g["t_yst"][b]], dma=True)

    def phase_A(self, l):
        k, nseq, S = self.k, self.nseq, self.S
        with ExitStack() as st:
            QT = self.sb(st, "A_QT", [128, 2, S], BF16)
            KT = self.sb(st, "A_KT", [128, 2, S], BF16)
            G = self.sb(st, "A_G", [128, S], BF16)
            t_QT = [T("QT0"), T("QT1")]
            t_KT = [T("KT0"), T("KT1")]
            t_G = T("G")
            NVB = 3
            VD = self.sb(st, "A_VD", [128, NVB, self.NT, 192], BF16)
            t_VD = [T("VD%d" % i) for i in range(NVB)]
            OA = self.sb(st, "A_OA", [128, 2, 2, S], F32)
            t_OA = [[T("OA00"), T("OA01")], [T("OA10"), T("OA11")]]
            DT = self.sb(st, "A_DT", [128, S // 64], F32)
            RT = self.sb(st, "A_RT", [128, S // 64], F32)
            RROW = self.sb(st, "A_RROW", [2, S], F32)
            t_DT, t_RT, t_RROW = T("DT"), T("RT"), T("RROW")
            tg = self.sb(st, "A_tg", [128, 2, 512], F32)
            yst = self.sb(st, "A_yst", [128, 2, 512], BF16)
            t_tg = [T("tg0"), T("tg1")]
            t_yst = [T("yst0"), T("yst1")]
            pR = self.ps(st, "A_pR", [128, 512], F32)
            t_pR = T("pR")
            pending = []
            fin_n = [0]
            NPS = 5
            pS = self.ps(st, "A_pS", [128, NPS, 512], F32)
            t_pS = [T("pS%d" % i) for i in range(NPS)]
            pO = self.ps(st, "A_pO", [128, 2, 512], F32)
            t_pO = [T("pO0"), T("pO1")]
            NPT = 6
            pT = self.sb(st, "A_pT", [128, NPT, 512], BF16)
            pTm = self.sb(st, "A_pTm", [128, NPT, 512], BF16)
            t_pT = [T("pT%d" % i) for i in range(NPT)]
            t_pTm = [T("pTm%d" % i) for i in range(NPT)]
            MASK = self.CP[:, CP_MASK:CP_MASK + 512]
            LA = 3
            for i in range(NPS):
                k.op("dve", lambda e, i=i: e.memset(pS[:, i, :], 0.0), writes=[t_pS[i]])
            items = [(s, hp) for s in range(nseq) for hp in range(4)]

            def load_qk(idx):
                s, hp = items[idx]
                b = idx % 2
                k.op("sp", lambda e: e.dma_start(out=QT[:, b, :], in_=self.qkaT[s, hp * 128:(hp + 1) * 128, :]),
                     writes=[t_QT[b]], dma=True)
                k.op("sp", lambda e: e.dma_start(out=KT[:, b, :], in_=self.qkaT[s, 512 + hp * 128:512 + (hp + 1) * 128, :]),
                     writes=[t_KT[b]], dma=True)

            vloads = [(idx, d) for idx in range(len(items)) for d in PATTERNS]

            def load_v(vi):
                idx, d = vloads[vi]
                s, hp = items[idx]
                vb_ = vi % NVB
                nt = (S // d) // 128
                vv = self.va[s, :, hp, :].rearrange("(m p r) c -> r p m c", p=128, r=d)
                for r in range(d):
                    k.op("sp", lambda e, r=r: e.dma_start(out=VD[:, vb_, r * nt:(r + 1) * nt, :], in_=vv[r]),
                         writes=[t_VD[vb_]], dma=True, partial=(r > 0))

            groups = []
            vi = 0
            for idx, (s, hp) in enumerate(items):
                b = idx % 2
                first_of_item = True
                for d in PATTERNS:
                    vb_ = vi % NVB
                    first_of_pat = True
                    L = S // d
                    nt = L // 128
                    for hh in range(2):
                        qtiles = [(r, m) for r in range(d) for m in range(nt + 1)]
                        for g0 in range(0, len(qtiles), 2):
                            gd = dict(idx=idx, s=s, hp=hp, b=b, d=d, hh=hh, vb=vb_, nt=nt, L=L,
                                      tiles=qtiles[g0:g0 + 2], pre=[], last=False, n=len(groups))
                            if first_of_pat and vi + 2 < len(vloads):
                                gd["pre"].append(("v", vi + 2))
                            first_of_item = False
                            first_of_pat = False
                            groups.append(gd)
                    vi += 1
                groups[-1]["last"] = True
            oa_first = {}

            def plan(gd):
                mm = []
                segs = []
                for ti, (r, m) in enumerate(gd["tiles"]):
                    jlo = max(128 * m - 64, 0)
                    jhi = min(128 * m + 64, gd["L"])
                    n = jhi - jlo
                    qoff = jlo + 64 - 128 * m
                    if segs and segs[-1][0] == r and segs[-1][1] + segs[-1][2] == jlo:
                        segs[-1][2] += n
                    else:
                        segs.append([r, jlo, n, ti * 128 + qoff])
                    for kind, mk in ((0, m - 1), (1, m)):
                        if 0 <= mk < gd["nt"]:
                            mm.append((ti, kind, mk, jlo, n, qoff, r))
                return mm, segs

            def emit_qk(gd):
                mm, _ = plan(gd)
                b, d, hh = gd["b"], gd["d"], gd["hh"]
                sb_ = gd["n"] % NPS
                rows = slice(hh * 64, (hh + 1) * 64)
                for ii, (ti, kind, mk, jlo, n, qoff, r) in enumerate(mm):
                    c0 = (2 * ti + kind) * 128 + qoff
                    k.op("pe", lambda e, mk=mk, jlo=jlo, n=n, c0=c0, r=r: e.matmul(
                        pS[:, sb_, c0:c0 + n], lhsT=KT[rows, b, sl(128 * mk * d + r, 128, d)],
                        rhs=QT[rows, b, sl(jlo * d + r, n, d)], start=True, stop=True),
                        reads=[t_KT[b], t_QT[b]], writes=[t_pS[sb_]], signal=(ii == len(mm) - 1), partial=(ii > 0))

            def emit_em(gd):
                sb_ = gd["n"] % NPS
                pb = gd["n"] % NPT
                k.op("act", lambda e: e.activation(out=pT[:, pb, :], in_=pS[:, sb_, :], func=AF.Exp, scale=0.125),
                     reads=[t_pS[sb_]], writes=[t_pT[pb]])
                meng = "dve" if gd["n"] % 2 == 0 else "pool"
                k.op(meng, lambda e: e.tensor_tensor(out=pTm[:, pb, :], in0=pT[:, pb, :], in1=MASK, op=ALU.mult),
                     reads=[t_pT[pb]], writes=[t_pTm[pb]])

            def emit_pv(gd):
                for what, arg in gd["pre"]:
                    if what == "qk":
                        load_qk(arg)
                    else:
                        load_v(arg)
                mm, segs = plan(gd)
                b, d, hh, vb_, nt = gd["b"], gd["d"], gd["hh"], gd["vb"], gd["nt"]
                pb = gd["n"] % NPT
                ob = gd["n"] % 2
                vcol = slice(0, 128) if hh == 0 else slice(64, 192)
                for ii, (ti, kind, mk, jlo, n, qoff, r) in enumerate(mm):
                    c0 = (2 * ti + kind) * 128 + qoff
                    first = (ii == 0) or (mm[ii - 1][0] != ti)
                    lastk = (ii == len(mm) - 1) or (mm[ii + 1][0] != ti)
                    k.op("pe", lambda e, mk=mk, n=n, c0=c0, ti=ti, qoff=qoff, first=first, lastk=lastk, r=r: e.matmul(
                        pO[:, ob, ti * 128 + qoff:ti * 128 + qoff + n],
                        lhsT=VD[:, vb_, r * nt + mk, vcol], rhs=pTm[:, pb, c0:c0 + n], start=first, stop=lastk),
                        reads=[t_VD[vb_], t_pTm[pb]], writes=[t_pO[ob]], signal=(ii == len(mm) - 1), partial=(ii > 0))
                ib = gd["idx"] % 2
                for (r, jlo, n, po0) in segs:
                    dst = OA[:, ib, hh, sl(jlo * d + r, n, d)]
                    if d == PATTERNS[0]:
                        fk = (gd["idx"], hh)
                        k.op("dve", lambda e, dst=dst, po0=po0, n=n: e.tensor_copy(out=dst, in_=pO[:, ob, po0:po0 + n]),
                             reads=[t_pO[ob]], writes=[t_OA[ib][hh]], partial=(fk in oa_first))
                        oa_first[fk] = True
                    else:
                        k.op("dve", lambda e, dst=dst, po0=po0, n=n: e.tensor_tensor(
                            out=dst, in0=pO[:, ob, po0:po0 + n], in1=dst, op=ALU.add),
                            reads=[t_pO[ob], t_OA[ib][hh]], writes=[t_OA[ib][hh]], partial=True)
                if gd["last"]:
                    queue_finalize(gd)
                elif pending:
                    fn = pending.pop(0)
                    if fn is not None:
                        fn()

            def queue_finalize(gd):
                s, hp, idx, b = gd["s"], gd["hp"], gd["idx"], gd["b"]
                ib = idx % 2
                tA, tB = t_OA[ib][0], t_OA[ib][1]
                k.op("sp", lambda e: e.dma_start(out=DT[0:64, :], in_=OA[64:65, ib, 0, :].rearrange("p (a b) -> p a b", b=S // 64)),
                     reads=[tA], writes=[t_DT], dma=True)
                k.op("sp", lambda e: e.dma_start(out=DT[64:128, :], in_=OA[0:1, ib, 1, :].rearrange("p (a b) -> p a b", b=S // 64)),
                     reads=[tB], writes=[t_DT], dma=True, partial=True)

                def stage_b():
                    k.op("sp", lambda e: e.dma_start(out=G[:, :], in_=self.gateT[s, hp * 128:(hp + 1) * 128, :]),
                         writes=[t_G], dma=True)
                    k.op("dve", lambda e: e.reciprocal(out=RT[:, :], in_=DT[:, :]), reads=[t_DT], writes=[t_RT])
                    k.op("sp", lambda e: e.dma_start(out=RROW[0:1, :].rearrange("p (a b) -> p a b", b=S // 64), in_=RT[0:64, :]),
                         reads=[t_RT], writes=[t_RROW], dma=True)
                    k.op("sp", lambda e: e.dma_start(out=RROW[1:2, :].rearrange("p (a b) -> p a b", b=S // 64), in_=RT[64:128, :]),
                         reads=[t_RT], writes=[t_RROW], dma=True, partial=True)

                def stage_c(c):
                    def run():
                        cols = slice(c * 512, (c + 1) * 512)
                        fb = fin_n[0] % 2
                        fin_n[0] += 1
                        k.op("pe", lambda e: e.matmul(pR[:, :], lhsT=self.SEL2[:, :], rhs=RROW[:, cols], start=True, stop=True),
                             reads=[t_RROW], writes=[t_pR])
                        k.op("dve", lambda e: e.tensor_tensor(out=tg[0:64, fb, :], in0=OA[0:64, ib, 0, cols], in1=G[0:64, cols],
                                                              op=ALU.mult),
                             reads=[tA, t_G], writes=[t_tg[fb]])
                        k.op("dve", lambda e: e.tensor_tensor(out=tg[64:128, fb, :], in0=OA[64:128, ib, 1, cols],
                                                              in1=G[64:128, cols], op=ALU.mult),
                             reads=[tB, t_G], writes=[t_tg[fb]], partial=True)
                        k.op("dve", lambda e: e.tensor_tensor(out=yst[:, fb, :], in0=tg[:, fb, :], in1=pR[:, :], op=ALU.mult),
                             reads=[t_tg[fb], t_pR], writes=[t_yst[fb]])
                        k.op("sp", lambda e: e.dma_start(out=self.yT[s, hp * 128:(hp + 1) * 128, cols], in_=yst[:, fb, :]),
                             reads=[t_yst[fb]], dma=True)
                    return run
                pending.append(None)
                pending.append(stage_b)
                pending.append(None)
                for c in range(self.NC):
                    pending.append(stage_c(c))
                if idx + 2 < len(items):
                    pending.append(lambda: load_qk(idx + 2))

            load_qk(0)
            if len(items) > 1:
                load_qk(1)
            load_v(0)
            load_v(1)
            for i in range(-LA, len(groups)):
                if 0 <= i + LA < len(groups):
                    emit_qk(groups[i + LA])
                if 0 <= i + 1 < len(groups):
                    emit_em(groups[i + 1])
                if 0 <= i:
                    emit_pv(groups[i])
            while pending:
                fn = pending.pop(0)
                if fn is not None:
                    fn()

    def phase_B(self, l):
        k, nseq, S = self.k, self.nseq, self.S
        with ExitStack() as st:
            g = self.alloc_fin(st)
            QT = self.sb(st, "B_QT", [128, 2, S], BF16)
            KT = self.sb(st, "B_KT", [128, 2, S], BF16)
            G = self.sb(st, "B_G", [128, 2, S], BF16)
            V1 = self.sb(st, "B_V1", [128, 2, self.NT, 192], BF16)
            t_QT = [T("QT0"), T("QT1")]
            t_KT = [T("KT0"), T("KT1")]
            t_G = [T("G0"), T("G1")]
            t_V1 = [T("V10"), T("V11")]
            pS = self.ps(st, "B_pS", [128, 2, 1024], F32)
            t_pS = [T("pS0"), T("pS1")]
            pO = self.ps(st, "B_pO", [128, 2, 2, 512], F32)
            t_pO = [T("pO0"), T("pO1")]
            NPT = 3
            pT = self.sb(st, "B_pT", [128, NPT, 1024], BF16)
            t_pT = [T("pT%d" % i) for i in range(NPT)]
            items = [(s, hp) for s in range(nseq) for hp in range(4)]

            def load(idx):
                s, hp = items[idx]
                b = idx % 2
                kv = hp // 2
                k.op("sp", lambda e: e.dma_start(out=QT[:, b, :], in_=self.qkbT[s, hp * 128:(hp + 1) * 128, :]),
                     writes=[t_QT[b]], dma=True)
                k.op("sp", lambda e: e.dma_start(out=KT[0:64, b, :], in_=self.qkbT[s, 512 + kv * 64:512 + (kv + 1) * 64, :]),
                     writes=[t_KT[b]], dma=True)
                k.op("sp", lambda e: e.dma_start(out=KT[64:128, b, :], in_=self.qkbT[s, 512 + kv * 64:512 + (kv + 1) * 64, :]),
                     writes=[t_KT[b]], dma=True, partial=True)
                k.op("sp", lambda e: e.dma_start(out=G[:, b, :], in_=self.gateT[s, 512 + hp * 128:512 + (hp + 1) * 128, :]),
                     writes=[t_G[b]], dma=True)
                k.op("sp", lambda e: e.dma_start(out=V1[:, b, :, :],
                                                 in_=self.vb[s, :, kv, :].rearrange("(m p) c -> p m c", p=128)),
                     writes=[t_V1[b]], dma=True)

            steps = []
            for idx, (s, hp) in enumerate(items):
                for qc in range(self.NC):
                    for kt in range(self.NT):
                        steps.append(dict(idx=idx, s=s, hp=hp, b=idx % 2, qc=qc, kt=kt, n=len(steps),
                                          ob=(idx * self.NC + qc) % 2))

            def emit_qk(sd):
                b, qc, kt = sd["b"], sd["qc"], sd["kt"]
                sb_ = sd["n"] % 2
                for hh in range(2):
                    rows = slice(hh * 64, (hh + 1) * 64)
                    k.op("pe", lambda e, hh=hh, rows=rows: e.matmul(
                        pS[:, sb_, hh * 512:(hh + 1) * 512], lhsT=KT[rows, b, kt * 128:(kt + 1) * 128],
                        rhs=QT[rows, b, qc * 512:(qc + 1) * 512], start=True, stop=True),
                        reads=[t_KT[b], t_QT[b]], writes=[t_pS[sb_]], signal=(hh == 1), partial=(hh == 1))

            def emit_exp(sd):
                if sd["qc"] == 0 and sd["kt"] == 0 and sd["idx"] + 1 < len(items):
                    load(sd["idx"] + 1)
                sb_ = sd["n"] % 2
                pb = sd["n"] % NPT
                k.op("act", lambda e: e.activation(out=pT[:, pb, :], in_=pS[:, sb_, :], func=AF.Exp, scale=0.125),
                     reads=[t_pS[sb_]], writes=[t_pT[pb]])

            def emit_pv(sd):
                b, qc, kt, ob = sd["b"], sd["qc"], sd["kt"], sd["ob"]
                pb = sd["n"] % NPT
                for hh in range(2):
                    vcol = slice(64, 192) if hh == 0 else slice(0, 128)
                    k.op("pe", lambda e, hh=hh, vcol=vcol: e.matmul(
                        pO[:, ob, hh, :], lhsT=V1[:, b, kt, vcol], rhs=pT[:, pb, hh * 512:(hh + 1) * 512],
                        start=(kt == 0), stop=(kt == self.NT - 1)),
                        reads=[t_V1[b], t_pT[pb]], writes=[t_pO[ob]], signal=(hh == 1), partial=not (kt == 0 and hh == 0))
                if kt == self.NT - 1:
                    s, hp = sd["s"], sd["hp"]
                    qcols = slice(qc * 512, (qc + 1) * 512)
                    self.finalize(g, pO[:, ob, 0, :], pO[:, ob, 1, :], [t_pO[ob]], G[:, b, qcols], t_G[b],
                                  self.yT[s, 512 + hp * 128:512 + (hp + 1) * 128, qcols])

            load(0)
            emit_qk(steps[0])
            if len(steps) > 1:
                emit_qk(steps[1])
            for n in range(len(steps)):
                emit_exp(steps[n])
                if n + 2 < len(steps):
                    emit_qk(steps[n + 2])
                emit_pv(steps[n])

    def phase_U(self, l, xin):
        k, nseq = self.k, self.nseq
        pi = l // 2
        with ExitStack() as st:
            f = self.alloc_front(st)
            W = self.sb(st, "U_W", [128, 8, 2048], BF16)
            t_W = T("W")
            wv = self.pool_w_in[pi].rearrange("(kc p) n -> p kc n", p=128)
            for kc in range(8):
                self.load_w(W[:, kc, :], wv[:, kc, :], t_W)
            pj = self.ps(st, "U_pj", [128, 4, 512], F32)
            t_pj = [T("pj%d" % i) for i in range(4)]
            NSTG = 4
            stg = self.sb(st, "U_stg", [128, NSTG, 512], BF16)
            t_stg = [T("stg%d" % i) for i in range(NSTG)]
            ust = self.sb(st, "U_ust", [128, 2, D], BF16)
            t_ust = [T("ust0"), T("ust1")]
            cnt = {"pj": 0, "stg": 0, "u": 0}
            items = [(s, c) for s in range(nseq) for c in range(self.NC)]
            self.load_x(f, xin, items[0][0], items[0][1], 0)
            if len(items) > 1:
                self.load_x(f, xin, items[1][0], items[1][1], 1)
            self.front_a(f, 0)
            self.front_b(f, l, items[0][0], 0)
            for idx, (s, c) in enumerate(items):
                if idx + 2 < len(items):
                    self.load_x(f, xin, items[idx + 2][0], items[idx + 2][1], idx + 2)
                if idx + 1 < len(items):
                    self.front_a(f, idx + 1)
                b = idx % 2
                hT, t_hT = f["hT"], f["t_hT"][b]
                cols = slice(c * 512, (c + 1) * 512)
                for fcg in range(8):
                    j = cnt["pj"] % 4
                    cnt["pj"] += 1
                    for kc in range(8):
                        k.op("pe", lambda e, kc=kc, fcg=fcg, j=j: e.matmul(
                            pj[:, j, :], lhsT=W[:, kc, 1024 + fcg * 128:1024 + (fcg + 1) * 128], rhs=hT[:, b, kc, :],
                            start=(kc == 0), stop=(kc == 7)),
                            reads=[t_W, t_hT], writes=[t_pj[j]], signal=(kc == 7), partial=(kc > 0))
                    si = cnt["stg"] % NSTG
                    cnt["stg"] += 1
                    k.op("act", lambda e, j=j, si=si: e.activation(out=stg[:, si, :], in_=pj[:, j, :], func=AF.Silu),
                         reads=[t_pj[j]], writes=[t_stg[si]])
                    k.op("sp", lambda e, si=si, fcg=fcg: e.dma_start(out=self.gateT[s, fcg * 128:(fcg + 1) * 128, cols],
                                                                   in_=stg[:, si, :]), reads=[t_stg[si]], dma=True)
                if idx + 1 < len(items):
                    self.front_b(f, l, items[idx + 1][0], idx + 1)
                for i in range(4):
                    ub = cnt["u"] % 2
                    cnt["u"] += 1
                    for half in range(2):
                        j = cnt["pj"] % 4
                        cnt["pj"] += 1
                        for kc in range(8):
                            k.op("pe", lambda e, kc=kc, i=i, half=half, j=j: e.matmul(
                                pj[:, j, :], lhsT=hT[:, b, kc, i * 128:(i + 1) * 128], rhs=W[:, kc, half * 512:(half + 1) * 512],
                                start=(kc == 0), stop=(kc == 7)),
                                reads=[t_W, t_hT], writes=[t_pj[j]], signal=(kc == 7), partial=(kc > 0))
                        k.op("dve", lambda e, j=j, ub=ub, half=half: e.tensor_copy(out=ust[:, ub, half * 512:(half + 1) * 512],
                                                                                 in_=pj[:, j, :]),
                             reads=[t_pj[j]], writes=[t_ust[ub]], partial=(half == 1))
                    tok0 = c * 512 + i * 128
                    k.op("sp", lambda e, ub=ub, tok0=tok0: e.dma_start(out=self.u[s, tok0:tok0 + 128, :], in_=ust[:, ub, :]),
                         reads=[t_ust[ub]], dma=True)

    def phase_O(self, l, xin, xout, attn):
        k, nseq = self.k, self.nseq
        wi = l // 2
        with ExitStack() as st:
            Wo = self.sb(st, "O_W", [128, 8, D], BF16)
            t_W = T("Wo")
            wsrc = self.attn_w_out[wi] if attn else self.pool_w_out[wi]
            wv = wsrc.rearrange("(kc p) n -> p kc n", p=128)
            for kc in range(0, 8, 2):
                self.load_w(Wo[:, kc:kc + 2, :], wv[:, kc:kc + 2, :], t_W)
            yTc = self.sb(st, "O_yT", [128, 2, 8, 512], BF16)
            t_yT = [T("yT0"), T("yT1")]
            xt = self.sb(st, "O_xt", [128, 2, 4, D], F32)
            t_xt = [T("xt0"), T("xt1")]
            xo = self.sb(st, "O_xo", [128, 2, 4, D], F32)
            t_xo = [T("xo0"), T("xo1")]
            gpb = self.sb(st, "O_gpb", [128, D], F32)
            t_gpb = T("gpb")
            tt = self.sb(st, "O_tt", [128, 2, D], F32)
            t_tt = [T("tt0"), T("tt1")]
            junk = self.sb(st, "O_junk", [128, D], BF16)
            t_junk = T("junk")
            ss = self.sb(st, "O_ss", [128, 2], F32)
            sd = self.sb(st, "O_sd", [128, 2], F32)
            rs = self.sb(st, "O_rs", [128, 2], F32)
            t_ss = [T("ss0"), T("ss1")]
            t_sd = [T("sd0"), T("sd1")]
            t_rs = [T("rs0"), T("rs1")]
            pm = self.ps(st, "O_pm", [128, 2, 1024], F32)
            t_pm = [T("pm0"), T("pm1")]
            cnt = {"m": 0, "pp": 0, "pg": 0}
            if not attn:
                Wg = self.sb(st, "V_Wg", [128, 4, 2, 256], BF16)
                t_Wg = T("Wg")
                self.load_w(Wg[:, :, :, :], self.pool_w_grp[wi].rearrange("g (cc p) d -> p g cc d", p=128), t_Wg)
                uc = self.sb(st, "V_uc", [128, 2, 4, D], BF16)
                t_uc = [T("uc0"), T("uc1")]
                HPt = self.sb(st, "V_HP", [128, 2, D], BF16)
                HNt = self.sb(st, "V_HN", [128, 2, D], BF16)
                t_HP = [T("HP0"), T("HP1")]
                t_HN = [T("HN0"), T("HN1")]
                for hb in range(2):
                    k.op("pool", lambda e, hb=hb: e.memset(HPt[:, hb, :], 0.0), writes=[t_HP[hb]])
                    k.op("pool", lambda e, hb=hb: e.memset(HNt[:, hb, :], 0.0), writes=[t_HN[hb]])
                gc = self.sb(st, "V_gc", [128, 2, 8, 512], BF16)
                t_gc = [T("gc0"), T("gc1")]
                pooled = self.sb(st, "V_pooled", [128, 8, 512], BF16)
                t_pooled = [T("pooled%d" % i) for i in range(8)]
                pp = self.ps(st, "V_pp", [128, 2, 512], F32)
                t_pp = [T("pp0"), T("pp1")]
                pg = self.ps(st, "V_pg", [128, 2, 512], F32)
                t_pg = [T("pg0"), T("pg1")]
            pgp = self.ps(st, "O_pgp", [128, 2, 512], F32) if attn else None
            t_pgp = T("pgp")
            items = [(s, c) for s in range(nseq) for c in range(self.NC)]

            def prefetch(idx):
                s, c = items[idx]
                b = idx % 2
                xv = xin[s].rearrange("(i p) d -> p i d", p=128)
                k.op("sp", lambda e: e.dma_start(out=xt[:, b, :, :], in_=xv[:, 4 * c:4 * c + 4, :]), writes=[t_xt[b]], dma=True)
                if attn:
                    yv = self.yT[s].rearrange("(kc p) t -> p kc t", p=128)
                    k.op("sp", lambda e: e.dma_start(out=yTc[:, b, :, :], in_=yv[:, :, c * 512:(c + 1) * 512]),
                         writes=[t_yT[b]], dma=True)
                else:
                    uv = self.u[s].rearrange("(i p) d -> p i d", p=128)
                    k.op("sp", lambda e: e.dma_start(out=uc[:, b, :, :], in_=uv[:, 4 * c:4 * c + 4, :]), writes=[t_uc[b]], dma=True)
                    if c > 0:
                        k.op("sp", lambda e: e.dma_start(out=HPt[96:128, b, :], in_=self.u[s, c * 512 - 32:c * 512, :]),
                             writes=[t_HP[b]], dma=True, partial=True)
                    if c < self.NC - 1:
                        k.op("sp", lambda e: e.dma_start(out=HNt[0:32, b, :], in_=self.u[s, (c + 1) * 512:(c + 1) * 512 + 32, :]),
                             writes=[t_HN[b]], dma=True, partial=True)
                    gv = self.gateT[s].rearrange("(kc p) t -> p kc t", p=128)
                    k.op("sp", lambda e: e.dma_start(out=gc[:, b, :, :], in_=gv[:, :, c * 512:(c + 1) * 512]),
                         writes=[t_gc[b]], dma=True)

            prefetch(0)
            for idx, (s, c) in enumerate(items):
                b = idx % 2
                if idx + 1 < len(items):
                    prefetch(idx + 1)
                if c == 0:
                    pgx = pgp if attn else pg
                    t_pgx = [t_pgp, t_pgp] if attn else t_pg
                    for half in range(2):
                        k.op("pe", lambda e, half=half: e.matmul(pgx[:, half, :], lhsT=self.SEL[:, s, :],
                                                                  rhs=self.GP[:, l, half * 512:(half + 1) * 512],
                                                                  start=True, stop=True),
                             writes=[t_pgx[half]])
                        k.op("act", lambda e, half=half: e.activation(out=gpb[:, half * 512:(half + 1) * 512],
                                                                      in_=pgx[:, half, :], func=AF.Copy),
                             reads=[t_pgx[half]], writes=[t_gpb], partial=(half == 1))
                if not attn:
                    for fc in range(8):
                        gI = fc // 2
                        ppb = cnt["pp"] % 2
                        cnt["pp"] += 1
                        ops = []
                        for i in range(4):
                            Tg = 4 * c + i
                            var = 1 if Tg == 0 else (2 if Tg == self.NT - 1 else 0)
                            fcols = slice(fc * 128, (fc + 1) * 128)
                            oc = slice(i * 128, (i + 1) * 128)
                            bc0 = CP_BC + (gI * 3 + var) * 128
                            lst = [(uc[:, b, i, fcols], self.CP[:, bc0:bc0 + 128], [t_uc[b]])]
                            if Tg > 0:
                                bp0 = CP_BP + gI * 128
                                if i > 0:
                                    lst.append((uc[64:128, b, i - 1, fcols], self.CP[64:128, bp0:bp0 + 128], [t_uc[b]]))
                                else:
                                    lst.append((HPt[64:128, b, fcols], self.CP[64:128, bp0:bp0 + 128], [t_HP[b]]))
                            if Tg < self.NT - 1:
                                bn0 = CP_BN + gI * 128
                                if i < 3:
                                    lst.append((uc[0:32, b, i + 1, fcols], self.CP[0:32, bn0:bn0 + 128], [t_uc[b]]))
                                else:
                                    lst.append((HNt[0:32, b, fcols], self.CP[0:32, bn0:bn0 + 128], [t_HN[b]]))
                            for q, (lh, rh, rd) in enumerate(lst):
                                ops.append((oc, lh, rh, rd, q == 0, q == len(lst) - 1))
                        for q, (oc, lh, rh, rd, stt, stp) in enumerate(ops):
                            k.op("pe", lambda e, oc=oc, lh=lh, rh=rh, stt=stt, stp=stp: e.matmul(
                                pp[:, ppb, oc], lhsT=lh, rhs=rh, start=stt, stop=stp),
                                reads=rd, writes=[t_pp[ppb]], signal=(q == len(ops) - 1), partial=(q > 0))
                        k.op("act", lambda e, fc=fc, ppb=ppb: e.activation(out=pooled[:, fc, :], in_=pp[:, ppb, :], func=AF.Copy),
                             reads=[t_pp[ppb]], writes=[t_pooled[fc]])
                    for dc in range(8):
                        gI = dc // 2
                        pgb = cnt["pg"] % 2
                        cnt["pg"] += 1
                        for cc in range(2):
                            k.op("pe", lambda e, dc=dc, gI=gI, cc=cc, pgb=pgb: e.matmul(
                                pg[:, pgb, :], lhsT=Wg[:, gI, cc, (dc % 2) * 128:(dc % 2 + 1) * 128],
                                rhs=pooled[:, 2 * gI + cc, :], start=(cc == 0), stop=(cc == 1)),
                                reads=[t_Wg, t_pooled[2 * gI + cc]], writes=[t_pg[pgb]], signal=(cc == 1), partial=(cc == 1))
                        k.op("dve", lambda e, dc=dc, pgb=pgb: e.scalar_tensor_tensor(
                            out=yTc[:, b, dc, :], in0=pg[:, pgb, :], scalar=self.PSC[:, wi, dc:dc + 1], in1=gc[:, b, dc, :],
                            op0=ALU.mult, op1=ALU.mult),
                            reads=[t_pg[pgb], t_gc[b]], writes=[t_yT[b]], partial=(dc > 0))
                for i in range(4):
                    mb = cnt["m"] % 2
                    cnt["m"] += 1
                    for half in range(2):
                        for kc in range(8):
                            k.op("pe", lambda e, i=i, half=half, kc=kc, mb=mb: e.matmul(
                                pm[:, mb, half * 512:(half + 1) * 512], lhsT=yTc[:, b, kc, i * 128:(i + 1) * 128],
                                rhs=Wo[:, kc, half * 512:(half + 1) * 512], start=(kc == 0), stop=(kc == 7)),
                                reads=[t_W, t_yT[b]], writes=[t_pm[mb]], signal=(kc == 7 and half == 1),
                                partial=not (kc == 0 and half == 0))
                    k.op("act", lambda e, mb=mb: e.activation(out=junk[:, :], in_=pm[:, mb, :], func=AF.Square,
                                                              accum_out=ss[:, mb:mb + 1]),
                         reads=[t_pm[mb]], writes=[t_junk, t_ss[mb]])
                    k.op("act", lambda e, mb=mb: e.activation(out=sd[:, mb:mb + 1], in_=ss[:, mb:mb + 1], func=AF.Sqrt,
                                                              scale=1.0 / D, bias=self.EPS[:, 0:1]),
                         reads=[t_ss[mb]], writes=[t_sd[mb]])
                    k.op("dve", lambda e, mb=mb: e.reciprocal(out=rs[:, mb:mb + 1], in_=sd[:, mb:mb + 1]),
                         reads=[t_sd[mb]], writes=[t_rs[mb]])
                    k.op("dve", lambda e, mb=mb: e.scalar_tensor_tensor(
                        out=tt[:, mb, :], in0=pm[:, mb, :], scalar=rs[:, mb:mb + 1], in1=gpb[:, :],
                        op0=ALU.mult, op1=ALU.mult),
                        reads=[t_pm[mb], t_rs[mb], t_gpb], writes=[t_tt[mb]])
                    k.op("pool", lambda e, i=i, mb=mb: e.tensor_tensor(out=xo[:, b, i, :], in0=tt[:, mb, :], in1=xt[:, b, i, :],
                                                                      op=ALU.add),
                         reads=[t_tt[mb], t_xt[b]], writes=[t_xo[b]], partial=(i > 0))
                ov = xout[s].rearrange("(i p) d -> p i d", p=128)
                k.op("sp", lambda e: e.dma_start(out=ov[:, 4 * c:4 * c + 4, :], in_=xo[:, b, :, :]), reads=[t_xo[b]], dma=True)


_PROG_CACHE = {}


def _get_prog(nseq, S, layers=(0, 1, 2, 3)):
    key = (nseq, S, tuple(layers))
    if key not in _PROG_CACHE:
        _PROG_CACHE[key] = Prog(nseq, S, layers)
    return _PROG_CACHE[key]


def _shared_inputs(nseq, S, ada_w, ada_b, pre_norm, post_norm, attn_w_in, attn_q_norm, attn_k_norm,
                   attn_w_out, pool_w_in, pool_w_grp, pool_scale, pool_w_out):
    f = lambda a: np.ascontiguousarray(np.asarray(a, dtype=np.float32))
    ada_b = f(ada_b)
    m = {}
    m["ada_w"] = f(ada_w)
    m["ada_bT"] = np.ascontiguousarray(ada_b[:, :2048].reshape(4, 16, 128).transpose(2, 0, 1))
    m["ada_bg"] = np.ascontiguousarray(np.broadcast_to(ada_b[None, :, 2048:], (nseq, 4, D)))
    m["pre_T"] = np.ascontiguousarray(f(pre_norm).reshape(4, 8, 128).transpose(2, 0, 1))
    m["post_rep"] = np.ascontiguousarray(np.broadcast_to(f(post_norm)[None], (nseq, 4, D)))
    m["attn_w_in"] = f(attn_w_in)
    qg = np.stack([f(attn_q_norm), f(attn_k_norm)], axis=-1)
    m["qk_g"] = np.ascontiguousarray(np.concatenate([qg, qg], axis=1).transpose(1, 0, 2))
    m["attn_w_out"] = f(attn_w_out)
    m["pool_w_in"] = f(pool_w_in)
    m["pool_w_grp"] = f(pool_w_grp)
    m["pool_scT"] = np.ascontiguousarray(f(pool_scale).reshape(2, 8, 128).transpose(2, 0, 1))
    m["pool_w_out"] = f(pool_w_out)
    m.update(_consts(S, nseq))
    return m


def run_trunk(xs, cs, weights, n_cores, layers=(0, 1, 2, 3)):
    nseq, S = xs.shape[1], xs.shape[2]
    prog = _get_prog(nseq, S, layers)
    shared = _shared_inputs(nseq, S, **weights)
    in_maps = []
    for i in range(n_cores):
        m = dict(shared)
        m["x"] = np.ascontiguousarray(xs[i])
        m["cT"] = np.ascontiguousarray(cs[i].reshape(nseq, 8, 128).transpose(2, 1, 0))
        in_maps.append(m)
    res = run_bass_kernel_spmd(prog.nc, in_maps, core_ids=list(range(n_cores)))
    return np.stack([np.asarray(r["y"]) for r in res.results], axis=0)


def kernel(x_prompt, x_sample, c_prompt, c_sample, ada_w, ada_b, pre_norm, post_norm,
           attn_w_in, attn_q_norm, attn_k_norm, attn_w_out,
           pool_w_in, pool_w_grp, pool_scale, pool_w_out):
    x_prompt = np.asarray(x_prompt, dtype=np.float32)
    x_sample = np.asarray(x_sample, dtype=np.float32)
    c_prompt = np.asarray(c_prompt, dtype=np.float32)
    c_sample = np.asarray(c_sample, dtype=np.float32)
    nb_p, nb_s = x_prompt.shape[0], x_sample.shape[0]
    nreal = nb_p + nb_s
    nslots = N_CORES * NSEQ_FULL
    xs = np.empty((nslots, S_FULL, D), np.float32)
    cs = np.empty((nslots, D), np.float32)
    xs[:nb_p] = x_prompt
    xs[nb_p:nreal] = x_sample
    cs[:nb_p] = c_prompt
    cs[nb_p:nreal] = c_sample
    for j in range(nreal, nslots):
        xs[j] = xs[j - nreal]
        cs[j] = cs[j - nreal]
    weights = dict(ada_w=ada_w, ada_b=ada_b, pre_norm=pre_norm, post_norm=post_norm, attn_w_in=attn_w_in,
                   attn_q_norm=attn_q_norm, attn_k_norm=attn_k_norm, attn_w_out=attn_w_out, pool_w_in=pool_w_in,
                   pool_w_grp=pool_w_grp, pool_scale=pool_scale, pool_w_out=pool_w_out)
    y = run_trunk(xs.reshape(N_CORES, NSEQ_FULL, S_FULL, D), cs.reshape(N_CORES, NSEQ_FULL, D), weights, N_CORES)
    y = y.reshape(nslots, S_FULL, D)
    return (np.ascontiguousarray(y[:nb_p]), np.ascontiguousarray(y[nb_p:nreal]))
```

```python
import math
from contextlib import ExitStack

import numpy as np
import ml_dtypes

import concourse.bass as bass
import concourse.mybir as mybir
from concourse.bass_utils import run_bass_kernel_spmd

F32 = mybir.dt.float32
BF16 = mybir.dt.bfloat16
AF = mybir.ActivationFunctionType
ALU = mybir.AluOpType
NPBF = ml_dtypes.bfloat16

D = 1024
HD = 64
EPS = 1e-6
PATTERNS = (1, 4, 16)
POOL_HALF = (1, 2, 4, 8)
N_CORES = 8
NSEQ_FULL = 3
S_FULL = 4096


def sl(start, count, step=1):
    return slice(start, start + (count - 1) * step + 1, step)


class Ref:
    __slots__ = ("kind", "eng", "sem", "value")

    def __init__(self, kind, eng, sem, value):
        self.kind, self.eng, self.sem, self.value = kind, eng, sem, value


class T:
    __slots__ = ("name", "writers", "readers")

    def __init__(self, name):
        self.name = name
        self.writers = []
        self.readers = {}


class EngS:
    def __init__(self, name, h, sem):
        self.name, self.h, self.sem = name, h, sem
        self.count = 0
        self.pending = []
        self.waited = {}


class KB:
    def __init__(self, nc, stack, n_dma_sems=40):
        self.nc = nc
        self.E = {}
        for name, h in (("pe", nc.tensor), ("act", nc.scalar), ("dve", nc.vector),
                        ("pool", nc.gpsimd), ("sp", nc.sync)):
            sem = stack.enter_context(nc.semaphore("sem_" + name))
            self.E[name] = EngS(name, h, sem)
        self.dsems = [[stack.enter_context(nc.semaphore("dsem%d" % i)), 0] for i in range(n_dma_sems)]
        self.dnext = 0
        self.swsems = [[stack.enter_context(nc.semaphore("swsem%d" % i)), 0] for i in range(8)]
        self.swnext = 0
        self.nops = 0

    def _wait(self, E, sem, val):
        key = id(sem)
        if E.waited.get(key, 0) >= val:
            return
        E.h.wait_ge(sem, val)
        E.waited[key] = val

    def op(self, eng, fn, reads=(), writes=(), signal=True, partial=False, dma=False):
        E = self.E[eng]
        self.nops += 1
        for t in reads:
            for r in t.writers:
                self._dep(E, eng, r, "raw")
        for t in writes:
            for r in t.readers.values():
                self._dep(E, eng, r, "war")
            for r in t.writers:
                self._dep(E, eng, r, "waw")
        if dma:
            if eng == "pool":
                slot = self.swsems[self.swnext]
                self.swnext = (self.swnext + 1) % len(self.swsems)
            else:
                slot = self.dsems[self.dnext]
                self.dnext = (self.dnext + 1) % len(self.dsems)
            if slot[1] > 0:
                self._wait(E, slot[0], slot[1])
            ins = fn(E.h)
            ins.then_inc(slot[0], 16)
            slot[1] += 16
            ref = Ref("dma", eng, slot[0], slot[1])
        else:
            ins = fn(E.h)
            ref = Ref("eng", eng, E.sem, None)
            if signal:
                E.count += 1
                ins.then_inc(E.sem, 1)
                ref.value = E.count
                for p in E.pending:
                    p.value = E.count
                E.pending = []
            else:
                E.pending.append(ref)
        for t in reads:
            t.readers[eng if not dma else ("dma", id(ref))] = ref
        for t in writes:
            if partial:
                t.writers.append(ref)
            else:
                t.writers = [ref]
                t.readers = {}
        return ref

    def _dep(self, E, eng, r, kind):
        if r.kind == "eng" and r.eng == eng:
            if eng == "pe":
                return
        assert r.value is not None, "dependency on an unsignaled op (engine %s)" % r.eng
        self._wait(E, r.sem, r.value)

    def barrier(self):
        for E in self.E.values():
            assert not E.pending, "pending unsignaled ops on %s at barrier" % E.name
        for E in self.E.values():
            for F in self.E.values():
                if F is not E and F.count > 0:
                    self._wait(E, F.sem, F.count)
            for sem, cnt in self.dsems + self.swsems:
                if cnt > 0:
                    self._wait(E, sem, cnt)


def _rope_tables(S):
    t = np.arange(S, dtype=np.float32)
    invA = (np.float32(500000.0) ** (-np.arange(0, 16, 2, dtype=np.float32) / np.float32(16))).astype(np.float32)
    angA = (t[:, None] * invA[None, :]).astype(np.float32)
    CA = np.ones((128, S), np.float32)
    SA = np.zeros((128, S), np.float32)
    for p in range(128):
        dd = p % 64
        if dd < 16:
            CA[p] = np.cos(angA[:, dd % 8])
            SA[p] = np.sin(angA[:, dd % 8])
    invB = (np.float32(10000.0) ** (-np.arange(0, 32, 2, dtype=np.float32) / np.float32(32))).astype(np.float32)
    row = (np.arange(S) // 64).astype(np.float32)
    col = (np.arange(S) % 64).astype(np.float32)
    CB = np.zeros((128, S), np.float32)
    SB = np.zeros((128, S), np.float32)
    for p in range(128):
        dd = p % 64
        pos = row if dd < 32 else col
        f = (dd % 32) % 16
        ang = (pos * invB[f]).astype(np.float32)
        CB[p] = np.cos(ang)
        SB[p] = np.sin(ang)
    return CA, SA, CB, SB


def _rot_mats():
    RA = np.zeros((128, 128), np.float32)
    RB = np.zeros((128, 128), np.float32)
    for m in range(128):
        dd = m % 64
        if dd < 8:
            RA[m + 8, m] = -1.0
        elif dd < 16:
            RA[m - 8, m] = 1.0
        e = dd % 32
        if e < 16:
            RB[m + 16, m] = -1.0
        else:
            RB[m - 16, m] = 1.0
    return RA, RB


def _band_mats():
    BC = np.zeros((128, 4, 3, 128), np.float32)
    BP = np.zeros((128, 4, 128), np.float32)
    BN = np.zeros((128, 4, 128), np.float32)
    for g, h in enumerate(POOL_HALF):
        for var in range(3):
            for tout in range(128):
                lo, hi = tout - h, tout + h
                clo = max(lo, 0) if var == 1 else lo
                chi = min(hi, 128) if var == 2 else hi
                cnt = chi - clo
                for tin in range(max(clo, 0), min(chi, 128)):
                    BC[tin, g, var, tout] += 1.0 / cnt
                BC[tout, g, var, tout] -= 1.0
        for tout in range(128):
            for i in range(32):
                if -32 + i >= tout - h:
                    BP[96 + i, g, tout] = 1.0 / (2 * h)
                if 128 + i < tout + h:
                    BN[i, g, tout] = 1.0 / (2 * h)
    return BC, BP, BN


def _consts(S, nseq):
    c = {}
    CA, SA, CB, SB = _rope_tables(S)
    c["rope"] = np.stack([CA, SA, CB, SB], axis=1).astype(np.float32)
    RA, RB = _rot_mats()
    ident = np.eye(128, dtype=np.float32)
    bd = np.zeros((128, 128), np.float32)
    bd[:64, :64] = 1.0
    bd[64:, 64:] = 1.0
    kj = np.arange(128)[:, None]
    qi = np.arange(128)[None, :]
    mlo = (kj >= qi).astype(np.float32)
    mhi = (kj <= qi).astype(np.float32)
    mask = np.concatenate([mlo, mhi] * 4, axis=1)
    BC, BP, BN = _band_mats()
    pack = np.concatenate([ident, RA, RB, bd, mask, BC.reshape(128, -1), BP.reshape(128, -1),
                           BN.reshape(128, -1)], axis=1)
    c["cpack"] = pack.astype(NPBF)
    sel = np.zeros((nseq, nseq, 128), np.float32)
    for s in range(nseq):
        sel[s, s, :] = 1.0
    c["sel"] = sel
    sel2 = np.zeros((2, 128), np.float32)
    sel2[0, :64] = 1.0
    sel2[1, 64:] = 1.0
    c["sel2"] = sel2
    return c


CP_IDENT, CP_RA, CP_RB, CP_BD, CP_MASK = 0, 128, 256, 384, 512
CP_BC = 512 + 1024
CP_BP = CP_BC + 4 * 3 * 128
CP_BN = CP_BP + 4 * 128
CP_END = CP_BN + 4 * 128


class Prog:
    def __init__(self, nseq, S, layers=(0, 1, 2, 3)):
        self.nseq, self.S, self.layers = nseq, S, tuple(layers)
        self.NT = S // 128
        self.NC = S // 512
        nc = bass.Bass("TRN2", target_bir_lowering=False)
        self.nc = nc
        dt = nc.dram_tensor
        self.x_in = dt("x", [nseq, S, D], F32, kind="ExternalInput")
        self.cT = dt("cT", [128, 8, nseq], F32, kind="ExternalInput")
        self.ada_w = dt("ada_w", [4, D, 3 * D], F32, kind="ExternalInput")
        self.ada_bT = dt("ada_bT", [128, 4, 16], F32, kind="ExternalInput")
        self.ada_bg = dt("ada_bg", [nseq, 4, D], F32, kind="ExternalInput")
        self.pre_T = dt("pre_T", [128, 4, 8], F32, kind="ExternalInput")
        self.post_rep = dt("post_rep", [nseq, 4, D], F32, kind="ExternalInput")
        self.attn_w_in = dt("attn_w_in", [2, D, 3328], F32, kind="ExternalInput")
        self.qk_g = dt("qk_g", [128, 2, 2], F32, kind="ExternalInput")
        self.attn_w_out = dt("attn_w_out", [2, D, D], F32, kind="ExternalInput")
        self.pool_w_in = dt("pool_w_in", [2, D, 2 * D], F32, kind="ExternalInput")
        self.pool_w_grp = dt("pool_w_grp", [2, 4, 256, 256], F32, kind="ExternalInput")
        self.pool_scT = dt("pool_scT", [128, 2, 8], F32, kind="ExternalInput")
        self.pool_w_out = dt("pool_w_out", [2, D, D], F32, kind="ExternalInput")
        self.rope = dt("rope", [128, 4, S], F32, kind="ExternalInput")
        self.cpack = dt("cpack", [128, CP_END], BF16, kind="ExternalInput")
        self.sel = dt("sel", [nseq, nseq, 128], F32, kind="ExternalInput")
        self.sel2 = dt("sel2", [2, 128], F32, kind="ExternalInput")
        self.y_out = dt("y", [nseq, S, D], F32, kind="ExternalOutput")
        self.xs = [dt("xs0", [nseq, S, D], F32, kind="Internal"), dt("xs1", [nseq, S, D], F32, kind="Internal")]
        self.qkaT = dt("qkaT", [nseq, 1024, S], BF16, kind="Internal")
        self.qkbT = dt("qkbT", [nseq, 640, S], BF16, kind="Internal")
        self.gateT = dt("gateT", [nseq, 1024, S], BF16, kind="Internal")
        self.va = dt("va", [nseq, S, 4, 192], BF16, kind="Internal")
        self.vb = dt("vb", [nseq, S, 2, 192], BF16, kind="Internal")
        self.yT = dt("yT", [nseq, 1024, S], BF16, kind="Internal")
        self.u = dt("u", [nseq, S, D], BF16, kind="Internal")
        self.build()

    def sb(self, st, name, shape, dtype):
        self._uid = getattr(self, "_uid", 0) + 1
        return st.enter_context(self.nc.sbuf_tensor("%s_%d" % (name, self._uid), list(shape), dtype))

    def ps(self, st, name, shape, dtype):
        self._uid = getattr(self, "_uid", 0) + 1
        return st.enter_context(self.nc.psum_tensor("%s_%d" % (name, self._uid), list(shape), dtype))

    def build(self):
        nc = self.nc
        with ExitStack() as st:
            k = KB(nc, st)
            self.k = k
            self.CP = self.sb(st, "cpk", [128, CP_END], BF16)
            self.SC = self.sb(st, "SC", [128, 4, self.nseq, 8], F32)
            self.SH = self.sb(st, "SH", [128, 4, self.nseq, 8], F32)
            self.GP = self.sb(st, "GP", [self.nseq, 4, D], F32)
            self.SEL = self.sb(st, "SEL", [self.nseq, self.nseq, 128], F32)
            self.SEL2 = self.sb(st, "SEL2", [2, 128], F32)
            self.QKG = self.sb(st, "QKG", [128, 2, 2], F32)
            self.PSC = self.sb(st, "PSC", [128, 2, 8], F32)
            self.EPS = self.sb(st, "EPSt", [128, 1], F32)
            tc = T("consts")
            k.op("sp", lambda e: e.dma_start(out=self.CP[:, :], in_=self.cpack[:, :]), writes=[tc], dma=True)
            k.op("sp", lambda e: e.dma_start(out=self.SEL[:, :, :], in_=self.sel[:, :, :]), writes=[tc], partial=True, dma=True)
            k.op("sp", lambda e: e.dma_start(out=self.QKG[:, :, :], in_=self.qk_g[:, :, :]), writes=[tc], partial=True, dma=True)
            k.op("sp", lambda e: e.dma_start(out=self.SEL2[:, :], in_=self.sel2[:, :]), writes=[tc], partial=True, dma=True)
            k.op("sp", lambda e: e.dma_start(out=self.PSC[:, :, :], in_=self.pool_scT[:, :, :]), writes=[tc], partial=True, dma=True)
            k.op("dve", lambda e: e.memset(self.EPS[:, :], EPS), writes=[tc], partial=True)
            self.prologue()
            k.barrier()
            xin = self.x_in
            for li, l in enumerate(self.layers):
                last = li == len(self.layers) - 1
                xout = self.y_out if last else self.xs[li % 2]
                if l % 2 == 0:
                    self.phase_P(l, xin)
                    k.barrier()
                    self.phase_A(l)
                    k.barrier()
                    self.phase_B(l)
                    k.barrier()
                    self.phase_O(l, xin, xout, attn=True)
                    k.barrier()
                else:
                    self.phase_U(l, xin)
                    k.barrier()
                    self.phase_O(l, xin, xout, attn=False)
                    k.barrier()
                xin = xout
            k.barrier()

    def prologue(self):
        k, nseq = self.k, self.nseq
        with ExitStack() as st:
            cT = self.sb(st, "cTs", [128, 8, nseq], F32)
            scT = self.sb(st, "scT", [128, 8, nseq], BF16)
            bT = self.sb(st, "bT", [128, 4, 16], F32)
            preT = self.sb(st, "preT", [128, 4, 8], F32)
            bg = self.sb(st, "bg", [nseq, 4, D], F32)
            pr = self.sb(st, "pr", [nseq, 4, D], F32)
            blk32 = self.sb(st, "blk32", [128, 2, 8, 512], F32)
            blk = self.sb(st, "blk", [128, 2, 8, 512], BF16)
            t_b32 = [T("b32_0"), T("b32_1")]
            modT = self.sb(st, "modT", [128, 16, nseq], F32)
            pm = self.ps(st, "pm", [128, 2, 512], F32)
            pg = self.ps(st, "pg", [128, 2, 512], F32)
            t_in = T("pin")
            k.op("sp", lambda e: e.dma_start(out=cT[:, :, :], in_=self.cT[:, :, :]), writes=[t_in], dma=True)
            for dst, src in ((bT, self.ada_bT), (preT, self.pre_T), (bg, self.ada_bg), (pr, self.post_rep)):
                k.op("sp", lambda e, dst=dst, src=src: e.dma_start(out=dst[:, :, :], in_=src[:, :, :]),
                     writes=[t_in], partial=True, dma=True)
            t_sc = T("scT")
            k.op("act", lambda e: e.activation(out=scT[:, :, :], in_=cT[:, :, :], func=AF.Silu),
                 reads=[t_in], writes=[t_sc])
            t_blk = [T("blk0"), T("blk1")]
            t_pm = [T("pm0"), T("pm1")]
            t_pg = T("pg")
            t_mod = T("modT")
            t_out = T("modout")
            nb_i = 0
            for l in self.layers:
                wv = self.ada_w[l].rearrange("(kc p) n -> p kc n", p=128)
                for nb in range(6):
                    b = nb_i % 2
                    nb_i += 1
                    k.op("sp", lambda e, b=b, nb=nb, wv=wv: e.dma_start(out=blk32[:, b, :, :], in_=wv[:, :, nb * 512:(nb + 1) * 512]),
                         writes=[t_b32[b]], dma=True)
                    k.op("dve", lambda e, b=b: e.tensor_copy(out=blk[:, b, 0:4, :], in_=blk32[:, b, 0:4, :]),
                         reads=[t_b32[b]], writes=[t_blk[b]])
                    k.op("act", lambda e, b=b: e.activation(out=blk[:, b, 4:8, :], in_=blk32[:, b, 4:8, :], func=AF.Copy),
                         reads=[t_b32[b]], writes=[t_blk[b]], partial=True)
                    if nb < 4:
                        pb = nb % 2
                        for j in range(4):
                            for kc in range(8):
                                k.op("pe", lambda e, b=b, j=j, kc=kc, pb=pb: e.matmul(
                                    pm[:, pb, j * nseq:(j + 1) * nseq], lhsT=blk[:, b, kc, j * 128:(j + 1) * 128],
                                    rhs=scT[:, kc, :], start=(kc == 0), stop=(kc == 7)),
                                    reads=[t_blk[b], t_sc], writes=[t_pm[pb]], signal=(kc == 7 and j == 3),
                                    partial=not (j == 0 and kc == 0))
                        k.op("dve", lambda e, nb=nb, pb=pb: e.tensor_copy(
                            out=modT[:, nb * 4:(nb + 1) * 4, :],
                            in_=pm[:, pb, 0:4 * nseq].rearrange("p (j s) -> p j s", s=nseq)),
                            reads=[t_pm[pb]], writes=[t_mod], partial=(nb > 0))
                    else:
                        half = nb - 4
                        for kc in range(8):
                            k.op("pe", lambda e, b=b, kc=kc, half=half: e.matmul(
                                pg[0:nseq, half, :], lhsT=scT[:, kc, :], rhs=blk[:, b, kc, :],
                                start=(kc == 0), stop=(kc == 7)),
                                reads=[t_blk[b], t_sc], writes=[t_pg], signal=(kc == 7),
                                partial=not (half == 0 and kc == 0))
                for s in range(nseq):
                    k.op("dve", lambda e, l=l, s=s: e.tensor_tensor(
                        out=self.SH[:, l, s, :], in0=modT[:, 0:8, s], in1=bT[:, l, 0:8], op=ALU.add),
                        reads=[t_mod, t_in], writes=[t_out], partial=True)
                    k.op("dve", lambda e, l=l, s=s: e.tensor_tensor(
                        out=self.SC[:, l, s, :], in0=modT[:, 8:16, s], in1=bT[:, l, 8:16], op=ALU.add),
                        reads=[t_mod, t_in], writes=[t_out], partial=True)
                    k.op("dve", lambda e, l=l, s=s: e.scalar_tensor_tensor(
                        out=self.SC[:, l, s, :], in0=self.SC[:, l, s, :], scalar=1.0, in1=preT[:, l, :],
                        op0=ALU.add, op1=ALU.mult),
                        reads=[t_out, t_in], writes=[t_out], partial=True)
                k.op("dve", lambda e, l=l: e.tensor_tensor(
                    out=self.GP[:, l, :], in0=pg[0:nseq, :, :].rearrange("p a b -> p (a b)"), in1=bg[:, l, :], op=ALU.add),
                    reads=[t_pg, t_in], writes=[t_out], partial=True)
                k.op("dve", lambda e, l=l: e.tensor_tensor(
                    out=self.GP[:, l, :], in0=self.GP[:, l, :], in1=pr[:, l, :], op=ALU.mult),
                    reads=[t_out, t_in], writes=[t_out], partial=True)
            k.barrier()

    def alloc_front(self, st):
        f = {}
        f["xt"] = self.sb(st, "f_xt", [128, 2, 4, D], F32)
        f["t_xt"] = [T("xt0"), T("xt1")]
        f["junk"] = self.sb(st, "f_junk", [128, D], BF16)
        f["t_junk"] = T("junk")
        f["ss"] = self.sb(st, "f_ss", [128, 2, 4], F32)
        f["sd"] = self.sb(st, "f_sd", [128, 2, 4], F32)
        f["rs"] = self.sb(st, "f_rs", [128, 2, 4], F32)
        f["t_ss"] = [T("ss0"), T("ss1")]
        f["t_sd"] = [T("sd0"), T("sd1")]
        f["t_rs"] = [T("rs0"), T("rs1")]
        f["xn"] = self.sb(st, "f_xn", [128, 4, D], BF16)
        f["t_xn"] = [T("xn%d" % i) for i in range(4)]
        f["hT"] = self.sb(st, "f_hT", [128, 2, 8, 512], BF16)
        f["t_hT"] = [T("hT0"), T("hT1")]
        f["ptr"] = self.ps(st, "f_ptr", [128, 2, 1024], BF16)
        f["t_ptr"] = [T("ptr0"), T("ptr1")]
        f["n"] = 0
        return f

    def load_x(self, f, xin, s, c, idx):
        b = idx % 2
        xv = xin[s].rearrange("(i p) d -> p i d", p=128)
        self.k.op("sp", lambda e: e.dma_start(out=f["xt"][:, b, :, :], in_=xv[:, 4 * c:4 * c + 4, :]),
                  writes=[f["t_xt"][b]], dma=True)

    def front_a(self, f, idx):
        k = self.k
        b = idx % 2
        xt = f["xt"]
        for i in range(4):
            k.op("act", lambda e, i=i: e.activation(out=f["junk"][:, :], in_=xt[:, b, i, :], func=AF.Square,
                                                     accum_out=f["ss"][:, b, i:i + 1]),
                 reads=[f["t_xt"][b]], writes=[f["t_junk"], f["t_ss"][b]], partial=(i > 0))
        k.op("act", lambda e: e.activation(out=f["sd"][:, b, :], in_=f["ss"][:, b, :], func=AF.Sqrt,
                                           scale=1.0 / D, bias=self.EPS[:, 0:1]),
             reads=[f["t_ss"][b]], writes=[f["t_sd"][b]])
        k.op("dve", lambda e: e.reciprocal(out=f["rs"][:, b, :], in_=f["sd"][:, b, :]),
             reads=[f["t_sd"][b]], writes=[f["t_rs"][b]])
        for i in range(4):
            k.op("dve", lambda e, i=i: e.tensor_scalar(out=f["xn"][:, i, :], in0=xt[:, b, i, :],
                                                        scalar1=f["rs"][:, b, i:i + 1], scalar2=None, op0=ALU.mult),
                 reads=[f["t_xt"][b], f["t_rs"][b]], writes=[f["t_xn"][i]])

    def front_b(self, f, l, s, idx):
        k = self.k
        b = idx % 2
        ident = self.CP[:, CP_IDENT:CP_IDENT + 128]
        for fc in range(8):
            pb = f["n"] % 2
            f["n"] += 1
            for i in range(4):
                k.op("pe", lambda e, i=i, fc=fc, pb=pb: e.transpose(
                    out=f["ptr"][:, pb, i * 128:(i + 1) * 128], in_=f["xn"][:, i, fc * 128:(fc + 1) * 128], identity=ident),
                    reads=[f["t_xn"][i]], writes=[f["t_ptr"][pb]], signal=(i == 3), partial=(i > 0))
            k.op("act", lambda e, fc=fc, pb=pb: e.activation(
                out=f["hT"][:, b, fc, :], in_=f["ptr"][:, pb, 0:512], func=AF.Identity,
                scale=self.SC[:, l, s, fc:fc + 1], bias=self.SH[:, l, s, fc:fc + 1]),
                reads=[f["t_ptr"][pb]], writes=[f["t_hT"][b]], partial=(fc > 0))
        return b

    def load_w(self, dst, src_view, t):
        self.k.op("pool", lambda e: e.dma_start(out=dst, in_=src_view), writes=[t], dma=True, partial=True)

    def run_skewed(self, tasks):
        maxage = max(len(t) for t in tasks)
        for n in range(len(tasks) + maxage):
            for age in range(maxage - 1, -1, -1):
                m = n - age
                if 0 <= m < len(tasks) and age < len(tasks[m]) and tasks[m][age] is not None:
                    tasks[m][age]()

    def phase_P(self, l, xin):
        k, nseq = self.k, self.nseq
        ai = l // 2
        with ExitStack() as st:
            f = self.alloc_front(st)
            W = self.sb(st, "P_W", [128, 8, 3328], BF16)
            t_W = T("W")
            wv = self.attn_w_in[ai].rearrange("(kc p) n -> p kc n", p=128)
            for kc in range(8):
                self.load_w(W[:, kc, :], wv[:, kc, :], t_W)
            rp = self.sb(st, "P_rope", [128, 2, 4, 512], F32)
            t_rp = [T("rp0"), T("rp1")]
            NPJ = 4
            pj = self.ps(st, "P_pj", [128, NPJ, 512], F32)
            t_pj = [T("pj%d" % i) for i in range(NPJ)]
            px = self.ps(st, "P_px", [128, 2, 512], F32)
            t_px = [T("px0"), T("px1")]
            NSTG = 6
            stg = self.sb(st, "P_stg", [128, NSTG, 512], BF16)
            t_stg = [T("stg%d" % i) for i in range(NSTG)]
            NR = 3
            q16 = self.sb(st, "P_q16", [128, NR, 512], BF16)
            sq = self.sb(st, "P_sq", [128, NR, 512], BF16)
            sdv = self.sb(st, "P_sdv", [128, NR, 512], F32)
            rsv = self.sb(st, "P_rsv", [128, NR, 512], F32)
            t1 = self.sb(st, "P_t1", [128, NR, 512], F32)
            t2 = self.sb(st, "P_t2", [128, NR, 512], F32)
            t_q16 = [T("q16_%d" % i) for i in range(NR)]
            t_sq = [T("sq%d" % i) for i in range(NR)]
            t_sdv = [T("sdv%d" % i) for i in range(NR)]
            t_rsv = [T("rsv%d" % i) for i in range(NR)]
            t_t1 = [T("t1_%d" % i) for i in range(NR)]
            t_t2 = [T("t2_%d" % i) for i in range(NR)]
            vst = self.sb(st, "P_vst", [128, 2, 4, 3, 64], BF16)
            vbst = self.sb(st, "P_vbst", [128, 2, 2, 3, 64], BF16)
            t_vst = [T("vst0"), T("vst1")]
            t_vbst = [T("vbst0"), T("vbst1")]
            for b in range(2):
                k.op("pool", lambda e, b=b: e.memset(vst[:, b, :, :, :], 1.0), writes=[t_vst[b]])
                k.op("pool", lambda e, b=b: e.memset(vbst[:, b, :, :, :], 1.0), writes=[t_vbst[b]])
            RA = self.CP[:, CP_RA:CP_RA + 128]
            RB = self.CP[:, CP_RB:CP_RB + 128]
            BD = self.CP[:, CP_BD:CP_BD + 128]
            ident = self.CP[:, CP_IDENT:CP_IDENT + 128]
            cnt = {"pj": 0, "stg": 0, "a": 0, "v": 0}
            items = [(s, c) for s in range(nseq) for c in range(self.NC)]
            hT = f["hT"]

            def new_pj():
                j = cnt["pj"] % NPJ
                cnt["pj"] += 1
                return j

            def new_stg():
                si = cnt["stg"] % NSTG
                cnt["stg"] += 1
                return si

            def projT(j, col0, b):
                for kc in range(8):
                    k.op("pe", lambda e, kc=kc: e.matmul(pj[:, j, :], lhsT=W[:, kc, col0:col0 + 128],
                                                          rhs=hT[:, b, kc, :], start=(kc == 0), stop=(kc == 7)),
                         reads=[t_W, f["t_hT"][b]], writes=[t_pj[j]], signal=(kc == 7), partial=(kc > 0))

            def store(si, dram_ap):
                k.op("sp", lambda e: e.dma_start(out=dram_ap, in_=stg[:, si, :]), reads=[t_stg[si]], dma=True)

            def gate_task(s, c, b, fcg):
                stt_ = {}

                def s0():
                    stt_["j"] = new_pj()
                    projT(stt_["j"], 2304 + fcg * 128, b)

                def s1():
                    j, si = stt_["j"], new_stg()
                    k.op("act", lambda e: e.activation(out=stg[:, si, :], in_=pj[:, j, :], func=AF.Silu),
                         reads=[t_pj[j]], writes=[t_stg[si]])
                    store(si, self.gateT[s, fcg * 128:(fcg + 1) * 128, c * 512:(c + 1) * 512])
                return [s0, s1]

            def fb_task(idx2, fc):
                s2 = items[idx2][0]
                b2 = idx2 % 2
                stt_ = {}

                def s0():
                    pb = f["n"] % 2
                    f["n"] += 1
                    stt_["pb"] = pb
                    for i in range(4):
                        k.op("pe", lambda e, i=i: e.transpose(
                            out=f["ptr"][:, pb, i * 128:(i + 1) * 128], in_=f["xn"][:, i, fc * 128:(fc + 1) * 128], identity=ident),
                            reads=[f["t_xn"][i]], writes=[f["t_ptr"][pb]], signal=(i == 3), partial=(i > 0))

                def s1():
                    pb = stt_["pb"]
                    k.op("act", lambda e: e.activation(
                        out=hT[:, b2, fc, :], in_=f["ptr"][:, pb, 0:512], func=AF.Identity,
                        scale=self.SC[:, l, s2, fc:fc + 1], bias=self.SH[:, l, s2, fc:fc + 1]),
                        reads=[f["t_ptr"][pb]], writes=[f["t_hT"][b2]], partial=(fc > 0))
                return [s0, s1]

            def rope_stages(stt_, b, R, ctab, stab, dram_ap):
                def rot():
                    a = stt_["a"]
                    k.op("pe", lambda e: e.matmul(px[:, 1, :], lhsT=R, rhs=q16[:, a, :], start=True, stop=True),
                         reads=[t_q16[a]], writes=[t_px[1]])
                    k.op("pool", lambda e: e.tensor_tensor(out=t1[:, a, :], in0=q16[:, a, :], in1=rp[:, b, ctab, :], op=ALU.mult),
                         reads=[t_q16[a], t_rp[b]], writes=[t_t1[a]])

                def comb():
                    a = stt_["a"]
                    k.op("dve", lambda e: e.tensor_tensor(out=t2[:, a, :], in0=px[:, 1, :], in1=rp[:, b, stab, :], op=ALU.mult),
                         reads=[t_px[1], t_rp[b]], writes=[t_t2[a]])
                    si = new_stg()
                    k.op("pool", lambda e: e.tensor_tensor(out=stg[:, si, :], in0=t1[:, a, :], in1=t2[:, a, :], op=ALU.add),
                         reads=[t_t1[a], t_t2[a]], writes=[t_stg[si]])
                    store(si, dram_ap)
                return rot, comb

            def a_task(s, c, b, tq):
                stt_ = {}

                def s0():
                    stt_["j"] = new_pj()
                    projT(stt_["j"], tq * 128, b)

                def s1():
                    j = stt_["j"]
                    a = cnt["a"] % NR
                    cnt["a"] += 1
                    stt_["a"] = a
                    k.op("act", lambda e: e.activation(out=q16[:, a, :], in_=pj[:, j, :], func=AF.Copy),
                         reads=[t_pj[j]], writes=[t_q16[a]])
                rot, comb = rope_stages(stt_, b, RA, 0, 1, self.qkaT[s, tq * 128:(tq + 1) * 128, c * 512:(c + 1) * 512])
                return [s0, s1, rot, comb]

            def b_task(s, c, b, tq):
                stt_ = {}
                gi = 0 if tq < 4 else 1

                def s0():
                    stt_["j"] = new_pj()
                    projT(stt_["j"], 1536 + tq * 128, b)

                def s1():
                    j = stt_["j"]
                    a = cnt["a"] % NR
                    cnt["a"] += 1
                    stt_["a"] = a
                    k.op("act", lambda e: e.activation(out=sq[:, a, :], in_=pj[:, j, :], func=AF.Square),
                         reads=[t_pj[j]], writes=[t_sq[a]])
                    k.op("pe", lambda e: e.matmul(px[:, 0, :], lhsT=BD, rhs=sq[:, a, :], start=True, stop=True),
                         reads=[t_sq[a]], writes=[t_px[0]])

                def s2():
                    j, a = stt_["j"], stt_["a"]
                    k.op("act", lambda e: e.activation(out=sdv[:, a, :], in_=px[:, 0, :], func=AF.Sqrt,
                                                       scale=1.0 / HD, bias=self.EPS[:, 0:1]),
                         reads=[t_px[0]], writes=[t_sdv[a]])
                    k.op("dve", lambda e: e.reciprocal(out=rsv[:, a, :], in_=sdv[:, a, :]),
                         reads=[t_sdv[a]], writes=[t_rsv[a]])
                    k.op("dve", lambda e: e.scalar_tensor_tensor(
                        out=q16[:, a, :], in0=pj[:, j, :], scalar=self.QKG[:, ai, gi:gi + 1], in1=rsv[:, a, :],
                        op0=ALU.mult, op1=ALU.mult),
                        reads=[t_pj[j], t_rsv[a]], writes=[t_q16[a]])
                rot, comb = rope_stages(stt_, b, RB, 2, 3, self.qkbT[s, tq * 128:(tq + 1) * 128, c * 512:(c + 1) * 512])
                return [s0, s1, s2, rot, comb]

            def v_task(s, c, b, i):
                stt_ = {}

                def s0():
                    j, j2 = new_pj(), new_pj()
                    stt_["j"], stt_["j2"] = j, j2
                    for kc in range(8):
                        k.op("pe", lambda e, kc=kc: e.matmul(
                            pj[:, j, :], lhsT=hT[:, b, kc, i * 128:(i + 1) * 128], rhs=W[:, kc, 1024:1536],
                            start=(kc == 0), stop=(kc == 7)),
                            reads=[t_W, f["t_hT"][b]], writes=[t_pj[j]], signal=(kc == 7), partial=(kc > 0))
                    for kc in range(8):
                        k.op("pe", lambda e, kc=kc: e.matmul(
                            pj[:, j2, 0:128], lhsT=hT[:, b, kc, i * 128:(i + 1) * 128], rhs=W[:, kc, 2176:2304],
                            start=(kc == 0), stop=(kc == 7)),
                            reads=[t_W, f["t_hT"][b]], writes=[t_pj[j2]], signal=(kc == 7), partial=(kc > 0))

                def s1():
                    j, j2 = stt_["j"], stt_["j2"]
                    vbuf = cnt["v"] % 2
                    cnt["v"] += 1
                    k.op("dve", lambda e: e.tensor_copy(
                        out=vst[:, vbuf, :, 0:3:2, :], in_=pj[:, j, :].rearrange("p (h t d) -> p h t d", h=4, t=2)),
                        reads=[t_pj[j]], writes=[t_vst[vbuf]])
                    k.op("dve", lambda e: e.tensor_copy(
                        out=vbst[:, vbuf, :, 1, :], in_=pj[:, j2, 0:128].rearrange("p (h d) -> p h d", h=2)),
                        reads=[t_pj[j2]], writes=[t_vbst[vbuf]])
                    tok0 = c * 512 + i * 128
                    k.op("sp", lambda e: e.dma_start(
                        out=self.va[s, tok0:tok0 + 128, :, :].rearrange("t h c -> t (h c)"),
                        in_=vst[:, vbuf, :, :, :].rearrange("p h a d -> p (h a d)")),
                        reads=[t_vst[vbuf]], dma=True)
                    k.op("sp", lambda e: e.dma_start(
                        out=self.vb[s, tok0:tok0 + 128, :, :].rearrange("t h c -> t (h c)"),
                        in_=vbst[:, vbuf, :, :, :].rearrange("p h a d -> p (h a d)")),
                        reads=[t_vbst[vbuf]], dma=True)
                return [s0, s1]

            def prefetch_r(idx):
                c = items[idx][1]
                b = idx % 2
                k.op("sp", lambda e: e.dma_start(out=rp[:, b, :, :], in_=self.rope[:, :, c * 512:(c + 1) * 512]),
                     writes=[t_rp[b]], dma=True)

            def head_task(idx):
                def s0():
                    if idx + 2 < len(items):
                        self.load_x(f, xin, items[idx + 2][0], items[idx + 2][1], idx + 2)
                    if idx + 1 < len(items):
                        prefetch_r(idx + 1)
                        self.front_a(f, idx + 1)
                return [s0]

            self.load_x(f, xin, items[0][0], items[0][1], 0)
            prefetch_r(0)
            if len(items) > 1:
                self.load_x(f, xin, items[1][0], items[1][1], 1)
            self.front_a(f, 0)
            self.front_b(f, l, items[0][0], 0)
            tasks = []
            for idx, (s, c) in enumerate(items):
                b = idx % 2
                tasks.append(head_task(idx))
                nxt = idx + 1 < len(items)
                fbq = [fb_task(idx + 1, fc) for fc in range(8)] if nxt else []
                for fcg in range(8):
                    tasks.append(gate_task(s, c, b, fcg))
                    if fcg >= 4 and fbq:
                        tasks.append(fbq.pop(0))
                        tasks.append(fbq.pop(0))
                for tq in range(8):
                    tasks.append(a_task(s, c, b, tq))
                for tq in range(5):
                    tasks.append(b_task(s, c, b, tq))
                for i in range(4):
                    tasks.append(v_task(s, c, b, i))
            self.run_skewed(tasks)

    def alloc_fin(self, st):
        g = {}
        g["R"] = self.sb(st, "fin_R", [128, 2, 512], F32)
        g["RS"] = self.sb(st, "fin_RS", [128, 2, 512], F32)
        g["tg"] = self.sb(st, "fin_tg", [128, 2, 512], F32)
        g["yst"] = self.sb(st, "fin_y", [128, 2, 512], BF16)
        for nm in ("R", "RS", "tg", "yst"):
            g["t_" + nm] = [T(nm + "0"), T(nm + "1")]
        g["n"] = 0
        return g

    def finalize(self, g, srcA, srcB, t_srcs, Gap, t_G, dram_ap):
        k = self.k
        b = g["n"] % 2
        g["n"] += 1
        R, RS, tg, yst = g["R"], g["RS"], g["tg"], g["yst"]
        k.op("dve", lambda e: e.reciprocal(out=R[0:64, b, :], in_=srcB[0:64, :]), reads=t_srcs, writes=[g["t_R"][b]])
        k.op("dve", lambda e: e.reciprocal(out=R[64:128, b, :], in_=srcA[64:128, :]), reads=t_srcs,
             writes=[g["t_R"][b]], partial=True)
        k.op("sp", lambda e: e.dma_start(out=RS[0:64, b, :], in_=R[64:128, b, :]), reads=[g["t_R"][b]],
             writes=[g["t_RS"][b]], dma=True)
        k.op("sp", lambda e: e.dma_start(out=RS[64:128, b, :], in_=R[0:64, b, :]), reads=[g["t_R"][b]],
             writes=[g["t_RS"][b]], dma=True, partial=True)
        k.op("dve", lambda e: e.tensor_tensor(out=tg[0:64, b, :], in0=srcA[0:64, :], in1=Gap[0:64, :], op=ALU.mult),
             reads=t_srcs + [t_G], writes=[g["t_tg"][b]])
        k.op("dve", lambda e: e.tensor_tensor(out=tg[64:128, b, :], in0=srcB[64:128, :], in1=Gap[64:128, :], op=ALU.mult),
             reads=t_srcs + [t_G], writes=[g["t_tg"][b]], partial=True)
        k.op("pool", lambda e: e.tensor_tensor(out=yst[:, b, :], in0=tg[:, b, :], in1=RS[:, b, :], op=ALU.mult),
             reads=[g["t_tg"][b], g["t_RS"][b]], writes=[g["t_yst"][b]])
        k.op("sp", lambda e: e.dma_start(out=dram_ap, in_=yst[:, b, :]), reads=[g["t_yst"][b]], dma=True)

    def phase_A(self, l):
        k, nseq, S = self.k, self.nseq, self.S
        with ExitStack() as st:
            QT = self.sb(st, "A_QT", [128, 2, S], BF16)
            KT = self.sb(st, "A_KT", [128, 2, S], BF16)
            G = self.sb(st, "A_G", [128, S], BF16)
            t_QT = [T("QT0"), T("QT1")]
            t_KT = [T("KT0"), T("KT1")]
            t_G = T("G")
            NVB = 3
            VD = self.sb(st, "A_VD", [128, NVB, self.NT, 192], BF16)
            t_VD = [T("VD%d" % i) for i in range(NVB)]
            OA = self.sb(st, "A_OA", [128, 2, 2, S], F32)
            t_OA = [[T("OA00"), T("OA01")], [T("OA10"), T("OA11")]]
            DT = self.sb(st, "A_DT", [128, S // 64], F32)
            RT = self.sb(st, "A_RT", [128, S // 64], F32)
            RROW = self.sb(st, "A_RROW", [2, S], F32)
            t_DT, t_RT, t_RROW = T("DT"), T("RT"), T("RROW")
            tg = self.sb(st, "A_tg", [128, 2, 512], F32)
            yst = self.sb(st, "A_yst", [128, 2, 512], BF16)
            t_tg = [T("tg0"), T("tg1")]
            t_yst = [T("yst0"), T("yst1")]
            pR = self.ps(st, "A_pR", [128, 512], F32)
            t_pR = T("pR")
            pending = []
            fin_n = [0]
            NPS = 5
            pS = self.ps(st, "A_pS", [128, NPS, 512], F32)
            t_pS = [T("pS%d" % i) for i in range(NPS)]
            pO = self.ps(st, "A_pO", [128, 2, 512], F32)
            t_pO = [T("pO0"), T("pO1")]
            NPT = 6
            pT = self.sb(st, "A_pT", [128, NPT, 512], BF16)
            pTm = self.sb(st, "A_pTm", [128, NPT, 512], BF16)
            t_pT = [T("pT%d" % i) for i in range(NPT)]
            t_pTm = [T("pTm%d" % i) for i in range(NPT)]
            MASK = self.CP[:, CP_MASK:CP_MASK + 512]
            LA = 3
            for i in range(NPS):
                k.op("dve", lambda e, i=i: e.memset(pS[:, i, :], 0.0), writes=[t_pS[i]])
            items = [(s, hp) for s in range(nseq) for hp in range(4)]

            def load_qk(idx):
                s, hp = items[idx]
                b = idx % 2
                k.op("sp", lambda e: e.dma_start(out=QT[:, b, :], in_=self.qkaT[s, hp * 128:(hp + 1) * 128, :]),
                     writes=[t_QT[b]], dma=True)
                k.op("sp", lambda e: e.dma_start(out=KT[:, b, :], in_=self.qkaT[s, 512 + hp * 128:512 + (hp + 1) * 128, :]),
                     writes=[t_KT[b]], dma=True)

            vloads = [(idx, d) for idx in range(len(items)) for d in PATTERNS]

            def load_v(vi):
                idx, d = vloads[vi]
                s, hp = items[idx]
                vb_ = vi % NVB
                nt = (S // d) // 128
                vv = self.va[s, :, hp, :].rearrange("(m p r) c -> r p m c", p=128, r=d)
                for r in range(d):
                    k.op("sp", lambda e, r=r: e.dma_start(out=VD[:, vb_, r * nt:(r + 1) * nt, :], in_=vv[r]),
                         writes=[t_VD[vb_]], dma=True, partial=(r > 0))

            groups = []
            vi = 0
            for idx, (s, hp) in enumerate(items):
                b = idx % 2
                first_of_item = True
                for d in PATTERNS:
                    vb_ = vi % NVB
                    first_of_pat = True
                    L = S // d
                    nt = L // 128
                    for hh in range(2):
                        qtiles = [(r, m) for r in range(d) for m in range(nt + 1)]
                        for g0 in range(0, len(qtiles), 2):
                            gd = dict(idx=idx, s=s, hp=hp, b=b, d=d, hh=hh, vb=vb_, nt=nt, L=L,
                                      tiles=qtiles[g0:g0 + 2], pre=[], last=False, n=len(groups))
                            if first_of_pat and vi + 2 < len(vloads):
                                gd["pre"].append(("v", vi + 2))
                            first_of_item = False
                            first_of_pat = False
                            groups.append(gd)
                    vi += 1
                groups[-1]["last"] = True
            oa_first = {}

            def plan(gd):
                mm = []
                segs = []
                for ti, (r, m) in enumerate(gd["tiles"]):
                    jlo = max(128 * m - 64, 0)
                    jhi = min(128 * m + 64, gd["L"])
                    n = jhi - jlo
                    qoff = jlo + 64 - 128 * m
                    if segs and segs[-1][0] == r and segs[-1][1] + segs[-1][2] == jlo:
                        segs[-1][2] += n
                    else:
                        segs.append([r, jlo, n, ti * 128 + qoff])
                    for kind, mk in ((0, m - 1), (1, m)):
                        if 0 <= mk < gd["nt"]:
                            mm.append((ti, kind, mk, jlo, n, qoff, r))
                return mm, segs

            def emit_qk(gd):
                mm, _ = plan(gd)
                b, d, hh = gd["b"], gd["d"], gd["hh"]
                sb_ = gd["n"] % NPS
                rows = slice(hh * 64, (hh + 1) * 64)
                for ii, (ti, kind, mk, jlo, n, qoff, r) in enumerate(mm):
                    c0 = (2 * ti + kind) * 128 + qoff
                    k.op("pe", lambda e, mk=mk, jlo=jlo, n=n, c0=c0, r=r: e.matmul(
                        pS[:, sb_, c0:c0 + n], lhsT=KT[rows, b, sl(128 * mk * d + r, 128, d)],
                        rhs=QT[rows, b, sl(jlo * d + r, n, d)], start=True, stop=True),
                        reads=[t_KT[b], t_QT[b]], writes=[t_pS[sb_]], signal=(ii == len(mm) - 1), partial=(ii > 0))

            def emit_em(gd):
                sb_ = gd["n"] % NPS
                pb = gd["n"] % NPT
                k.op("act", lambda e: e.activation(out=pT[:, pb, :], in_=pS[:, sb_, :], func=AF.Exp, scale=0.125),
                     reads=[t_pS[sb_]], writes=[t_pT[pb]])
                meng = "dve" if gd["n"] % 2 == 0 else "pool"
                k.op(meng, lambda e: e.tensor_tensor(out=pTm[:, pb, :], in0=pT[:, pb, :], in1=MASK, op=ALU.mult),
                     reads=[t_pT[pb]], writes=[t_pTm[pb]])

            def emit_pv(gd):
                for what, arg in gd["pre"]:
                    if what == "qk":
                        load_qk(arg)
                    else:
                        load_v(arg)
                mm, segs = plan(gd)
                b, d, hh, vb_, nt = gd["b"], gd["d"], gd["hh"], gd["vb"], gd["nt"]
                pb = gd["n"] % NPT
                ob = gd["n"] % 2
                vcol = slice(0, 128) if hh == 0 else slice(64, 192)
                for ii, (ti, kind, mk, jlo, n, qoff, r) in enumerate(mm):
                    c0 = (2 * ti + kind) * 128 + qoff
                    first = (ii == 0) or (mm[ii - 1][0] != ti)
                    lastk = (ii == len(mm) - 1) or (mm[ii + 1][0] != ti)
                    k.op("pe", lambda e, mk=mk, n=n, c0=c0, ti=ti, qoff=qoff, first=first, lastk=lastk, r=r: e.matmul(
                        pO[:, ob, ti * 128 + qoff:ti * 128 + qoff + n],
                        lhsT=VD[:, vb_, r * nt + mk, vcol], rhs=pTm[:, pb, c0:c0 + n], start=first, stop=lastk),
                        reads=[t_VD[vb_], t_pTm[pb]], writes=[t_pO[ob]], signal=(ii == len(mm) - 1), partial=(ii > 0))
                ib = gd["idx"] % 2
                for (r, jlo, n, po0) in segs:
                    dst = OA[:, ib, hh, sl(jlo * d + r, n, d)]
                    if d == PATTERNS[0]:
                        fk = (gd["idx"], hh)
                        k.op("dve", lambda e, dst=dst, po0=po0, n=n: e.tensor_copy(out=dst, in_=pO[:, ob, po0:po0 + n]),
                             reads=[t_pO[ob]], writes=[t_OA[ib][hh]], partial=(fk in oa_first))
                        oa_first[fk] = True
                    else:
                        k.op("dve", lambda e, dst=dst, po0=po0, n=n: e.tensor_tensor(
                            out=dst, in0=pO[:, ob, po0:po0 + n], in1=dst, op=ALU.add),
                            reads=[t_pO[ob], t_OA[ib][hh]], writes=[t_OA[ib][hh]], partial=True)
                if gd["last"]:
                    queue_finalize(gd)
                elif pending:
                    fn = pending.pop(0)
                    if fn is not None:
                        fn()

            def queue_finalize(gd):
                s, hp, idx, b = gd["s"], gd["hp"], gd["idx"], gd["b"]
                ib = idx % 2
                tA, tB = t_OA[ib][0], t_OA[ib][1]
                k.op("sp", lambda e: e.dma_start(out=DT[0:64, :], in_=OA[64:65, ib, 0, :].rearrange("p (a b) -> p a b", b=S // 64)),
                     reads=[tA], writes=[t_DT], dma=True)
                k.op("sp", lambda e: e.dma_start(out=DT[64:128, :], in_=OA[0:1, ib, 1, :].rearrange("p (a b) -> p a b", b=S // 64)),
                     reads=[tB], writes=[t_DT], dma=True, partial=True)

                def stage_b():
                    k.op("sp", lambda e: e.dma_start(out=G[:, :], in_=self.gateT[s, hp * 128:(hp + 1) * 128, :]),
                         writes=[t_G], dma=True)
                    k.op("dve", lambda e: e.reciprocal(out=RT[:, :], in_=DT[:, :]), reads=[t_DT], writes=[t_RT])
                    k.op("sp", lambda e: e.dma_start(out=RROW[0:1, :].rearrange("p (a b) -> p a b", b=S // 64), in_=RT[0:64, :]),
                         reads=[t_RT], writes=[t_RROW], dma=True)
                    k.op("sp", lambda e: e.dma_start(out=RROW[1:2, :].rearrange("p (a b) -> p a b", b=S // 64), in_=RT[64:128, :]),
                         reads=[t_RT], writes=[t_RROW], dma=True, partial=True)

                def stage_c(c):
                    def run():
                        cols = slice(c * 512, (c + 1) * 512)
                        fb = fin_n[0] % 2
                        fin_n[0] += 1
                        k.op("pe", lambda e: e.matmul(pR[:, :], lhsT=self.SEL2[:, :], rhs=RROW[:, cols], start=True, stop=True),
                             reads=[t_RROW], writes=[t_pR])
                        k.op("dve", lambda e: e.tensor_tensor(out=tg[0:64, fb, :], in0=OA[0:64, ib, 0, cols], in1=G[0:64, cols],
                                                              op=ALU.mult),
                             reads=[tA, t_G], writes=[t_tg[fb]])
                        k.op("dve", lambda e: e.tensor_tensor(out=tg[64:128, fb, :], in0=OA[64:128, ib, 1, cols],
                                                              in1=G[64:128, cols], op=ALU.mult),
                             reads=[tB, t_G], writes=[t_tg[fb]], partial=True)
                        k.op("dve", lambda e: e.tensor_tensor(out=yst[:, fb, :], in0=tg[:, fb, :], in1=pR[:, :], op=ALU.mult),
                             reads=[t_tg[fb], t_pR], writes=[t_yst[fb]])
                        k.op("sp", lambda e: e.dma_start(out=self.yT[s, hp * 128:(hp + 1) * 128, cols], in_=yst[:, fb, :]),
                             reads=[t_yst[fb]], dma=True)
                    return run
                pending.append(None)
                pending.append(stage_b)
                pending.append(None)
                for c in range(self.NC):
                    pending.append(stage_c(c))
                if idx + 2 < len(items):
                    pending.append(lambda: load_qk(idx + 2))

            load_qk(0)
            if len(items) > 1:
                load_qk(1)
            load_v(0)
            load_v(1)
            for i in range(-LA, len(groups)):
                if 0 <= i + LA < len(groups):
                    emit_qk(groups[i + LA])
                if 0 <= i + 2 < len(groups):
                    emit_em(groups[i + 2])
                if 0 <= i:
                    emit_pv(groups[i])
            while pending:
                fn = pending.pop(0)
                if fn is not None:
                    fn()

    def phase_B(self, l):
        k, nseq, S = self.k, self.nseq, self.S
        with ExitStack() as st:
            g = self.alloc_fin(st)
            QT = self.sb(st, "B_QT", [128, 2, S], BF16)
            KT = self.sb(st, "B_KT", [128, 2, S], BF16)
            G = self.sb(st, "B_G", [128, 2, S], BF16)
            V1 = self.sb(st, "B_V1", [128, 2, self.NT, 192], BF16)
            t_QT = [T("QT0"), T("QT1")]
            t_KT = [T("KT0"), T("KT1")]
            t_G = [T("G0"), T("G1")]
            t_V1 = [T("V10"), T("V11")]
            pS = self.ps(st, "B_pS", [128, 2, 1024], F32)
            t_pS = [T("pS0"), T("pS1")]
            pO = self.ps(st, "B_pO", [128, 2, 2, 512], F32)
            t_pO = [T("pO0"), T("pO1")]
            NPT = 3
            pT = self.sb(st, "B_pT", [128, NPT, 1024], BF16)
            t_pT = [T("pT%d" % i) for i in range(NPT)]
            items = [(s, hp) for s in range(nseq) for hp in range(4)]

            def load(idx):
                s, hp = items[idx]
                b = idx % 2
                kv = hp // 2
                k.op("sp", lambda e: e.dma_start(out=QT[:, b, :], in_=self.qkbT[s, hp * 128:(hp + 1) * 128, :]),
                     writes=[t_QT[b]], dma=True)
                k.op("sp", lambda e: e.dma_start(out=KT[0:64, b, :], in_=self.qkbT[s, 512 + kv * 64:512 + (kv + 1) * 64, :]),
                     writes=[t_KT[b]], dma=True)
                k.op("sp", lambda e: e.dma_start(out=KT[64:128, b, :], in_=self.qkbT[s, 512 + kv * 64:512 + (kv + 1) * 64, :]),
                     writes=[t_KT[b]], dma=True, partial=True)
                k.op("sp", lambda e: e.dma_start(out=G[:, b, :], in_=self.gateT[s, 512 + hp * 128:512 + (hp + 1) * 128, :]),
                     writes=[t_G[b]], dma=True)
                k.op("sp", lambda e: e.dma_start(out=V1[:, b, :, :],
                                                 in_=self.vb[s, :, kv, :].rearrange("(m p) c -> p m c", p=128)),
                     writes=[t_V1[b]], dma=True)

            steps = []
            for idx, (s, hp) in enumerate(items):
                for qc in range(self.NC):
                    for kt in range(self.NT):
                        steps.append(dict(idx=idx, s=s, hp=hp, b=idx % 2, qc=qc, kt=kt, n=len(steps),
                                          ob=(idx * self.NC + qc) % 2))

            def emit_qk(sd):
                b, qc, kt = sd["b"], sd["qc"], sd["kt"]
                sb_ = sd["n"] % 2
                for hh in range(2):
                    rows = slice(hh * 64, (hh + 1) * 64)
                    k.op("pe", lambda e, hh=hh, rows=rows: e.matmul(
                        pS[:, sb_, hh * 512:(hh + 1) * 512], lhsT=KT[rows, b, kt * 128:(kt + 1) * 128],
                        rhs=QT[rows, b, qc * 512:(qc + 1) * 512], start=True, stop=True),
                        reads=[t_KT[b], t_QT[b]], writes=[t_pS[sb_]], signal=(hh == 1), partial=(hh == 1))

            def emit_exp(sd):
                if sd["qc"] == 0 and sd["kt"] == 0 and sd["idx"] + 1 < len(items):
                    load(sd["idx"] + 1)
                sb_ = sd["n"] % 2
                pb = sd["n"] % NPT
                k.op("act", lambda e: e.activation(out=pT[:, pb, :], in_=pS[:, sb_, :], func=AF.Exp, scale=0.125),
                     reads=[t_pS[sb_]], writes=[t_pT[pb]])

            def emit_pv(sd):
                b, qc, kt, ob = sd["b"], sd["qc"], sd["kt"], sd["ob"]
                pb = sd["n"] % NPT
                for hh in range(2):
                    vcol = slice(64, 192) if hh == 0 else slice(0, 128)
                    k.op("pe", lambda e, hh=hh, vcol=vcol: e.matmul(
                        pO[:, ob, hh, :], lhsT=V1[:, b, kt, vcol], rhs=pT[:, pb, hh * 512:(hh + 1) * 512],
                        start=(kt == 0), stop=(kt == self.NT - 1)),
                        reads=[t_V1[b], t_pT[pb]], writes=[t_pO[ob]], signal=(hh == 1), partial=not (kt == 0 and hh == 0))
                if kt == self.NT - 1:
                    s, hp = sd["s"], sd["hp"]
                    qcols = slice(qc * 512, (qc + 1) * 512)
                    self.finalize(g, pO[:, ob, 0, :], pO[:, ob, 1, :], [t_pO[ob]], G[:, b, qcols], t_G[b],
                                  self.yT[s, 512 + hp * 128:512 + (hp + 1) * 128, qcols])

            load(0)
            emit_qk(steps[0])
            if len(steps) > 1:
                emit_qk(steps[1])
            for n in range(len(steps)):
                emit_exp(steps[n])
                if n + 2 < len(steps):
                    emit_qk(steps[n + 2])
                emit_pv(steps[n])

    def phase_U(self, l, xin):
        k, nseq = self.k, self.nseq
        pi = l // 2
        with ExitStack() as st:
            f = self.alloc_front(st)
            W = self.sb(st, "U_W", [128, 8, 2048], BF16)
            t_W = T("W")
            wv = self.pool_w_in[pi].rearrange("(kc p) n -> p kc n", p=128)
            for kc in range(8):
                self.load_w(W[:, kc, :], wv[:, kc, :], t_W)
            pj = self.ps(st, "U_pj", [128, 4, 512], F32)
            t_pj = [T("pj%d" % i) for i in range(4)]
            NSTG = 4
            stg = self.sb(st, "U_stg", [128, NSTG, 512], BF16)
            t_stg = [T("stg%d" % i) for i in range(NSTG)]
            ust = self.sb(st, "U_ust", [128, 2, D], BF16)
            t_ust = [T("ust0"), T("ust1")]
            cnt = {"pj": 0, "stg": 0, "u": 0}
            items = [(s, c) for s in range(nseq) for c in range(self.NC)]
            self.load_x(f, xin, items[0][0], items[0][1], 0)
            if len(items) > 1:
                self.load_x(f, xin, items[1][0], items[1][1], 1)
            self.front_a(f, 0)
            self.front_b(f, l, items[0][0], 0)
            for idx, (s, c) in enumerate(items):
                if idx + 2 < len(items):
                    self.load_x(f, xin, items[idx + 2][0], items[idx + 2][1], idx + 2)
                if idx + 1 < len(items):
                    self.front_a(f, idx + 1)
                b = idx % 2
                hT, t_hT = f["hT"], f["t_hT"][b]
                cols = slice(c * 512, (c + 1) * 512)
                for fcg in range(8):
                    j = cnt["pj"] % 4
                    cnt["pj"] += 1
                    for kc in range(8):
                        k.op("pe", lambda e, kc=kc, fcg=fcg, j=j: e.matmul(
                            pj[:, j, :], lhsT=W[:, kc, 1024 + fcg * 128:1024 + (fcg + 1) * 128], rhs=hT[:, b, kc, :],
                            start=(kc == 0), stop=(kc == 7)),
                            reads=[t_W, t_hT], writes=[t_pj[j]], signal=(kc == 7), partial=(kc > 0))
                    si = cnt["stg"] % NSTG
                    cnt["stg"] += 1
                    k.op("act", lambda e, j=j, si=si: e.activation(out=stg[:, si, :], in_=pj[:, j, :], func=AF.Silu),
                         reads=[t_pj[j]], writes=[t_stg[si]])
                    k.op("sp", lambda e, si=si, fcg=fcg: e.dma_start(out=self.gateT[s, fcg * 128:(fcg + 1) * 128, cols],
                                                                   in_=stg[:, si, :]), reads=[t_stg[si]], dma=True)
                if idx + 1 < len(items):
                    self.front_b(f, l, items[idx + 1][0], idx + 1)
                for i in range(4):
                    ub = cnt["u"] % 2
                    cnt["u"] += 1
                    for half in range(2):
                        j = cnt["pj"] % 4
                        cnt["pj"] += 1
                        for kc in range(8):
                            k.op("pe", lambda e, kc=kc, i=i, half=half, j=j: e.matmul(
                                pj[:, j, :], lhsT=hT[:, b, kc, i * 128:(i + 1) * 128], rhs=W[:, kc, half * 512:(half + 1) * 512],
                                start=(kc == 0), stop=(kc == 7)),
                                reads=[t_W, t_hT], writes=[t_pj[j]], signal=(kc == 7), partial=(kc > 0))
                        k.op("dve", lambda e, j=j, ub=ub, half=half: e.tensor_copy(out=ust[:, ub, half * 512:(half + 1) * 512],
                                                                                 in_=pj[:, j, :]),
                             reads=[t_pj[j]], writes=[t_ust[ub]], partial=(half == 1))
                    tok0 = c * 512 + i * 128
                    k.op("sp", lambda e, ub=ub, tok0=tok0: e.dma_start(out=self.u[s, tok0:tok0 + 128, :], in_=ust[:, ub, :]),
                         reads=[t_ust[ub]], dma=True)

    def phase_O(self, l, xin, xout, attn):
        k, nseq = self.k, self.nseq
        wi = l // 2
        with ExitStack() as st:
            Wo = self.sb(st, "O_W", [128, 8, D], BF16)
            t_W = T("Wo")
            wsrc = self.attn_w_out[wi] if attn else self.pool_w_out[wi]
            wv = wsrc.rearrange("(kc p) n -> p kc n", p=128)
            for kc in range(0, 8, 2):
                self.load_w(Wo[:, kc:kc + 2, :], wv[:, kc:kc + 2, :], t_W)
            yTc = self.sb(st, "O_yT", [128, 2, 8, 512], BF16)
            t_yT = [T("yT0"), T("yT1")]
            xt = self.sb(st, "O_xt", [128, 2, 4, D], F32)
            t_xt = [T("xt0"), T("xt1")]
            xo = self.sb(st, "O_xo", [128, 2, 4, D], F32)
            t_xo = [T("xo0"), T("xo1")]
            gpb = self.sb(st, "O_gpb", [128, D], F32)
            t_gpb = T("gpb")
            tt = self.sb(st, "O_tt", [128, 2, D], F32)
            t_tt = [T("tt0"), T("tt1")]
            junk = self.sb(st, "O_junk", [128, D], BF16)
            t_junk = T("junk")
            ss = self.sb(st, "O_ss", [128, 2], F32)
            sd = self.sb(st, "O_sd", [128, 2], F32)
            rs = self.sb(st, "O_rs", [128, 2], F32)
            t_ss = [T("ss0"), T("ss1")]
            t_sd = [T("sd0"), T("sd1")]
            t_rs = [T("rs0"), T("rs1")]
            pm = self.ps(st, "O_pm", [128, 2, 1024], F32)
            t_pm = [T("pm0"), T("pm1")]
            cnt = {"m": 0, "pp": 0, "pg": 0}
            if not attn:
                Wg = self.sb(st, "V_Wg", [128, 4, 2, 256], BF16)
                t_Wg = T("Wg")
                self.load_w(Wg[:, :, :, :], self.pool_w_grp[wi].rearrange("g (cc p) d -> p g cc d", p=128), t_Wg)
                uc = self.sb(st, "V_uc", [128, 2, 4, D], BF16)
                t_uc = [T("uc0"), T("uc1")]
                HPt = self.sb(st, "V_HP", [128, 2, D], BF16)
                HNt = self.sb(st, "V_HN", [128, 2, D], BF16)
                t_HP = [T("HP0"), T("HP1")]
                t_HN = [T("HN0"), T("HN1")]
                for hb in range(2):
                    k.op("pool", lambda e, hb=hb: e.memset(HPt[:, hb, :], 0.0), writes=[t_HP[hb]])
                    k.op("pool", lambda e, hb=hb: e.memset(HNt[:, hb, :], 0.0), writes=[t_HN[hb]])
                gc = self.sb(st, "V_gc", [128, 2, 8, 512], BF16)
                t_gc = [T("gc0"), T("gc1")]
                pooled = self.sb(st, "V_pooled", [128, 8, 512], BF16)
                t_pooled = [T("pooled%d" % i) for i in range(8)]
                pp = self.ps(st, "V_pp", [128, 2, 512], F32)
                t_pp = [T("pp0"), T("pp1")]
                pg = self.ps(st, "V_pg", [128, 2, 512], F32)
                t_pg = [T("pg0"), T("pg1")]
            pgp = self.ps(st, "O_pgp", [128, 2, 512], F32) if attn else None
            t_pgp = T("pgp")
            items = [(s, c) for s in range(nseq) for c in range(self.NC)]

            def prefetch(idx):
                s, c = items[idx]
                b = idx % 2
                xv = xin[s].rearrange("(i p) d -> p i d", p=128)
                k.op("sp", lambda e: e.dma_start(out=xt[:, b, :, :], in_=xv[:, 4 * c:4 * c + 4, :]), writes=[t_xt[b]], dma=True)
                if attn:
                    yv = self.yT[s].rearrange("(kc p) t -> p kc t", p=128)
                    k.op("sp", lambda e: e.dma_start(out=yTc[:, b, :, :], in_=yv[:, :, c * 512:(c + 1) * 512]),
                         writes=[t_yT[b]], dma=True)
                else:
                    uv = self.u[s].rearrange("(i p) d -> p i d", p=128)
                    k.op("sp", lambda e: e.dma_start(out=uc[:, b, :, :], in_=uv[:, 4 * c:4 * c + 4, :]), writes=[t_uc[b]], dma=True)
                    if c > 0:
                        k.op("sp", lambda e: e.dma_start(out=HPt[96:128, b, :], in_=self.u[s, c * 512 - 32:c * 512, :]),
                             writes=[t_HP[b]], dma=True, partial=True)
                    if c < self.NC - 1:
                        k.op("sp", lambda e: e.dma_start(out=HNt[0:32, b, :], in_=self.u[s, (c + 1) * 512:(c + 1) * 512 + 32, :]),
                             writes=[t_HN[b]], dma=True, partial=True)
                    gv = self.gateT[s].rearrange("(kc p) t -> p kc t", p=128)
                    k.op("sp", lambda e: e.dma_start(out=gc[:, b, :, :], in_=gv[:, :, c * 512:(c + 1) * 512]),
                         writes=[t_gc[b]], dma=True)

            prefetch(0)
            for idx, (s, c) in enumerate(items):
                b = idx % 2
                if idx + 1 < len(items):
                    prefetch(idx + 1)
                if c == 0:
                    pgx = pgp if attn else pg
                    t_pgx = [t_pgp, t_pgp] if attn else t_pg
                    for half in range(2):
                        k.op("pe", lambda e, half=half: e.matmul(pgx[:, half, :], lhsT=self.SEL[:, s, :],
                                                                  rhs=self.GP[:, l, half * 512:(half + 1) * 512],
                                                                  start=True, stop=True),
                             writes=[t_pgx[half]])
                        k.op("act", lambda e, half=half: e.activation(out=gpb[:, half * 512:(half + 1) * 512],
                                                                      in_=pgx[:, half, :], func=AF.Copy),
                             reads=[t_pgx[half]], writes=[t_gpb], partial=(half == 1))
                if not attn:
                    for fc in range(8):
                        gI = fc // 2
                        ppb = cnt["pp"] % 2
                        cnt["pp"] += 1
                        ops = []
                        for i in range(4):
                            Tg = 4 * c + i
                            var = 1 if Tg == 0 else (2 if Tg == self.NT - 1 else 0)
                            fcols = slice(fc * 128, (fc + 1) * 128)
                            oc = slice(i * 128, (i + 1) * 128)
                            bc0 = CP_BC + (gI * 3 + var) * 128
                            lst = [(uc[:, b, i, fcols], self.CP[:, bc0:bc0 + 128], [t_uc[b]])]
                            if Tg > 0:
                                bp0 = CP_BP + gI * 128
                                if i > 0:
                                    lst.append((uc[64:128, b, i - 1, fcols], self.CP[64:128, bp0:bp0 + 128], [t_uc[b]]))
                                else:
                                    lst.append((HPt[64:128, b, fcols], self.CP[64:128, bp0:bp0 + 128], [t_HP[b]]))
                            if Tg < self.NT - 1:
                                bn0 = CP_BN + gI * 128
                                if i < 3:
                                    lst.append((uc[0:32, b, i + 1, fcols], self.CP[0:32, bn0:bn0 + 128], [t_uc[b]]))
                                else:
                                    lst.append((HNt[0:32, b, fcols], self.CP[0:32, bn0:bn0 + 128], [t_HN[b]]))
                            for q, (lh, rh, rd) in enumerate(lst):
                                ops.append((oc, lh, rh, rd, q == 0, q == len(lst) - 1))
                        for q, (oc, lh, rh, rd, stt, stp) in enumerate(ops):
                            k.op("pe", lambda e, oc=oc, lh=lh, rh=rh, stt=stt, stp=stp: e.matmul(
                                pp[:, ppb, oc], lhsT=lh, rhs=rh, start=stt, stop=stp),
                                reads=rd, writes=[t_pp[ppb]], signal=(q == len(ops) - 1), partial=(q > 0))
                        k.op("act", lambda e, fc=fc, ppb=ppb: e.activation(out=pooled[:, fc, :], in_=pp[:, ppb, :], func=AF.Copy),
                             reads=[t_pp[ppb]], writes=[t_pooled[fc]])
                    for dc in range(8):
                        gI = dc // 2
                        pgb = cnt["pg"] % 2
                        cnt["pg"] += 1
                        for cc in range(2):
                            k.op("pe", lambda e, dc=dc, gI=gI, cc=cc, pgb=pgb: e.matmul(
                                pg[:, pgb, :], lhsT=Wg[:, gI, cc, (dc % 2) * 128:(dc % 2 + 1) * 128],
                                rhs=pooled[:, 2 * gI + cc, :], start=(cc == 0), stop=(cc == 1)),
                                reads=[t_Wg, t_pooled[2 * gI + cc]], writes=[t_pg[pgb]], signal=(cc == 1), partial=(cc == 1))
                        k.op("dve", lambda e, dc=dc, pgb=pgb: e.scalar_tensor_tensor(
                            out=yTc[:, b, dc, :], in0=pg[:, pgb, :], scalar=self.PSC[:, wi, dc:dc + 1], in1=gc[:, b, dc, :],
                            op0=ALU.mult, op1=ALU.mult),
                            reads=[t_pg[pgb], t_gc[b]], writes=[t_yT[b]], partial=(dc > 0))
                for i in range(4):
                    mb = cnt["m"] % 2
                    cnt["m"] += 1
                    for half in range(2):
                        for kc in range(8):
                            k.op("pe", lambda e, i=i, half=half, kc=kc, mb=mb: e.matmul(
                                pm[:, mb, half * 512:(half + 1) * 512], lhsT=yTc[:, b, kc, i * 128:(i + 1) * 128],
                                rhs=Wo[:, kc, half * 512:(half + 1) * 512], start=(kc == 0), stop=(kc == 7)),
                                reads=[t_W, t_yT[b]], writes=[t_pm[mb]], signal=(kc == 7 and half == 1),
                                partial=not (kc == 0 and half == 0))
                    k.op("act", lambda e, mb=mb: e.activation(out=junk[:, :], in_=pm[:, mb, :], func=AF.Square,
                                                              accum_out=ss[:, mb:mb + 1]),
                         reads=[t_pm[mb]], writes=[t_junk, t_ss[mb]])
                    k.op("act", lambda e, mb=mb: e.activation(out=sd[:, mb:mb + 1], in_=ss[:, mb:mb + 1], func=AF.Sqrt,
                                                              scale=1.0 / D, bias=self.EPS[:, 0:1]),
                         reads=[t_ss[mb]], writes=[t_sd[mb]])
                    k.op("dve", lambda e, mb=mb: e.reciprocal(out=rs[:, mb:mb + 1], in_=sd[:, mb:mb + 1]),
                         reads=[t_sd[mb]], writes=[t_rs[mb]])
                    k.op("dve", lambda e, mb=mb: e.scalar_tensor_tensor(
                        out=tt[:, mb, :], in0=pm[:, mb, :], scalar=rs[:, mb:mb + 1], in1=gpb[:, :],
                        op0=ALU.mult, op1=ALU.mult),
                        reads=[t_pm[mb], t_rs[mb], t_gpb], writes=[t_tt[mb]])
                    k.op("pool", lambda e, i=i, mb=mb: e.tensor_tensor(out=xo[:, b, i, :], in0=tt[:, mb, :], in1=xt[:, b, i, :],
                                                                      op=ALU.add),
                         reads=[t_tt[mb], t_xt[b]], writes=[t_xo[b]], partial=(i > 0))
                ov = xout[s].rearrange("(i p) d -> p i d", p=128)
                k.op("sp", lambda e: e.dma_start(out=ov[:, 4 * c:4 * c + 4, :], in_=xo[:, b, :, :]), reads=[t_xo[b]], dma=True)


_PROG_CACHE = {}


def _get_prog(nseq, S, layers=(0, 1, 2, 3)):
    key = (nseq, S, tuple(layers))
    if key not in _PROG_CACHE:
        _PROG_CACHE[key] = Prog(nseq, S, layers)
    return _PROG_CACHE[key]


def _shared_inputs(nseq, S, ada_w, ada_b, pre_norm, post_norm, attn_w_in, attn_q_norm, attn_k_norm,
                   attn_w_out, pool_w_in, pool_w_grp, pool_scale, pool_w_out):
    f = lambda a: np.ascontiguousarray(np.asarray(a, dtype=np.float32))
    ada_b = f(ada_b)
    m = {}
    m["ada_w"] = f(ada_w)
    m["ada_bT"] = np.ascontiguousarray(ada_b[:, :2048].reshape(4, 16, 128).transpose(2, 0, 1))
    m["ada_bg"] = np.ascontiguousarray(np.broadcast_to(ada_b[None, :, 2048:], (nseq, 4, D)))
    m["pre_T"] = np.ascontiguousarray(f(pre_norm).reshape(4, 8, 128).transpose(2, 0, 1))
    m["post_rep"] = np.ascontiguousarray(np.broadcast_to(f(post_norm)[None], (nseq, 4, D)))
    m["attn_w_in"] = f(attn_w_in)
    qg = np.stack([f(attn_q_norm), f(attn_k_norm)], axis=-1)
    m["qk_g"] = np.ascontiguousarray(np.concatenate([qg, qg], axis=1).transpose(1, 0, 2))
    m["attn_w_out"] = f(attn_w_out)
    m["pool_w_in"] = f(pool_w_in)
    m["pool_w_grp"] = f(pool_w_grp)
    m["pool_scT"] = np.ascontiguousarray(f(pool_scale).reshape(2, 8, 128).transpose(2, 0, 1))
    m["pool_w_out"] = f(pool_w_out)
    m.update(_consts(S, nseq))
    return m


def run_trunk(xs, cs, weights, n_cores, layers=(0, 1, 2, 3)):
    nseq, S = xs.shape[1], xs.shape[2]
    prog = _get_prog(nseq, S, layers)
    shared = _shared_inputs(nseq, S, **weights)
    in_maps = []
    for i in range(n_cores):
        m = dict(shared)
        m["x"] = np.ascontiguousarray(xs[i])
        m["cT"] = np.ascontiguousarray(cs[i].reshape(nseq, 8, 128).transpose(2, 1, 0))
        in_maps.append(m)
    res = run_bass_kernel_spmd(prog.nc, in_maps, core_ids=list(range(n_cores)))
    return np.stack([np.asarray(r["y"]) for r in res.results], axis=0)


def kernel(x_prompt, x_sample, c_prompt, c_sample, ada_w, ada_b, pre_norm, post_norm,
           attn_w_in, attn_q_norm, attn_k_norm, attn_w_out,
           pool_w_in, pool_w_grp, pool_scale, pool_w_out):
    x_prompt = np.asarray(x_prompt, dtype=np.float32)
    x_sample = np.asarray(x_sample, dtype=np.float32)
    c_prompt = np.asarray(c_prompt, dtype=np.float32)
    c_sample = np.asarray(c_sample, dtype=np.float32)
    nb_p, nb_s = x_prompt.shape[0], x_sample.shape[0]
    nreal = nb_p + nb_s
    nslots = N_CORES * NSEQ_FULL
    xs = np.empty((nslots, S_FULL, D), np.float32)
    cs = np.empty((nslots, D), np.float32)
    xs[:nb_p] = x_prompt
    xs[nb_p:nreal] = x_sample
    cs[:nb_p] = c_prompt
    cs[nb_p:nreal] = c_sample
    for j in range(nreal, nslots):
        xs[j] = xs[j - nreal]
        cs[j] = cs[j - nreal]
    weights = dict(ada_w=ada_w, ada_b=ada_b, pre_norm=pre_norm, post_norm=post_norm, attn_w_in=attn_w_in,
                   attn_q_norm=attn_q_norm, attn_k_norm=attn_k_norm, attn_w_out=attn_w_out, pool_w_in=pool_w_in,
                   pool_w_grp=pool_w_grp, pool_scale=pool_scale, pool_w_out=pool_w_out)
    y = run_trunk(xs.reshape(N_CORES, NSEQ_FULL, S_FULL, D), cs.reshape(N_CORES, NSEQ_FULL, D), weights, N_CORES)
    y = y.reshape(nslots, S_FULL, D)
    return (np.ascontiguousarray(y[:nb_p]), np.ascontiguousarray(y[nb_p:nreal]))
```
